# Optimizing a Trainium2 kernel written in Bass

```python
import math
import jax, jax.numpy as jnp
from jax import lax
import numpy as np


D_MODEL = 1024
BATCH = 4
SEQ = 8192
DEPTH = 4
DEC_BATCH = 16
DEC_SEQ = 32
PAST_LEN = 2048

CHUNK = 64
N_ATT = 8
ATT_HD = 64
ATT_W = N_ATT * 2 * ATT_HD
N_HG = 8
HG_DK = 128
HG_DV = 128
HG_KW = N_HG * HG_DK
HG_VW = N_HG * HG_DV
D_FF = 2816
N_ADA = 9
Q_BLOCK = 128
MACARON_WEIGHT = 0.5
EPS = 1e-6
MASK_VALUE = -1e30
TINY = 1e-30
IN_SPLIT_POINTS = (ATT_W, 2 * ATT_W, 3 * ATT_W,
                   3 * ATT_W + HG_KW, 3 * ATT_W + 2 * HG_KW,
                   3 * ATT_W + 2 * HG_KW + HG_VW, 3 * ATT_W + 2 * HG_KW + 2 * HG_VW,
                   3 * ATT_W + 2 * HG_KW + 2 * HG_VW + D_MODEL)
D_IN = 3 * ATT_W + 2 * HG_KW + 2 * HG_VW + 2 * D_MODEL

kernel_name = 'hybrid_diffattn_hgrn2_macaron_adaln_stream_step'


def rmsnorm(x, g):
    xf = x.astype(jnp.float32)
    y = xf * lax.rsqrt(jnp.mean(xf * xf, axis=-1, keepdims=True) + EPS)
    return (y * g.astype(jnp.float32)).astype(x.dtype)


def swiglu(h, w_gu, w_d):
    gate, up = jnp.split(h @ w_gu, 2, axis=-1)
    return (jax.nn.silu(gate) * up) @ w_d


def diff_attention(q, k, v, q_pos, k_pos, lam):
    B, Tq = q.shape[:2]
    kf = k.astype(jnp.float32)
    vf = v.astype(jnp.float32)

    def block(args):
        qb, pb = args
        s = jnp.einsum('bqhmd,bkhmd->bhmqk', qb.astype(jnp.float32), kf) * (ATT_HD ** -0.5)
        visible = k_pos[None, :] < ((pb // CHUNK + 1) * CHUNK)[:, None]
        p = jax.nn.softmax(jnp.where(visible, s, MASK_VALUE), axis=-1)
        a = p[:, :, 0] - lam * p[:, :, 1]
        return jnp.einsum('bhqk,bkhe->bqhe', a, vf)

    if Tq > Q_BLOCK:
        nb = Tq // Q_BLOCK
        qb = jnp.moveaxis(q.reshape(B, nb, Q_BLOCK, *q.shape[2:]), 1, 0)
        pb = q_pos.reshape(nb, Q_BLOCK)
        o = lax.map(block, (qb, pb))
        return jnp.moveaxis(o, 0, 1).reshape(B, Tq, *o.shape[3:])
    return block((q, q_pos))


def hgrn2_chunked(q, k, v, logf, s0):
    B, T = q.shape[:2]
    L = min(CHUNK, T)
    n = T // L

    def to_chunks(a):
        return jnp.moveaxis(a.reshape(B, n, L, *a.shape[2:]), 1, 0)

    causal = jnp.tril(jnp.ones((L, L), dtype=bool))[None, :, :, None, None]

    def step(S, inp):
        qc, kc, vc, lfc = inp
        b = lax.cumsum(lfc, axis=1)
        b_last = b[:, -1]
        o_inter = jnp.einsum('bthk,bhkv->bthv', qc * jnp.exp(b), S)
        rel = b[:, :, None] - b[:, None, :]
        decay = jnp.where(causal, jnp.exp(jnp.where(causal, rel, 0.0)), 0.0)
        A = jnp.einsum('bthk,btshk,bshk->bhts', qc, decay, kc)
        o_intra = jnp.einsum('bhts,bshv->bthv', A, vc)
        S_new = S * jnp.exp(b_last)[..., None] + jnp.einsum(
            'bshk,bshv->bhkv', kc * jnp.exp(b_last[:, None] - b), vc)
        return S_new, o_inter + o_intra

    s_fin, o = lax.scan(step, s0, (to_chunks(q), to_chunks(k), to_chunks(v), to_chunks(logf)))
    o = jnp.moveaxis(o, 0, 1).reshape(B, T, *o.shape[3:])
    return o, s_fin


def trunk(x, c, q_pos, k_pos, past_k, past_v, past_s, p):
    B, T, _ = x.shape
    f32 = jnp.float32
    lb_sm = jax.nn.softmax(p['hg_lb_logits'].astype(f32), axis=0)
    lb_all = lax.cumsum(lb_sm, axis=0) - lb_sm[0]
    c_act = jax.nn.silu(c.astype(f32))
    new_k, new_v, new_s = [], [], []
    for l in range(DEPTH):
        m = (c_act @ p['w_ada'][l].astype(f32) + p['b_ada'][l].astype(f32)).astype(x.dtype)
        m = m.reshape(B, N_ADA, 1, D_MODEL)
        sh1, sc1, g1, sh2, sc2, g2, sh3, sc3, g3 = [m[:, i] for i in range(N_ADA)]

        h = rmsnorm(x, p['g_ffn1'][l]) * (1 + sc1) + sh1
        x = x + MACARON_WEIGHT * g1 * swiglu(h, p['w_ffn1_gu'][l], p['w_ffn1_d'][l])

        u = rmsnorm(x, p['g_mix'][l]) * (1 + sc2) + sh2
        z = u @ p['w_in'][l]
        zq, zk, zv, hq, hf, hi, hg, za, zh = jnp.split(z, IN_SPLIT_POINTS, axis=-1)

        q = zq.reshape(B, T, N_ATT, 2, ATT_HD)
        k_rows = zk.reshape(B, T, N_ATT, 2 * ATT_HD)
        v_rows = zv.reshape(B, T, N_ATT, 2 * ATT_HD)
        new_k.append(k_rows)
        new_v.append(v_rows)
        if past_k is None:
            k_all, v_all = k_rows, v_rows
        else:
            k_all = jnp.concatenate([past_k[l].astype(k_rows.dtype), k_rows], axis=1)
            v_all = jnp.concatenate([past_v[l].astype(v_rows.dtype), v_rows], axis=1)
        lam_init = 0.8 - 0.6 * math.exp(-0.3 * l)
        lp = p['att_lambda'][l].astype(f32)
        lam = jnp.exp(jnp.sum(lp[0] * lp[1])) - jnp.exp(jnp.sum(lp[2] * lp[3])) + lam_init
        o_att = diff_attention(q, k_all.reshape(B, -1, N_ATT, 2, ATT_HD), v_all, q_pos, k_pos, lam)
        o_att = rmsnorm(o_att, p['g_att_sub'][l]) * (1.0 - lam_init)
        o_att = o_att.reshape(B, T, ATT_W).astype(x.dtype)

        lb = lb_all[l].reshape(N_HG, HG_DK)
        fr = hf.astype(f32).reshape(B, T, N_HG, HG_DK)
        sig = jax.nn.sigmoid(fr)
        f_gate = lb + (1.0 - lb) * sig
        logf = jnp.log(jnp.maximum(f_gate, TINY))
        kk = (1.0 - lb) * (1.0 - sig)
        qq = jax.nn.silu(hq.astype(f32)).reshape(B, T, N_HG, HG_DK)
        vv = hi.astype(f32).reshape(B, T, N_HG, HG_DV)
        s0 = jnp.zeros((B, N_HG, HG_DK, HG_DV), f32) if past_s is None else past_s[l].astype(f32)
        o_h, s_fin = hgrn2_chunked(qq, kk, vv, logf, s0)
        new_s.append(s_fin.astype(x.dtype))
        o_h = rmsnorm(o_h, p['g_hg_norm'][l]) * jax.nn.silu(hg.astype(f32).reshape(B, T, N_HG, HG_DV))
        o_h = o_h.reshape(B, T, HG_VW).astype(x.dtype)

        merged = (jax.nn.sigmoid(za) * (o_att @ p['w_br_att'][l])
                  + jax.nn.sigmoid(zh) * (o_h @ p['w_br_hg'][l]))
        x = x + g2 * (merged @ p['w_out'][l])

        h = rmsnorm(x, p['g_ffn2'][l]) * (1 + sc3) + sh3
        x = x + MACARON_WEIGHT * g3 * swiglu(h, p['w_ffn2_gu'][l], p['w_ffn2_d'][l])
    y = rmsnorm(x, p['g_final'])
    return y, jnp.stack(new_k), jnp.stack(new_v), jnp.stack(new_s)


def setup_inputs(seed: int = 0) -> dict:
    key = jax.random.key(seed)
    ks = jax.random.split(key, 32)
    f32 = jnp.float32

    def nrm(k, shape, scale):
        return jax.random.normal(k, shape, f32) * scale

    def gain(k, shape):
        return 1.0 + 0.02 * jax.random.normal(k, shape, f32)

    return {
        'x_prompt': nrm(ks[0], (BATCH, SEQ, D_MODEL), 1.0),
        'x_sample': nrm(ks[1], (DEC_BATCH, DEC_SEQ, D_MODEL), 1.0),
        'cache_k': nrm(ks[2], (DEPTH, DEC_BATCH, PAST_LEN, N_ATT, 2 * ATT_HD), 1.0),
        'cache_v': nrm(ks[3], (DEPTH, DEC_BATCH, PAST_LEN, N_ATT, 2 * ATT_HD), 1.0),
        'state_hgrn': nrm(ks[4], (DEPTH, DEC_BATCH, N_HG, HG_DK, HG_DV), 0.5),
        'c_prompt': nrm(ks[5], (BATCH, D_MODEL), 1.0),
        'c_sample': nrm(ks[6], (DEC_BATCH, D_MODEL), 1.0),
        'w_ada': nrm(ks[7], (DEPTH, D_MODEL, N_ADA * D_MODEL), 0.5 * D_MODEL ** -0.5),
        'b_ada': nrm(ks[8], (DEPTH, N_ADA * D_MODEL), 0.01),
        'g_ffn1': gain(ks[9], (DEPTH, D_MODEL)),
        'w_ffn1_gu': nrm(ks[10], (DEPTH, D_MODEL, 2 * D_FF), D_MODEL ** -0.5),
        'w_ffn1_d': nrm(ks[11], (DEPTH, D_FF, D_MODEL), D_FF ** -0.5),
        'g_mix': gain(ks[12], (DEPTH, D_MODEL)),
        'w_in': nrm(ks[13], (DEPTH, D_MODEL, D_IN), D_MODEL ** -0.5),
        'att_lambda': nrm(ks[14], (DEPTH, 4, ATT_HD), 0.1),
        'g_att_sub': gain(ks[15], (DEPTH, 2 * ATT_HD)),
        'hg_lb_logits': nrm(ks[16], (DEPTH, HG_KW), 0.5),
        'g_hg_norm': gain(ks[17], (DEPTH, HG_DV)),
        'w_br_att': nrm(ks[18], (DEPTH, ATT_W, D_MODEL), ATT_W ** -0.5),
        'w_br_hg': nrm(ks[19], (DEPTH, HG_VW, D_MODEL), HG_VW ** -0.5),
        'w_out': nrm(ks[20], (DEPTH, D_MODEL, D_MODEL), D_MODEL ** -0.5),
        'g_ffn2': gain(ks[21], (DEPTH, D_MODEL)),
        'w_ffn2_gu': nrm(ks[22], (DEPTH, D_MODEL, 2 * D_FF), D_MODEL ** -0.5),
        'w_ffn2_d': nrm(ks[23], (DEPTH, D_FF, D_MODEL), D_FF ** -0.5),
        'g_final': gain(ks[24], (D_MODEL,)),
    }


def reference(x_prompt, x_sample, cache_k, cache_v, state_hgrn, c_prompt, c_sample,
              w_ada, b_ada, g_ffn1, w_ffn1_gu, w_ffn1_d, g_mix, w_in, att_lambda, g_att_sub,
              hg_lb_logits, g_hg_norm, w_br_att, w_br_hg, w_out, g_ffn2, w_ffn2_gu, w_ffn2_d,
              g_final):
    params = dict(w_ada=w_ada, b_ada=b_ada, g_ffn1=g_ffn1, w_ffn1_gu=w_ffn1_gu, w_ffn1_d=w_ffn1_d,
                  g_mix=g_mix, w_in=w_in, att_lambda=att_lambda, g_att_sub=g_att_sub,
                  hg_lb_logits=hg_lb_logits, g_hg_norm=g_hg_norm, w_br_att=w_br_att,
                  w_br_hg=w_br_hg, w_out=w_out, g_ffn2=g_ffn2, w_ffn2_gu=w_ffn2_gu,
                  w_ffn2_d=w_ffn2_d, g_final=g_final)
    t_p = x_prompt.shape[1]
    pos_p = jnp.arange(t_p, dtype=jnp.int32)
    y_prompt, k_prompt, v_prompt, s_prompt = trunk(
        x_prompt, c_prompt, pos_p, pos_p, None, None, None, params)
    past = cache_k.shape[2]
    t_s = x_sample.shape[1]
    pos_s = past + jnp.arange(t_s, dtype=jnp.int32)
    kpos_s = jnp.arange(past + t_s, dtype=jnp.int32)
    y_sample, k_sample, v_sample, s_sample = trunk(
        x_sample, c_sample, pos_s, kpos_s, cache_k, cache_v, state_hgrn, params)
    return (y_prompt, y_sample, k_prompt, v_prompt, s_prompt, k_sample, v_sample, s_sample)
```

```python
import contextlib
import math
import numpy as np
import concourse.bass as bass
import concourse.mybir as mybir
from concourse.bass_utils import run_bass_kernel_spmd

F32 = mybir.dt.float32
BF16 = mybir.dt.bfloat16
AF = mybir.ActivationFunctionType
ALU = mybir.AluOpType

D = 1024
DFF = 2816
NH = 8
KC = 8
FC = 22
DIN = 9216
EPS = 1e-6
TINY = 1e-30
NWG = 62
WSLOT = 4096
NBUF = 4


class Cfg:
    def __init__(self, SEQ=8192, DEPTH=4, PAST=2048, TS=32, NS=2, TT=512):
        self.SEQ, self.DEPTH, self.PAST, self.TS, self.NS, self.TT = SEQ, DEPTH, PAST, TS, NS, TT
        self.DEBUG = False


class Tile:
    pass


class B:
    def __init__(self, cfg):
        self.cfg = cfg
        self.nc = bass.Bass("TRN2", target_bir_lowering=False)
        self.es = contextlib.ExitStack()
        self.cnt = {}
        self.sem = {}
        self.waited = {}
        self.eng = {"pe": self.nc.tensor, "act": self.nc.scalar, "dve": self.nc.vector,
                    "pool": self.nc.gpsimd, "sp": self.nc.sync}
        for e in ("pe", "act", "dve", "pool"):
            self.newsem(e)
        self.bar_n = 0
        self.newsem("bar")
        self.last_tok = {}
        self.last_ins = {}

    def newsem(self, name):
        self.sem[name] = self.es.enter_context(self.nc.semaphore(name))
        self.cnt[name] = 0
        return name

    def tick(self, E, ins):
        if self.last_ins.get(E) is ins and self.last_tok.get(E) is not None:
            return self.last_tok[E]
        ins.then_inc(self.sem[E], 1)
        self.cnt[E] += 1
        tok = (E, self.cnt[E])
        if self.last_ins.get(E) is ins:
            self.last_tok[E] = tok
        return tok

    def pre_issue(self, E):
        li = self.last_ins.get(E)
        if li is None:
            return
        tok = self.tick(E, li)
        k = (E, E)
        if self.waited.get(k, 0) < tok[1]:
            self.eng[E].wait_ge(self.sem[E], tok[1])
            self.waited[k] = tok[1]

    def post_issue(self, E, ins):
        self.last_ins[E] = ins
        self.last_tok[E] = None

    def dtick(self, S, ins):
        ins.then_inc(self.sem[S], 16)
        self.cnt[S] += 16
        return (S, self.cnt[S])

    def wait(self, who, tok):
        if tok is None:
            return
        S, v = tok
        if S == who:
            return
        k = (who, S)
        if self.waited.get(k, 0) >= v:
            return
        self.eng[who].wait_ge(self.sem[S], v)
        self.waited[k] = v

    def sb(self, name, shape, dt):
        self.uid = getattr(self, "uid", 0) + 1
        return self.es_cur.enter_context(self.nc.sbuf_tensor("%s_%d" % (name, self.uid), shape, dt))

    def barrier(self, pool_tokens=()):
        nc = self.nc
        for t in pool_tokens:
            self.wait("pool", t)
        self.pre_issue("act")
        nc.scalar.copy(out=self.scrA[0:1, 0:1], in_=self.scrA[0:1, 1:2]).then_inc(self.sem["bar"], 1)
        self.pre_issue("dve")
        nc.vector.memset(self.scrV[0:1, 0:1], 0.0).then_inc(self.sem["bar"], 1)
        nc.gpsimd.memset(self.scrP[0:1, 0:1], 0.0).then_inc(self.sem["bar"], 1)
        self.bar_n += 3
        for e in ("act", "dve", "pool"):
            self.eng[e].wait_ge(self.sem["bar"], self.bar_n)
        self.last_ins["act"] = None
        self.last_ins["dve"] = None

    def bank_next(self):
        b = self.banks[self.bank_rr % 8]
        self.bank_rr += 1
        return b

    def wnext(self, kind):
        nc = self.nc
        i = self.w_used
        assert self.wseq[i][0] == kind, (self.wseq[i], kind)
        upto = min(i + NBUF - 1, len(self.wseq) - 1)
        while self.w_issued <= upto:
            j = self.w_issued
            slot = j % NBUF
            if j >= NBUF:
                self.wait("sp", self.wfree_tok[j - NBUF])
            _, l, gi, nk, ncol, first = self.wseq[j]
            if first:
                self.wait("sp", self.conv_tok[l])
            src = self.wscr[l * NWG + gi, :, 0:nk * ncol]
            ins = nc.sync.dma_start(out=self.wbuf[:, slot, 0:nk * ncol], in_=src)
            self.wld_tok[j] = self.dtick("wld%d" % slot, ins)
            self.w_issued += 1
        self.wait("pe", self.wld_tok[i])
        _, l, gi, nk, ncol, _ = self.wseq[i]
        self.w_used += 1
        self.w_cur = i
        return self.wbuf[:, i % NBUF, 0:nk * ncol].rearrange("p (k c) -> p k c", k=nk)

    def wdone(self, ins):
        self.wfree_tok[self.w_cur] = self.tick("pe", ins)
        return self.wfree_tok[self.w_cur]


class EngProxy:
    def __init__(self, k, name, eng):
        self._k, self._name, self._eng = k, name, eng

    def __getattr__(self, attr):
        f = getattr(self._eng, attr)
        if attr in ("wait_ge",):
            return f
        k, name = self._k, self._name

        def wrapped(*a, **kw):
            k.pre_issue(name)
            ins = f(*a, **kw)
            k.post_issue(name, ins)
            return ins
        return wrapped


def _wgroups():
    g = []
    for i in range(11):
        g.append(("gu1", i, 8, 512 if i < 10 else 512))
    for i in range(8):
        g.append(("d1", i, 22, 128))
    for i in range(4):
        g.append(("qk", i, 8, 512))
    for i in range(2):
        g.append(("v", i, 8, 512))
    for i in range(8):
        g.append(("hg", i, 8, 512))
    for i in range(8):
        g.append(("mrg", i, 8, 512))
    for i in range(2):
        g.append(("wo", i, 8, 512))
    for i in range(11):
        g.append(("gu2", i, 8, 512))
    for i in range(8):
        g.append(("d2", i, 22, 128))
    assert len(g) == NWG
    return g


def build(cfg):
    k = B(cfg)
    nc = k.nc
    es = k.es
    SEQ, DEPTH, PAST, TS, NS, TT = cfg.SEQ, cfg.DEPTH, cfg.PAST, cfg.TS, cfg.NS, cfg.TT
    NTS = NS * TS
    NPT = SEQ // TT
    NSQ = 1 + NS

    def din(name, shape):
        return nc.dram_tensor(name, list(shape), F32, kind="ExternalInput").ap()

    def dout(name, shape):
        return nc.dram_tensor(name, list(shape), F32, kind="ExternalOutput").ap()

    xp = din("xp", [SEQ, D]); xs = din("xs", [NTS, D])
    ck = din("ck", [DEPTH, NS, PAST, D]); cv = din("cv", [DEPTH, NS, PAST, D])
    st = din("st", [DEPTH, NS, NH, 128, 128]); cc = din("cc", [NSQ, D])
    w_ada = din("w_ada", [DEPTH, D, 9 * D]); b_ada = din("b_ada", [DEPTH, 9 * D])
    g_ffn1 = din("g_ffn1", [DEPTH, D]); w_gu1 = din("w_ffn1_gu", [DEPTH, D, 2 * DFF]); w_d1 = din("w_ffn1_d", [DEPTH, DFF, D])
    g_mix = din("g_mix", [DEPTH, D]); w_in = din("w_in", [DEPTH, D, DIN])
    att_lambda = din("att_lambda", [DEPTH, 4, 64]); g_att_sub = din("g_att_sub", [DEPTH, 128])
    hg_lb = din("hg_lb_logits", [DEPTH, D]); g_hg_norm = din("g_hg_norm", [DEPTH, 128])
    w_ba = din("w_br_att", [DEPTH, D, D]); w_bh = din("w_br_hg", [DEPTH, D, D]); w_o = din("w_out", [DEPTH, D, D])
    g_ffn2 = din("g_ffn2", [DEPTH, D]); w_gu2 = din("w_ffn2_gu", [DEPTH, D, 2 * DFF]); w_d2 = din("w_ffn2_d", [DEPTH, DFF, D])
    g_final = din("g_final", [D])

    yp = dout("yp", [SEQ, D]); ys = dout("ys", [NTS, D])
    kp = dout("kp", [DEPTH, SEQ, D]); vp = dout("vp", [DEPTH, SEQ, D]); sp_o = dout("sp", [DEPTH, NH, 128, 128])
    ks = dout("ks", [DEPTH, NTS, D]); vs = dout("vs", [DEPTH, NTS, D]); ss_o = dout("ss", [DEPTH, NS, NH, 128, 128])

    if cfg.DEBUG:
        dbg_att = dout("dbg_att", [128, NH, TT]); dbg_h = dout("dbg_h", [128, NH, TT])
        dbg_mod = dout("dbg_mod", [128, DEPTH * 9 * KC * NSQ]); dbg_x = dout("dbg_x", [128, KC * TT]); dbg_h1 = dout("dbg_h1", [128, KC * TT])
        dbg_hid = dout("dbg_hid", [128, FC * TT])
        k.newsem("dbg")
    k.wscr = nc.dram_tensor("wscr", [DEPTH * NWG, 128, WSLOT], BF16, kind="Internal").ap()
    kscr = nc.dram_tensor("kscr", [DEPTH, NPT, 128, NH * TT], BF16, kind="Internal").ap()
    vscr = nc.dram_tensor("vscr", [DEPTH, NPT, 128, NH * TT], BF16, kind="Internal").ap()

    k.es_cur = es
    sb = k.sb
    x_fm = sb("x_fm", [128, KC, TT], F32)
    h_bf = sb("h_bf", [128, KC, TT], BF16)
    o_att = sb("o_att", [128, NH, TT], BF16)
    o_h = sb("o_h", [128, NH, TT], BF16)
    merged = sb("merged", [128, KC, TT], BF16)
    rs = sb("rs", [128, TT], F32)
    S_all = sb("S_all", [128, DEPTH, NH, 128], F32)
    Sb = sb("Sb", [128, NH, 128], BF16)
    k.wbuf = sb("wbuf", [128, NBUF, WSLOT], BF16)
    identF = sb("identF", [128, 128], F32)
    identB = sb("identB", [128, 128], BF16)
    onesB = sb("onesB", [128, 128], BF16)
    triP = sb("triP", [128, 128], F32)
    triS = sb("triS", [128, 128], F32)
    M0p = sb("M0p", [128, TT], F32)
    M0s = sb("M0s", [128, NTS], F32)
    MOD = sb("MOD", [128, DEPTH, 9, KC, NSQ], F32)
    gfin = sb("gfin", [128, KC], F32)
    gatt = sb("gatt", [128, DEPTH], F32)
    ghg = sb("ghg", [128, DEPTH], F32)
    lamt = sb("lamt", [128, DEPTH], F32)
    lbt = sb("lbt", [128, DEPTH, NH], F32)
    omlt = sb("omlt", [128, DEPTH, NH], F32)
    nomlt = sb("nomlt", [128, DEPTH, NH], F32)
    k.scrA = sb("scrA", [128, 4], F32); k.scrV = sb("scrV", [128, 4], F32); k.scrP = sb("scrP", [128, 4], F32)

    pbs = [es.enter_context(nc.psum_tensor("pb%d" % i, [128, 2, 512], F32)) for i in range(4)]

    class Bank:
        pass
    k.banks = []
    for i in range(8):
        b = Bank()
        b.ap = pbs[i // 2][:, i % 2, :]
        b.pair = pbs[i // 2]
        b.ready = None
        b.free = None
        k.banks.append(b)
    k.bank_rr = 0

    for i in range(NBUF):
        k.newsem("wld%d" % i)
    for s in ("wad0", "wad1", "cv0", "ld", "sld", "st0", "st1", "st2", "xld", "kvst", "kv0", "kv1", "kv2", "kv3", "out"):
        k.newsem(s)

    tick, dtick, wait = k.tick, k.dtick, k.wait
    PE, POOL = nc.tensor, nc.gpsimd
    ACT = EngProxy(k, "act", nc.scalar)
    DVE = EngProxy(k, "dve", nc.vector)

    groups = _wgroups()
    k.conv_tok = {}
    for l in range(DEPTH):
        k.newsem("cvl%d" % l)
        gu = {1: w_gu1[l].rearrange("(kc p) c -> p kc c", p=128), 2: w_gu2[l].rearrange("(kc p) c -> p kc c", p=128)}
        dd = {1: w_d1[l].rearrange("(kc p) c -> p kc c", p=128), 2: w_d2[l].rearrange("(kc p) c -> p kc c", p=128)}
        wi = w_in[l].rearrange("(kc p) c -> p kc c", p=128)
        wba = w_ba[l].rearrange("(kc p) c -> p kc c", p=128)
        wbh = w_bh[l].rearrange("(kc p) c -> p kc c", p=128)
        wo = w_o[l].rearrange("(kc p) c -> p kc c", p=128)
        for gi, (kind, i, nk, ncol) in enumerate(groups):
            dst = k.wscr[l * NWG + gi, :, 0:nk * ncol].rearrange("p (k c) -> p k c", k=nk)
            parts = []
            if kind in ("gu1", "gu2"):
                W = gu[1 if kind == "gu1" else 2]
                parts = [(0, 256, W[:, :, 256 * i:256 * i + 256]), (256, 512, W[:, :, DFF + 256 * i:DFF + 256 * i + 256])]
            elif kind in ("d1", "d2"):
                W = dd[1 if kind == "d1" else 2]
                parts = [(0, 128, W[:, :, 128 * i:128 * i + 128])]
            elif kind == "qk":
                parts = [(0, 512, wi[:, :, 512 * i:512 * i + 512])]
            elif kind == "v":
                parts = [(0, 512, wi[:, :, 2048 + 512 * i:2048 + 512 * i + 512])]
            elif kind == "hg":
                parts = [(0, 512, wi[:, :, 3072 + 512 * i:3072 + 512 * i + 512])]
            elif kind == "mrg":
                parts = [(0, 128, wi[:, :, 7168 + 128 * i:7168 + 128 * i + 128]),
                         (128, 256, wi[:, :, 8192 + 128 * i:8192 + 128 * i + 128]),
                         (256, 384, wba[:, :, 128 * i:128 * i + 128]),
                         (384, 512, wbh[:, :, 128 * i:128 * i + 128])]
            elif kind == "wo":
                parts = [(0, 512, wo[:, :, 512 * i:512 * i + 512])]
            for (c0, c1, src) in parts:
                ins = POOL.dma_start(out=dst[:, :, c0:c1], in_=src)
                k.conv_tok[l] = dtick("cvl%d" % l, ins)

    tiles = []
    for t in range(NPT):
        T = Tile(); T.kind = "p"; T.t = t; T.NT = TT; T.nblk = TT // 128; T.bp = 128
        T.segs = [(0, TT, 0)]
        tiles.append(T)
    T = Tile(); T.kind = "s"; T.t = 0; T.NT = NTS; T.nblk = NS; T.bp = TS
    T.segs = [(i * TS, (i + 1) * TS, 1 + i) for i in range(NS)]
    tiles.append(T)
    k.wseq = []
    for ti, T in enumerate(tiles):
        for l in range(DEPTH):
            for gi, (kind, i, nk, ncol) in enumerate(groups):
                k.wseq.append((kind, l, gi, nk, ncol, ti == 0 and gi == 0))
    k.w_used = 0; k.w_issued = 0; k.wld_tok = {}; k.wfree_tok = {}

    POOL.memset(identF[:], 1.0)
    POOL.affine_select(out=identF[:], in_=identF[:], pattern=[[-1, 128]], compare_op=ALU.is_equal, fill=0.0, base=0, channel_multiplier=1)
    POOL.tensor_copy(out=identB[:], in_=identF[:])
    POOL.memset(onesB[:], 1.0)
    POOL.memset(triP[:], 1.0)
    POOL.affine_select(out=triP[:], in_=triP[:], pattern=[[1, 128]], compare_op=ALU.is_ge, fill=0.0, base=0, channel_multiplier=-1)
    POOL.tensor_copy(out=triS[:], in_=triP[:])
    POOL.memset(triP[0:64, 64:128], 0.0)
    POOL.memset(triS[0:32, 32:64], 0.0)
    POOL.memset(M0p[:], 1.0)
    for c in range(TT // 64):
        POOL.memset(M0p[:, 64 * c:64 * c + 1], 0.0)
    POOL.memset(M0s[:], 1.0)
    for c in range(NS):
        POOL.memset(M0s[:, TS * c:TS * c + 1], 0.0)
    POOL.memset(k.scrP[:], 0.0)
    c_tok = tick("pool", POOL.memset(k.scrP[:, 0:1], 0.0))
    DVE.memset(k.scrV[:], 0.0)
    wait("act", c_tok)
    ACT.copy(out=k.scrA[:], in_=identF[:, 0:4])

    with contextlib.ExitStack() as pes:
        k.es_cur = pes
        cT = sb("cT", [128, KC, NSQ], F32)
        cact = sb("cact", [128, KC, NSQ], F32)
        badaT = sb("badaT", [128, DEPTH, 72], F32)
        gT = sb("gT", [128, 3, DEPTH, KC], F32)
        lbl = sb("lbl", [128, DEPTH, NH], F32)
        lam_in = sb("lam_in", [128, DEPTH, 4, 64], F32)
        lam_w = sb("lam_w", [128, DEPTH, 2, 64], F32)
        lam_s = sb("lam_s", [128, DEPTH, 2], F32)
        wad = sb("wad", [128, 2, KC, 1024], F32)
        ld = []
        for s_ in range(NSQ):
            ld.append(dtick("ld", nc.sync.dma_start(out=cT[:, :, s_], in_=cc[s_].rearrange("(c p) -> p c", p=128), allow_slow_non_contiguous=True)))
        for l in range(DEPTH):
            ld.append(dtick("ld", nc.sync.dma_start(out=badaT[:, l, :], in_=b_ada[l].rearrange("(j p) -> p j", p=128), allow_slow_non_contiguous=True)))
            for gi, gsrc in enumerate((g_ffn1, g_mix, g_ffn2)):
                ld.append(dtick("ld", nc.sync.dma_start(out=gT[:, gi, l, :], in_=gsrc[l].rearrange("(c p) -> p c", p=128), allow_slow_non_contiguous=True)))
            ld.append(dtick("ld", nc.sync.dma_start(out=lbl[:, l, :], in_=hg_lb[l].rearrange("(h p) -> p h", p=128), allow_slow_non_contiguous=True)))
        ld.append(dtick("ld", nc.sync.dma_start(out=gfin[:], in_=g_final.rearrange("(c p) -> p c", p=128), allow_slow_non_contiguous=True)))
        ld.append(dtick("ld", nc.sync.dma_start(out=gatt[:], in_=g_att_sub.rearrange("l p -> p l"), allow_slow_non_contiguous=True)))
        ld.append(dtick("ld", nc.sync.dma_start(out=ghg[:], in_=g_hg_norm.rearrange("l p -> p l"), allow_slow_non_contiguous=True)))
        ld.append(dtick("ld", nc.sync.dma_start(out=lam_in[:].rearrange("p l a b -> p (l a b)"),
                                                 in_=att_lambda.rearrange("l a b -> (l a b)").partition_broadcast(128))))
        ldall = ld[-1]
        wait("dve", ldall); wait("act", ldall)
        ct = tick("act", ACT.activation(out=cact[:], in_=cT[:], func=AF.Silu))
        for l in range(DEPTH):
            DVE.tensor_tensor(out=lam_w[:, l, 0, :], in0=lam_in[:, l, 0, :], in1=lam_in[:, l, 1, :], op=ALU.mult)
            DVE.tensor_tensor(out=lam_w[:, l, 1, :], in0=lam_in[:, l, 2, :], in1=lam_in[:, l, 3, :], op=ALU.mult)
        dt_ = tick("dve", DVE.reduce_sum(out=lam_s[:].rearrange("p l a -> p (l a)"), in_=lam_w[:].rearrange("p l a b -> p (l a) b"), axis=mybir.AxisListType.X))
        wait("act", dt_)
        at_ = tick("act", ACT.activation(out=lam_s[:], in_=lam_s[:], func=AF.Exp))
        wait("dve", at_)
        for l in range(DEPTH):
            lam_init = 0.8 - 0.6 * math.exp(-0.3 * l)
            DVE.tensor_tensor(out=lamt[:, l:l + 1], in0=lam_s[:, l, 1:2], in1=lam_s[:, l, 0:1], op=ALU.subtract)
            DVE.tensor_scalar(out=lamt[:, l:l + 1], in0=lamt[:, l:l + 1], scalar1=-lam_init, scalar2=None, op0=ALU.add)
            DVE.tensor_scalar(out=gatt[:, l:l + 1], in0=gatt[:, l:l + 1], scalar1=(1.0 - lam_init) * math.sqrt(128.0), scalar2=None, op0=ALU.mult)
        DVE.tensor_scalar(out=ghg[:], in0=ghg[:], scalar1=math.sqrt(128.0), scalar2=None, op0=ALU.mult)
        DVE.tensor_scalar(out=gfin[:], in0=gfin[:], scalar1=32.0, scalar2=None, op0=ALU.mult)
        at2 = tick("act", ACT.activation(out=lbl[:], in_=lbl[:], func=AF.Exp))
        wait("dve", at2)
        lsum = lam_w[:, 0, 0, 0:NH]
        DVE.tensor_copy(out=lsum, in_=lbl[:, 0, :])
        for l in range(1, DEPTH):
            DVE.tensor_tensor(out=lsum, in0=lsum, in1=lbl[:, l, :], op=ALU.add)
        DVE.reciprocal(out=lsum, in_=lsum)
        DVE.memset(lbt[:, 0, :], 0.0)
        for l in range(1, DEPTH):
            DVE.tensor_tensor(out=lbl[:, l, :], in0=lbl[:, l, :], in1=lsum, op=ALU.mult)
            DVE.tensor_tensor(out=lbt[:, l, :], in0=lbt[:, l - 1, :], in1=lbl[:, l, :], op=ALU.add)
        DVE.tensor_scalar(out=omlt[:], in0=lbt[:], scalar1=-1.0, scalar2=1.0, op0=ALU.mult, op1=ALU.add)
        DVE.tensor_scalar(out=nomlt[:], in0=omlt[:], scalar1=-1.0, scalar2=None, op0=ALU.mult)
        wait("pe", ct)
        wtok = [None, None]
        wfree = [None, None]
        gidx = 0
        for l in range(DEPTH):
            wv = w_ada[l].rearrange("(kc p) c -> p kc c", p=128)
            bk = k.bank_next()
            wait("pe", bk.free)
            outv = bk.ap[:, 0:72 * NSQ].rearrange("p (j s) -> p j s", s=NSQ)
            for g9 in range(9):
                slot = gidx % 2
                wait("sp", wfree[slot])
                wtok[slot] = dtick("wad%d" % slot, nc.sync.dma_start(out=wad[:, slot], in_=wv[:, :, 1024 * g9:1024 * g9 + 1024]))
                wait("pe", wtok[slot])
                for f in range(8):
                    for kc in range(KC):
                        ins = PE.matmul(outv[:, g9 * 8 + f, :], lhsT=wad[:, slot, kc, 128 * f:128 * f + 128], rhs=cact[:, kc, :],
                                        start=(kc == 0), stop=(kc == KC - 1))
                wfree[slot] = tick("pe", ins)
                gidx += 1
            bk.ready = wfree[(gidx - 1) % 2]
            wait("dve", bk.ready)
            ins = DVE.tensor_tensor(out=MOD[:, l].rearrange("p j c s -> p (j c) s"), in0=outv,
                                    in1=badaT[:, l, :].unsqueeze(2).to_broadcast([128, 72, NSQ]), op=ALU.add)
            bk.free = tick("dve", ins)
            for (j_sc, gi_, j_g, half) in ((1, 0, 2, 0.5), (4, 1, 5, 1.0), (7, 2, 8, 0.5)):
                DVE.tensor_scalar(out=MOD[:, l, j_sc], in0=MOD[:, l, j_sc], scalar1=1.0, scalar2=32.0, op0=ALU.add, op1=ALU.mult)
                DVE.tensor_tensor(out=MOD[:, l, j_sc], in0=MOD[:, l, j_sc],
                                  in1=gT[:, gi_, l, :].unsqueeze(2).to_broadcast([128, KC, NSQ]), op=ALU.mult)
                if half != 1.0:
                    DVE.tensor_scalar(out=MOD[:, l, j_g], in0=MOD[:, l, j_g], scalar1=half, scalar2=None, op0=ALU.mult)
        if cfg.DEBUG:
            wait("pool", ("dve", k.cnt["dve"]))
            dmt = dtick("dbg", POOL.dma_start(out=dbg_mod, in_=MOD[:].rearrange("p l j c s -> p (l j c s)")))
            wait("pool", dmt)
        k.barrier()
    k.es_cur = es


    evac_rr = [0]

    def evac_copy(out, in_, bank, scale=None):
        evac_rr[0] += 1
        if scale is not None or evac_rr[0] % 2 == 0:
            wait("act", bank.ready)
            if scale is not None:
                t = tick("act", ACT.mul(out=out, in_=in_, mul=scale))
            else:
                t = tick("act", ACT.copy(out=out, in_=in_))
        else:
            wait("dve", bank.ready)
            t = tick("dve", DVE.tensor_copy(out=out, in_=in_))
        bank.free = t
        return t

    def rstd_act(bank, out_ap, in_ap, eps_total):
        wait("act", bank.ready)
        ACT.activation(out=out_ap, in_=in_ap, func=AF.Ln, bias=eps_total, scale=1.0)
        t = tick("act", ACT.activation(out=out_ap, in_=out_ap, func=AF.Exp, scale=-0.5))
        bank.free = t
        return t

    def norm_phase(T, l, ja, jb, x_tmp):
        NT = T.NT
        sqt = None
        for c in range(KC):
            sqt = tick("act", ACT.activation(out=h_bf[:, c, 0:NT], in_=x_fm[:, c, 0:NT], func=AF.Square))
        bk = k.bank_next()
        wait("pe", bk.free); wait("pe", sqt)
        for c in range(KC):
            ins = PE.matmul(bk.ap[:, 0:NT], lhsT=onesB[:], rhs=h_bf[:, c, 0:NT], start=(c == 0), stop=(c == KC - 1))
        bk.ready = tick("pe", ins)
        wait("dve", rstd_act(bk, rs[:, 0:NT], bk.ap[:, 0:NT], 1024.0 * EPS))
        ht = None
        for c in range(KC):
            for (c0, c1, sq) in T.segs:
                a_ap = gfin[:, c:c + 1] if ja is None else MOD[:, l, ja, c, sq:sq + 1]
                ht = tick("dve", DVE.scalar_tensor_tensor(out=x_tmp[:, c, c0:c1], in0=x_fm[:, c, c0:c1], scalar=a_ap, in1=rs[:, c0:c1], op0=ALU.mult, op1=ALU.mult))
                if jb is not None:
                    ht = tick("dve", DVE.tensor_scalar(out=h_bf[:, c, c0:c1], in0=x_tmp[:, c, c0:c1], scalar1=MOD[:, l, jb, c, sq:sq + 1], scalar2=None, op0=ALU.add))
        return ht

    def norm_mod(T, l, ja, jb):
        with contextlib.ExitStack() as pes:
            k.es_cur = pes
            NT = T.NT
            x_tmp = sb("x_tmp", [128, KC, NT], F32)
            ht = norm_phase(T, l, ja, jb, x_tmp)
            if cfg.DEBUG and T.kind == "p" and T.t == 0 and l == 0 and ja == 1:
                wait("pool", ht)
                wait("pool", dtick("dbg", POOL.dma_start(out=dbg_h1, in_=h_bf[:].rearrange("p c t -> p (c t)"))))
            k.barrier()
        k.es_cur = es
        return ht

    def ffn_phase(T, l, which, h_tok):
        NT = T.NT
        jg = 2 if which == 1 else 8
        with contextlib.ExitStack() as pes:
            k.es_cur = pes
            hid = sb("hid", [128, FC, NT], BF16)
            stmp = sb("stmp", [128, 2, NT], F32)
            st_free = [None, None]
            hid_tok = None
            u = 0
            for g in range(11):
                w = k.wnext("gu%d" % which)
                for j in range(2):
                    jj = 2 * g + j
                    bA = k.bank_next(); bB = k.bank_next()
                    wait("pe", bA.free); wait("pe", bB.free); wait("pe", h_tok)
                    for kc in range(KC):
                        PE.matmul(bA.ap[:, 0:NT], lhsT=w[:, kc, 128 * j:128 * j + 128], rhs=h_bf[:, kc, 0:NT], start=(kc == 0), stop=(kc == KC - 1))
                    for kc in range(KC):
                        ins = PE.matmul(bB.ap[:, 0:NT], lhsT=w[:, kc, 256 + 128 * j:256 + 128 * j + 128], rhs=h_bf[:, kc, 0:NT], start=(kc == 0), stop=(kc == KC - 1))
                    if j == 1:
                        bA.ready = bB.ready = k.wdone(ins)
                    else:
                        bA.ready = bB.ready = tick("pe", ins)
                    s = u % 2
                    wait("act", bA.ready); wait("act", st_free[s])
                    a_t = tick("act", ACT.activation(out=stmp[:, s, 0:NT], in_=bA.ap[:, 0:NT], func=AF.Silu))
                    bA.free = a_t
                    wait("dve", a_t); wait("dve", bB.ready)
                    d_t = tick("dve", DVE.tensor_tensor(out=hid[:, jj, 0:NT], in0=stmp[:, s, 0:NT], in1=bB.ap[:, 0:NT], op=ALU.mult))
                    bB.free = d_t; st_free[s] = d_t; hid_tok = d_t
                    u += 1
            if cfg.DEBUG and T.kind == "p" and T.t == 0 and l == 0 and which == 1:
                wait("pool", hid_tok)
                wait("pool", dtick("dbg", POOL.dma_start(out=dbg_hid, in_=hid[:].rearrange("p c t -> p (c t)"))))
            for m in range(8):
                w = k.wnext("d%d" % which)
                bk = k.bank_next()
                wait("pe", bk.free); wait("pe", hid_tok)
                for kc in range(FC):
                    ins = PE.matmul(bk.ap[:, 0:NT], lhsT=w[:, kc, :], rhs=hid[:, kc, 0:NT], start=(kc == 0), stop=(kc == FC - 1))
                bk.ready = k.wdone(ins)
                wait("dve", bk.ready)
                for (c0, c1, sq) in T.segs:
                    ins = DVE.scalar_tensor_tensor(out=x_fm[:, m, c0:c1], in0=bk.ap[:, c0:c1], scalar=MOD[:, l, jg, m, sq:sq + 1],
                                                   in1=x_fm[:, m, c0:c1], op0=ALU.mult, op1=ALU.add)
                bk.free = tick("dve", ins)
            k.barrier()
        k.es_cur = es

    def load_x(T):
        NT, nblk, bp = T.NT, T.nblk, T.bp
        with contextlib.ExitStack() as pes:
            k.es_cur = pes
            xin = sb("xin", [128, nblk, D], F32)
            if T.kind == "p":
                src = xp[T.t * TT:(T.t + 1) * TT, :].rearrange("(b p) f -> p b f", p=128)
            else:
                src = xs.rearrange("(b p) f -> p b f", p=bp)
            tok = dtick("xld", POOL.dma_start(out=xin[0:bp], in_=src))
            wait("pe", tok)
            for c in range(KC):
                bk = k.bank_next()
                wait("pe", bk.free)
                for b in range(nblk):
                    ins = PE.transpose(out=bk.ap[:, b * bp:(b + 1) * bp], in_=xin[0:bp, b, 128 * c:128 * c + 128], identity=identF[0:bp, 0:bp])
                bk.ready = tick("pe", ins)
                evac_copy(x_fm[:, c, 0:NT], bk.ap[:, 0:NT], bk)
            if cfg.DEBUG and T.kind == "p" and T.t == 0:
                wait("pool", ("dve", k.cnt["dve"])); wait("pool", ("act", k.cnt["act"]))
                wait("pool", dtick("dbg", POOL.dma_start(out=dbg_x, in_=x_fm[:].rearrange("p c t -> p (c t)"))))
            k.barrier()
        k.es_cur = es

    kvst_tok = {}

    def mixing(T, l, h_tok):
        NT, nblk, bp = T.NT, T.nblk, T.bp
        isp = T.kind == "p"
        kout = kp if isp else ks
        vout = vp if isp else vs
        r0 = T.t * TT if isp else 0
        with contextlib.ExitStack() as pes:
            k.es_cur = pes
            qT = sb("qT", [128, NH, NT], BF16)
            kT = sb("kT", [128, NH, NT], BF16)
            v_tm = sb("v_tm", [128, NH, nblk, 128], BF16)
            stage = sb("stage", [128, 3, 512], F32)
            Pb = sb("Pb", [128, 3, 2, NT], BF16)
            fin = sb("fin", [128, 5, NT], F32)
            sqb = sb("sqb", [128, NT], BF16)
            rsa = sb("rsa", [128, NT], F32)
            st_tok = [None, None, None]
            st_n = [0]
            last_store = []

            def store_rows(bank, dst_ap, also=None):
                s = st_n[0] % 3
                st_n[0] += 1
                wait("act", bank.ready); wait("act", st_tok[s])
                t1 = tick("act", ACT.copy(out=stage[0:bp, s, :], in_=bank.ap[0:bp, 0:512]))
                if also is not None:
                    wait("dve", bank.ready)
                    wait("dve", t1)
                    bank.free = tick("dve", DVE.tensor_copy(out=also, in_=bank.ap[0:bp, 0:512].rearrange("p (h e) -> p h e", e=128)))
                    ret = bank.free
                else:
                    bank.free = t1
                    ret = None
                wait("pool", t1)
                st_tok[s] = dtick("st%d" % s, POOL.dma_start(out=dst_ap, in_=stage[0:bp, s, :]))
                last_store.append(st_tok[s])
                return ret

            kt_tok = None
            vt_tok = None
            q_tok = None
            for g in range(4):
                w = k.wnext("qk")
                ins = None
                for j in range(4):
                    bk = k.bank_next()
                    wait("pe", bk.free); wait("pe", h_tok)
                    for kc in range(KC):
                        ins = PE.matmul(bk.ap[:, 0:NT], lhsT=w[:, kc, 128 * j:128 * j + 128], rhs=h_bf[:, kc, 0:NT], start=(kc == 0), stop=(kc == KC - 1))
                    bk.ready = tick("pe", ins)
                    if g < 2:
                        q_tok = evac_copy(qT[:, 4 * g + j, 0:NT], bk.ap[:, 0:NT], bk, scale=0.125)
                    else:
                        wait("dve", bk.ready)
                        kt_tok = bk.free = tick("dve", DVE.tensor_copy(out=kT[:, 4 * (g - 2) + j, 0:NT], in_=bk.ap[:, 0:NT]))
                if g >= 2:
                    for b in range(nblk):
                        bk = k.bank_next()
                        wait("pe", bk.free)
                        for kc in range(KC):
                            ins = PE.matmul(bk.ap[0:bp, 0:512], lhsT=h_bf[:, kc, b * bp:(b + 1) * bp], rhs=w[:, kc, :], start=(kc == 0), stop=(kc == KC - 1))
                        bk.ready = tick("pe", ins)
                        store_rows(bk, kout[l, r0 + b * bp:r0 + (b + 1) * bp, 512 * (g - 2):512 * (g - 2) + 512])
                k.wfree_tok[k.w_cur] = bk.ready
            for g in range(2):
                w = k.wnext("v")
                for b in range(nblk):
                    bk = k.bank_next()
                    wait("pe", bk.free); wait("pe", h_tok)
                    for kc in range(KC):
                        ins = PE.matmul(bk.ap[0:bp, 0:512], lhsT=h_bf[:, kc, b * bp:(b + 1) * bp], rhs=w[:, kc, :], start=(kc == 0), stop=(kc == KC - 1))
                    bk.ready = tick("pe", ins)
                    vt_tok = store_rows(bk, vout[l, r0 + b * bp:r0 + (b + 1) * bp, 512 * g:512 * g + 512], also=v_tm[0:bp, 4 * g:4 * g + 4, b, :])
                k.wfree_tok[k.w_cur] = bk.ready
            if isp and T.t < NPT - 1:
                wait("pool", kt_tok); wait("pool", vt_tok)
                dtick("kvst", POOL.dma_start(out=kscr[l, T.t].rearrange("p (h t) -> p h t", h=NH), in_=kT[:, :, :]))
                kvst_tok[(l, T.t)] = dtick("kvst", POOL.dma_start(out=vscr[l, T.t], in_=v_tm[:].rearrange("p h b e -> p (h b e)")))
                last_store.append(kvst_tok[(l, T.t)])

            O = [k.banks[0], k.banks[1]]
            L = [k.banks[2], k.banks[3]]
            Sp = [(k.banks[4], k.banks[5]), (k.banks[6], k.banks[7])]
            if isp:
                kring = sb("kring", [128, 4, TT], BF16)
                vring = sb("vring", [128, 4, TT], BF16)
            ring_free = [None] * 4
            if not isp:
                NBK = PAST // 128
                kc_in = sb("kc_in", [128, 2, NBK, 128], F32)
                vc_in = sb("vc_in", [128, 2, NBK, 128], F32)
                kcT = sb("kcT", [128, 2, PAST], BF16)
                vcb = sb("vcb", [128, 2, NBK, 128], BF16)
            units = []
            if isp:
                for h in range(NH):
                    units.append(dict(h=h, q0=0, N=NT, seq=None))
            else:
                for i in range(NS):
                    for h in range(NH):
                        units.append(dict(h=h, q0=i * TS, N=TS, seq=i))
            loads = []
            if isp:
                for ui, u in enumerate(units):
                    for s in range(T.t):
                        loads.append((ui, s))
            load_tok = {}
            issued = [0]

            def issue_upto(n):
                while issued[0] < min(n, len(loads)):
                    i = issued[0]
                    ui, s = loads[i]
                    slot = i % 4
                    wait("pool", ring_free[slot]); wait("pool", kvst_tok[(l, s)])
                    h = units[ui]["h"]
                    dtick("kv%d" % slot, POOL.dma_start(out=kring[:, slot, :], in_=kscr[l, s][:, h * TT:(h + 1) * TT]))
                    load_tok[i] = dtick("kv%d" % slot, POOL.dma_start(out=vring[:, slot, :], in_=vscr[l, s][:, h * TT:(h + 1) * TT]))
                    issued[0] += 1

            cin_tok = {}
            cin_free = [None, None]
            cprep_tok = {}

            def cache_load(ui):
                u = units[ui]
                s = ui % 2
                wait("pool", cin_free[s])
                h = u["h"]; i = u["seq"]
                dtick("kv%d" % s, POOL.dma_start(out=kc_in[:, s], in_=ck[l, i, :, 128 * h:128 * h + 128].rearrange("(b p) d -> p b d", p=128)))
                cin_tok[ui] = dtick("kv%d" % s, POOL.dma_start(out=vc_in[:, s], in_=cv[l, i, :, 128 * h:128 * h + 128].rearrange("(b p) d -> p b d", p=128)))

            cprep_free = [None, None]

            def cache_prep(ui):
                s = ui % 2
                wait("pe", cin_tok[ui]); wait("dve", cin_tok[ui]); wait("act", cin_tok[ui])
                wait("dve", cprep_free[s]); wait("act", cprep_free[s])
                t = None
                for b4 in range(NBK // 4):
                    bk = k.bank_next()
                    wait("pe", bk.free)
                    for j in range(4):
                        ins = PE.transpose(out=bk.ap[:, 128 * j:128 * j + 128], in_=kc_in[:, s, 4 * b4 + j, :], identity=identF[:])
                    bk.ready = tick("pe", ins)
                    t = evac_copy(kcT[:, s, 512 * b4:512 * b4 + 512], bk.ap[:, :], bk)
                wait("dve", t)
                t2 = tick("dve", DVE.tensor_copy(out=vcb[:, s], in_=vc_in[:, s]))
                wait("act", t2)
                t3 = tick("act", ACT.copy(out=k.scrA[0:1, 2:3], in_=k.scrA[0:1, 1:2]))
                cin_free[s] = t3
                cprep_tok[ui] = t3

            pending = []
            pn = [0]
            sn = [0]
            p_free = [None, None, None]
            wait("pe", q_tok); wait("pe", kt_tok); wait("pe", vt_tok)
            if not isp:
                cache_load(0)
            for ui, u in enumerate(units):
                h, q0, N = u["h"], u["q0"], u["N"]
                blocks = []
                if isp:
                    base = sum(1 for (a, _) in loads if a < ui)
                    for s in range(T.t):
                        li = base + s
                        for b in range(4):
                            blocks.append(dict(kt=kring[:, li % 4, 128 * b:128 * b + 128], v=vring[:, li % 4, 128 * b:128 * b + 128], nk=128, qs=0, zero=False,
                                               li=li, last=(b == 3)))
                    for b in range(nblk):
                        blocks.append(dict(kt=kT[:, h, 128 * b:128 * b + 128], v=v_tm[:, h, b, :], nk=128, qs=128 * b, zero=True, li=None, last=False))
                else:
                    s2 = ui % 2
                    if ui + 1 < len(units):
                        cache_load(ui + 1)
                    cache_prep(ui)
                    wait("pe", cprep_tok[ui])
                    for b in range(NBK):
                        blocks.append(dict(kt=kcT[:, s2, 128 * b:128 * b + 128], v=vcb[:, s2, b, :], nk=128, qs=0, zero=False, li=None, last=False))
                    i = u["seq"]
                    blocks.append(dict(kt=kT[:, h, i * TS:(i + 1) * TS], v=v_tm[0:TS, h, i, :], nk=TS, qs=0, zero=False, li=None, last=False))
                nb = len(blocks)

                def emit_S(j):
                    bl = blocks[j]
                    pr = Sp[sn[0] % 2]
                    bl["pr"] = pr
                    sn[0] += 1
                    if bl["li"] is not None:
                        issue_upto(bl["li"] + 3)
                        wait("pe", load_tok[bl["li"]])
                    wait("pe", pr[0].free); wait("pe", pr[1].free)
                    nk, qs = bl["nk"], bl["qs"]
                    PE.matmul(pr[0].ap[0:nk, qs:N], lhsT=bl["kt"][0:64, :], rhs=qT[0:64, h, q0 + qs:q0 + N], start=True, stop=True)
                    ins = PE.matmul(pr[1].ap[0:nk, qs:N], lhsT=bl["kt"][64:128, :], rhs=qT[64:128, h, q0 + qs:q0 + N], start=True, stop=True)
                    pr[0].ready = pr[1].ready = tick("pe", ins)

                wait("pe", O[0].free); wait("pe", O[1].free); wait("pe", L[0].free); wait("pe", L[1].free)
                emit_S(0)
                for j in range(nb):
                    bl = blocks[j]
                    nk, qs = bl["nk"], bl["qs"]
                    pr = bl["pr"]
                    ps = pn[0] % 3
                    pn[0] += 1
                    wait("act", pr[0].ready); wait("act", p_free[ps])
                    e_t = tick("act", ACT.activation(out=Pb[0:nk, ps, :, qs:N], in_=pr[0].pair[0:nk, :, qs:N], func=AF.Exp))
                    if bl["zero"]:
                        e_t = tick("act", ACT.mul(out=Pb[64:128, ps, :, qs:qs + 64], in_=Pb[64:128, ps, :, qs:qs + 64], mul=0.0))
                    pr[0].free = pr[1].free = e_t
                    if j + 1 < nb:
                        emit_S(j + 1)
                    wait("pe", e_t)
                    for m in range(2):
                        PE.matmul(O[m].ap[:, qs:N], lhsT=bl["v"], rhs=Pb[0:nk, ps, m, qs:N], start=(j == 0), stop=(j == nb - 1))
                    for m in range(2):
                        ins = PE.matmul(L[m].ap[:, qs:N], lhsT=onesB[0:nk, :], rhs=Pb[0:nk, ps, m, qs:N], start=(j == 0), stop=(j == nb - 1))
                    pv_t = tick("pe", ins)
                    p_free[ps] = pv_t
                    if bl["last"]:
                        ring_free[bl["li"] % 4] = pv_t
                    if j == 1 and pending:
                        pending.pop(0)()
                while pending:
                    pending.pop(0)()
                wait("dve", pv_t)
                DVE.reciprocal(out=fin[:, 0, 0:N], in_=L[0].ap[:, 0:N])
                DVE.reciprocal(out=fin[:, 1, 0:N], in_=L[1].ap[:, 0:N])
                DVE.tensor_tensor(out=fin[:, 2, 0:N], in0=O[0].ap[:, 0:N], in1=fin[:, 0, 0:N], op=ALU.mult)
                f_t = tick("dve", DVE.tensor_tensor(out=fin[:, 3, 0:N], in0=O[1].ap[:, 0:N], in1=fin[:, 1, 0:N], op=ALU.mult))
                O[0].free = O[1].free = L[0].free = L[1].free = f_t
                DVE.scalar_tensor_tensor(out=fin[:, 4, 0:N], in0=fin[:, 3, 0:N], scalar=lamt[:, l:l + 1], in1=fin[:, 2, 0:N], op0=ALU.mult, op1=ALU.add)
                sq_t = tick("dve", DVE.tensor_tensor(out=sqb[:, 0:N], in0=fin[:, 4, 0:N], in1=fin[:, 4, 0:N], op=ALU.mult))

                def fin2(h=h, q0=q0, N=N, sq_t=sq_t):
                    pr = Sp[sn[0] % 2]
                    sn[0] += 1
                    bk = pr[0]
                    wait("pe", pr[0].free); wait("pe", pr[1].free); wait("pe", sq_t)
                    ins = PE.matmul(bk.ap[:, 0:N], lhsT=onesB[:], rhs=sqb[:, 0:N], start=True, stop=True)
                    pr[0].ready = pr[1].ready = tick("pe", ins)
                    t = rstd_act(bk, rsa[:, 0:N], bk.ap[:, 0:N], 128.0 * EPS)
                    pr[0].free = pr[1].free = t
                    wait("dve", t)
                    return tick("dve", DVE.scalar_tensor_tensor(out=o_att[:, h, q0:q0 + N], in0=fin[:, 4, 0:N], scalar=gatt[:, l:l + 1], in1=rsa[:, 0:N], op0=ALU.mult, op1=ALU.mult))
                pending.append(fin2)
            oa_tok = None
            while pending:
                oa_tok = pending.pop(0)()
            if cfg.DEBUG and isp and T.t == 0 and l == 0:
                wait("pool", oa_tok)
                last_store.append(dtick("dbg", POOL.dma_start(out=dbg_att, in_=o_att[:])))
            k.barrier(pool_tokens=last_store)
        k.es_cur = es

        with contextlib.ExitStack() as pes:
            k.es_cur = pes
            if isp:
                Lc = 64; h2 = 32
                nch = NT // 64
                M0 = M0p
            else:
                Lc = TS; h2 = TS
                nch = NS
                M0 = M0s
            two = h2 < Lc
            hq = sb("hq", [128, NH, NT], F32)
            sigf = sb("sigf", [128, NH, NT], F32)
            hgt = sb("hgt", [128, NH, NT], BF16)
            hi_tm = sb("hi_tm", [64, NH, nch, 128], BF16)
            fw = sb("fw", [128, 10, NT], F32)
            qk4 = sb("qk4", [128, 2, 5, NT], BF16)
            kd_tm = sb("kd_tm", [64, 2, nch, 128], BF16)
            Am = sb("Am", [64, 2, nch, 64], BF16)
            sm = sb("sm", [128, 4, TT // 32], F32)
            osq = sb("osq", [128, NT], BF16)
            ot = sb("ot", [128, NT], F32)
            rsh = sb("rsh", [128, NT], F32)
            if not isp:
                Ssm = sb("Ssm", [128, NS, NH, 128], F32)
                Sbs = sb("Sbs", [128, NS, NH, 128], BF16)
            if two:
                DVE.memset(Am[h2:Lc, :, :, 0:h2], 0.0)
            hq_tok = sig_tok = hg_tok = hi_tok = None
            for i in range(8):
                w = k.wnext("hg")
                if i in (4, 5):
                    for b in range(nch):
                        bk = k.bank_next()
                        wait("pe", bk.free); wait("pe", h_tok)
                        for kc in range(KC):
                            ins = PE.matmul(bk.ap[0:Lc, 0:512], lhsT=h_bf[:, kc, b * Lc:(b + 1) * Lc], rhs=w[:, kc, :], start=(kc == 0), stop=(kc == KC - 1))
                        bk.ready = tick("pe", ins)
                        hi_tok = evac_copy(hi_tm[0:Lc, 4 * (i - 4):4 * (i - 4) + 4, b, :], bk.ap[0:Lc, 0:512].rearrange("p (h e) -> p h e", e=128), bk)
                else:
                    for j in range(4):
                        bk = k.bank_next()
                        wait("pe", bk.free); wait("pe", h_tok)
                        for kc in range(KC):
                            ins = PE.matmul(bk.ap[:, 0:NT], lhsT=w[:, kc, 128 * j:128 * j + 128], rhs=h_bf[:, kc, 0:NT], start=(kc == 0), stop=(kc == KC - 1))
                        bk.ready = tick("pe", ins)
                        wait("act", bk.ready)
                        if i < 2:
                            hq_tok = bk.free = tick("act", ACT.activation(out=hq[:, 4 * i + j, 0:NT], in_=bk.ap[:, 0:NT], func=AF.Silu))
                        elif i < 4:
                            sig_tok = bk.free = tick("act", ACT.activation(out=sigf[:, 4 * (i - 2) + j, 0:NT], in_=bk.ap[:, 0:NT], func=AF.Sigmoid))
                        else:
                            hg_tok = bk.free = tick("act", ACT.activation(out=hgt[:, 4 * (i - 6) + j, 0:NT], in_=bk.ap[:, 0:NT], func=AF.Silu))
                k.wfree_tok[k.w_cur] = bk.ready
            hi_toks = [("act", k.cnt["act"]), ("dve", k.cnt["dve"])]
            if isp:
                sb_tok = tick("dve", DVE.tensor_copy(out=Sb[:], in_=S_all[:, l]))
            else:
                s_ld = dtick("sld", POOL.dma_start(out=Ssm[:].rearrange("p i h v -> p (i h) v"), in_=st[l].rearrange("i h kk v -> kk (i h) v")))
                wait("dve", s_ld)
                sb_tok = tick("dve", DVE.tensor_copy(out=Sbs[:], in_=Ssm[:]))
            wait("dve", sig_tok); wait("dve", hq_tok)
            for t_ in hi_toks:
                wait("pe", t_)
            pe_use = [None, None]
            last_o = None

            def c3(ap):
                return ap.rearrange("p (c t) -> p c t", t=Lc)
            for h in range(NH):
                hs = h % 2
                fA, fK, fB, fC, fD, fE, gA, gB, gC, gD = [fw[:, i, 0:NT] for i in range(10)]
                qe, qa2, ka0, ka1, kd = [qk4[:, hs, i, 0:NT] for i in range(5)]
                DVE.tensor_scalar(out=fA, in0=sigf[:, h, 0:NT], scalar1=omlt[:, l, h:h + 1], scalar2=lbt[:, l, h:h + 1], op0=ALU.mult, op1=ALU.add)
                t1 = tick("dve", DVE.tensor_scalar(out=fA, in0=fA, scalar1=TINY, scalar2=None, op0=ALU.max))
                wait("act", t1)
                t2 = tick("act", ACT.activation(out=fB, in_=fA, func=AF.Ln))
                DVE.tensor_scalar(out=fK, in0=sigf[:, h, 0:NT], scalar1=nomlt[:, l, h:h + 1], scalar2=omlt[:, l, h:h + 1], op0=ALU.mult, op1=ALU.add)
                wait("dve", t2)
                DVE.tensor_tensor_scan(out=fC, data0=M0[:, 0:NT], data1=fB, initial=0.0, op0=ALU.mult, op1=ALU.add)
                b3 = c3(fC)
                if two:
                    DVE.tensor_tensor(out=c3(fD), in0=b3, in1=b3[:, :, h2 - 1:h2].to_broadcast([128, nch, Lc]), op=ALU.subtract)
                DVE.tensor_tensor(out=c3(fE), in0=b3[:, :, Lc - 1:Lc].to_broadcast([128, nch, Lc]), in1=b3, op=ALU.subtract)
                t3 = tick("dve", DVE.tensor_copy(out=sm[:, 0, 0:nch], in_=b3[:, :, Lc - 1]))
                wait("act", t3)
                ACT.activation(out=gC, in_=fC, func=AF.Exp)
                ACT.activation(out=c3(gA)[:, :, 0:h2], in_=c3(fC)[:, :, 0:h2], func=AF.Exp, scale=-1.0)
                if two:
                    ACT.activation(out=c3(gA)[:, :, h2:Lc], in_=c3(fD)[:, :, h2:Lc], func=AF.Exp)
                    ACT.activation(out=gB, in_=fD, func=AF.Exp, scale=-1.0)
                ACT.activation(out=gD, in_=fE, func=AF.Exp)
                t4 = tick("act", ACT.activation(out=sm[:, 1, 0:nch], in_=sm[:, 0, 0:nch], func=AF.Exp))
                wait("dve", t4); wait("dve", pe_use[hs])
                DVE.tensor_tensor(out=qe, in0=hq[:, h, 0:NT], in1=gC, op=ALU.mult)
                DVE.tensor_tensor(out=c3(ka0)[:, :, 0:h2], in0=c3(fK)[:, :, 0:h2], in1=c3(gA)[:, :, 0:h2], op=ALU.mult)
                if two:
                    DVE.tensor_tensor(out=c3(qa2)[:, :, h2:Lc], in0=c3(hq[:, h, 0:NT])[:, :, h2:Lc], in1=c3(gA)[:, :, h2:Lc], op=ALU.mult)
                    DVE.tensor_tensor(out=ka1, in0=fK, in1=gB, op=ALU.mult)
                DVE.tensor_copy(out=sm[:, 2 + hs, 0:nch], in_=sm[:, 1, 0:nch])
                t5 = tick("dve", DVE.tensor_tensor(out=kd, in0=fK, in1=gD, op=ALU.mult))
                ebl = sm[:, 2 + hs, :]
                bA = k.banks[0]; bT = k.banks[1]
                wait("pe", t5); wait("pe", bA.free); wait("pe", bT.free)
                for ci in range(nch):
                    cs = ci * Lc
                    ins = PE.matmul(bA.ap[0:h2, cs:cs + h2], lhsT=ka0[:, cs:cs + h2], rhs=qe[:, cs:cs + h2], start=True, stop=True)
                    if two:
                        ins = PE.matmul(bA.ap[0:Lc, cs + h2:cs + Lc], lhsT=ka1[:, cs:cs + Lc], rhs=qa2[:, cs + h2:cs + Lc], start=True, stop=True)
                bA.ready = tick("pe", ins)
                bTv = bT.ap.bitcast(BF16)
                for ci in range(nch):
                    ins = PE.transpose(out=bTv[0:Lc, 128 * ci:128 * ci + 128], in_=kd[:, ci * Lc:(ci + 1) * Lc], identity=identB[:])
                bT.ready = tick("pe", ins)
                wait("dve", bA.ready)
                bA3 = bA.ap[:, 0:nch * Lc].rearrange("p (c t) -> p c t", t=Lc)
                am_t = tick("dve", DVE.tensor_tensor(out=Am[0:h2, hs, :, 0:Lc], in0=bA3[0:h2], in1=triP[0:h2, 0:Lc].unsqueeze(1).to_broadcast([h2, nch, Lc]), op=ALU.mult))
                if two:
                    am_t = tick("dve", DVE.tensor_tensor(out=Am[h2:Lc, hs, :, h2:Lc], in0=bA3[h2:Lc, :, h2:Lc],
                                                         in1=triP[h2:Lc, h2:Lc].unsqueeze(1).to_broadcast([Lc - h2, nch, Lc - h2]), op=ALU.mult))
                bA.free = am_t
                wait("act", bT.ready); wait("act", pe_use[hs])
                kt_t = bT.free = tick("act", ACT.copy(out=kd_tm[0:Lc, hs], in_=bTv[0:Lc, 0:nch * 128].rearrange("p (b e) -> p b e", e=128)))
                bO = k.banks[2 + (h % 2)]
                wait("pe", bO.free); wait("pe", am_t); wait("pe", kt_t)
                for ci in range(nch):
                    cs = ci * Lc
                    if isp:
                        Sst = S_all[:, l, h, :]; Sbh = Sb[:, h, :]
                    else:
                        Sst = Ssm[:, ci, h, :]; Sbh = Sbs[:, ci, h, :]
                    wait("pe", sb_tok)
                    PE.matmul(bO.ap[:, cs:cs + Lc], lhsT=Sbh, rhs=qe[:, cs:cs + Lc], start=True, stop=False)
                    PE.matmul(bO.ap[:, cs:cs + Lc], lhsT=hi_tm[0:Lc, h, ci, :], rhs=Am[0:Lc, hs, ci, 0:Lc], start=False, stop=True)
                    bS = k.banks[4 + (ci % 2)]
                    wait("pe", bS.free)
                    ins = PE.matmul(bS.ap[:, 0:128], lhsT=kd_tm[0:Lc, hs, ci, :], rhs=hi_tm[0:Lc, h, ci, :], start=True, stop=True)
                    bS.ready = tick("pe", ins)
                    wait("dve", bS.ready)
                    bS.free = tick("dve", DVE.scalar_tensor_tensor(out=Sst, in0=Sst, scalar=ebl[:, ci:ci + 1], in1=bS.ap[:, 0:128], op0=ALU.mult, op1=ALU.add))
                    sb_tok = tick("dve", DVE.tensor_copy(out=Sbh, in_=Sst))
                bO.ready = bS.ready
                pe_use[hs] = bS.ready
                wait("act", bO.ready)
                q_t = tick("act", ACT.activation(out=osq[:, 0:NT], in_=bO.ap[:, 0:NT], func=AF.Square))
                bq = k.banks[6 + (h % 2)]
                wait("pe", bq.free); wait("pe", q_t)
                bq.ready = tick("pe", PE.matmul(bq.ap[:, 0:NT], lhsT=onesB[:], rhs=osq[:, 0:NT], start=True, stop=True))
                wait("dve", rstd_act(bq, rsh[:, 0:NT], bq.ap[:, 0:NT], 128.0 * EPS)); wait("dve", hg_tok)
                bO.free = tick("dve", DVE.scalar_tensor_tensor(out=ot[:, 0:NT], in0=bO.ap[:, 0:NT], scalar=ghg[:, l:l + 1], in1=rsh[:, 0:NT], op0=ALU.mult, op1=ALU.mult))
                last_o = tick("dve", DVE.tensor_tensor(out=o_h[:, h, 0:NT], in0=ot[:, 0:NT], in1=hgt[:, h, 0:NT], op=ALU.mult))
                if cfg.DEBUG and isp and T.t == 0 and l == 0:
                    wait("pool", last_o)
                    dd_ = dtick("dbg", POOL.dma_start(out=dbg_h[:, h, :], in_=ot[:, :]))
                    wait("dve", dd_)
            ptoks = []
            if isp and T.t == NPT - 1:
                wait("pool", sb_tok)
                ptoks.append(dtick("out", POOL.dma_start(out=sp_o[l].rearrange("h kk v -> kk h v"), in_=S_all[:, l])))
            if not isp:
                wait("pool", sb_tok)
                ptoks.append(dtick("out", POOL.dma_start(out=ss_o[l].rearrange("i h kk v -> kk (i h) v"), in_=Ssm[:].rearrange("p i h v -> p (i h) v"))))
            k.barrier(pool_tokens=ptoks)
        k.es_cur = es

        with contextlib.ExitStack() as pes:
            k.es_cur = pes
            mt = sb("mt", [128, 2, 4, NT], F32)
            mt_free = [None, None]
            m_tok = None
            for m in range(8):
                w = k.wnext("mrg")
                bks = [k.bank_next() for _ in range(4)]
                srcs = [h_bf, h_bf, o_att, o_h]
                for bi in range(4):
                    wait("pe", bks[bi].free)
                wait("pe", h_tok); wait("pe", oa_tok); wait("pe", last_o)
                for bi in range(4):
                    for kc in range(KC):
                        ins = PE.matmul(bks[bi].ap[:, 0:NT], lhsT=w[:, kc, 128 * bi:128 * bi + 128], rhs=srcs[bi][:, kc, 0:NT], start=(kc == 0), stop=(kc == KC - 1))
                r_t = k.wdone(ins)
                for bi in range(4):
                    bks[bi].ready = r_t
                s = m % 2
                wait("act", r_t); wait("act", mt_free[s])
                ACT.activation(out=mt[:, s, 0, 0:NT], in_=bks[0].ap[:, 0:NT], func=AF.Sigmoid)
                a_t = tick("act", ACT.activation(out=mt[:, s, 1, 0:NT], in_=bks[1].ap[:, 0:NT], func=AF.Sigmoid))
                bks[0].free = bks[1].free = a_t
                wait("dve", a_t); wait("dve", r_t)
                DVE.tensor_tensor(out=mt[:, s, 2, 0:NT], in0=mt[:, s, 0, 0:NT], in1=bks[2].ap[:, 0:NT], op=ALU.mult)
                d_t = tick("dve", DVE.tensor_tensor(out=mt[:, s, 3, 0:NT], in0=mt[:, s, 1, 0:NT], in1=bks[3].ap[:, 0:NT], op=ALU.mult))
                bks[2].free = bks[3].free = d_t
                m_tok = tick("dve", DVE.tensor_tensor(out=merged[:, m, 0:NT], in0=mt[:, s, 2, 0:NT], in1=mt[:, s, 3, 0:NT], op=ALU.add))
                mt_free[s] = m_tok
            for g in range(2):
                w = k.wnext("wo")
                for j in range(4):
                    m = 4 * g + j
                    bk = k.bank_next()
                    wait("pe", bk.free); wait("pe", m_tok)
                    for kc in range(KC):
                        ins = PE.matmul(bk.ap[:, 0:NT], lhsT=w[:, kc, 128 * j:128 * j + 128], rhs=merged[:, kc, 0:NT], start=(kc == 0), stop=(kc == KC - 1))
                    bk.ready = tick("pe", ins)
                    wait("dve", bk.ready)
                    for (c0, c1, sq) in T.segs:
                        ins = DVE.scalar_tensor_tensor(out=x_fm[:, m, c0:c1], in0=bk.ap[:, c0:c1], scalar=MOD[:, l, 5, m, sq:sq + 1],
                                                       in1=x_fm[:, m, c0:c1], op0=ALU.mult, op1=ALU.add)
                    bk.free = tick("dve", ins)
                k.wfree_tok[k.w_cur] = bk.ready
            k.barrier()
        k.es_cur = es

    def final_out(T):
        NT, nblk, bp = T.NT, T.nblk, T.bp
        with contextlib.ExitStack() as pes:
            k.es_cur = pes
            x_tmp = sb("x_tmp", [128, KC, NT], F32)
            ystage = sb("ystage", [128, nblk, D], F32)
            yt = norm_phase(T, 0, None, None, x_tmp)
            wait("pe", yt)
            et = None
            for b in range(nblk):
                for cg in range(2):
                    bk = k.bank_next()
                    wait("pe", bk.free)
                    for j in range(4):
                        ins = PE.transpose(out=bk.ap[0:bp, 128 * j:128 * j + 128], in_=x_tmp[:, 4 * cg + j, b * bp:(b + 1) * bp], identity=identF[:])
                    bk.ready = tick("pe", ins)
                    et = evac_copy(ystage[0:bp, b, 512 * cg:512 * cg + 512], bk.ap[0:bp, 0:512], bk)
                    wait("pool", et)
            wait("pool", ("act", k.cnt["act"])) if k.cnt["act"] else None
            wait("pool", ("dve", k.cnt["dve"])) if k.cnt["dve"] else None
            if T.kind == "p":
                dst = yp[T.t * TT:(T.t + 1) * TT, :].rearrange("(b p) f -> p b f", p=128)
            else:
                dst = ys.rearrange("(b p) f -> p b f", p=bp)
            ot_ = dtick("out", POOL.dma_start(out=dst, in_=ystage[0:bp]))
            k.barrier(pool_tokens=[ot_])
        k.es_cur = es

    DVE.memset(S_all[:], 0.0)
    for T in tiles:
        load_x(T)
        for l in range(DEPTH):
            ht = norm_mod(T, l, 1, 0)
            ffn_phase(T, l, 1, ht)
            ht = norm_mod(T, l, 4, 3)
            mixing(T, l, ht)
            ht = norm_mod(T, l, 7, 6)
            ffn_phase(T, l, 2, ht)
        final_out(T)
    for s in ("out", "st0", "st1", "st2", "kvst") + (("dbg",) if cfg.DEBUG else ()):
        if k.cnt[s]:
            POOL.wait_ge(k.sem[s], k.cnt[s])
    assert k.w_used == len(k.wseq)
    es.close()
    return nc


_WNAMES = ["w_ada", "b_ada", "g_ffn1", "w_ffn1_gu", "w_ffn1_d", "g_mix", "w_in", "att_lambda", "g_att_sub",
           "hg_lb_logits", "g_hg_norm", "w_br_att", "w_br_hg", "w_out", "g_ffn2", "w_ffn2_gu", "w_ffn2_d", "g_final"]


def run(cfg, inputs, n_cores=8):
    nc = build(cfg)
    NS, TS = cfg.NS, cfg.TS
    f = lambda a: np.ascontiguousarray(np.asarray(a, dtype=np.float32))
    in_maps = []
    nb = inputs["x_prompt"].shape[0]
    for c in range(n_cores):
        sq = (c * nb) // n_cores
        s0 = c * NS
        m = {
            "xp": f(inputs["x_prompt"][sq]),
            "xs": f(inputs["x_sample"][s0:s0 + NS]).reshape(NS * TS, D),
            "ck": f(np.asarray(inputs["cache_k"])[:, s0:s0 + NS].reshape(cfg.DEPTH, NS, cfg.PAST, D)),
            "cv": f(np.asarray(inputs["cache_v"])[:, s0:s0 + NS].reshape(cfg.DEPTH, NS, cfg.PAST, D)),
            "st": f(np.asarray(inputs["state_hgrn"])[:, s0:s0 + NS]),
            "cc": f(np.concatenate([np.asarray(inputs["c_prompt"])[sq:sq + 1], np.asarray(inputs["c_sample"])[s0:s0 + NS]], 0)),
        }
        for n in _WNAMES:
            m[n] = f(inputs[n])
        in_maps.append(m)
    res = run_bass_kernel_spmd(nc, in_maps, core_ids=list(range(n_cores)))
    R = res.results
    run.last = R
    per = n_cores // nb
    DEPTH = cfg.DEPTH
    y_prompt = np.stack([R[per * b]["yp"] for b in range(nb)], 0)
    k_prompt = np.stack([R[per * b]["kp"] for b in range(nb)], 1).reshape(DEPTH, nb, cfg.SEQ, NH, 128)
    v_prompt = np.stack([R[per * b]["vp"] for b in range(nb)], 1).reshape(DEPTH, nb, cfg.SEQ, NH, 128)
    s_prompt = np.stack([R[per * b]["sp"] for b in range(nb)], 1)
    y_sample = np.concatenate([R[c]["ys"].reshape(NS, TS, D) for c in range(n_cores)], 0)
    k_sample = np.concatenate([R[c]["ks"].reshape(DEPTH, NS, TS, NH, 128) for c in range(n_cores)], 1)
    v_sample = np.concatenate([R[c]["vs"].reshape(DEPTH, NS, TS, NH, 128) for c in range(n_cores)], 1)
    s_sample = np.concatenate([R[c]["ss"] for c in range(n_cores)], 1)
    return tuple(np.ascontiguousarray(a, dtype=np.float32) for a in
                 (y_prompt, y_sample, k_prompt, v_prompt, s_prompt, k_sample, v_sample, s_sample))


def kernel(**inputs):
    cfg = Cfg()
    return run(cfg, inputs, 8)
```

```python
import contextlib
import math
import numpy as np
import concourse.bass as bass
import concourse.mybir as mybir
from concourse.bass_utils import run_bass_kernel_spmd

F32 = mybir.dt.float32
BF16 = mybir.dt.bfloat16
AF = mybir.ActivationFunctionType
ALU = mybir.AluOpType

D = 1024
DFF = 2816
NH = 8
KC = 8
FC = 22
DIN = 9216
EPS = 1e-6
TINY = 1e-30
NWG = 62
WSLOT = 4096
NBUF = 4


class Cfg:
    def __init__(self, SEQ=8192, DEPTH=4, PAST=2048, TS=32, NS=2, TT=512):
        self.SEQ, self.DEPTH, self.PAST, self.TS, self.NS, self.TT = SEQ, DEPTH, PAST, TS, NS, TT
        self.DEBUG = False


class Tile:
    pass


class B:
    def __init__(self, cfg):
        self.cfg = cfg
        self.nc = bass.Bass("TRN2", target_bir_lowering=False)
        self.es = contextlib.ExitStack()
        self.cnt = {}
        self.sem = {}
        self.waited = {}
        self.eng = {"pe": self.nc.tensor, "act": self.nc.scalar, "dve": self.nc.vector,
                    "pool": self.nc.gpsimd, "sp": self.nc.sync}
        for e in ("pe", "act", "dve", "pool"):
            self.newsem(e)
        self.bar_n = 0
        self.newsem("bar")
        self.last_tok = {}
        self.last_ins = {}

    def newsem(self, name):
        self.sem[name] = self.es.enter_context(self.nc.semaphore(name))
        self.cnt[name] = 0
        return name

    def tick(self, E, ins):
        if self.last_ins.get(E) is ins and self.last_tok.get(E) is not None:
            return self.last_tok[E]
        ins.then_inc(self.sem[E], 1)
        self.cnt[E] += 1
        tok = (E, self.cnt[E])
        if self.last_ins.get(E) is ins:
            self.last_tok[E] = tok
        return tok

    def pre_issue(self, E):
        li = self.last_ins.get(E)
        if li is None:
            return
        tok = self.tick(E, li)
        k = (E, E)
        if self.waited.get(k, 0) < tok[1]:
            self.eng[E].wait_ge(self.sem[E], tok[1])
            self.waited[k] = tok[1]

    def post_issue(self, E, ins):
        self.last_ins[E] = ins
        self.last_tok[E] = None

    def dtick(self, S, ins):
        ins.then_inc(self.sem[S], 16)
        self.cnt[S] += 16
        return (S, self.cnt[S])

    def wait(self, who, tok):
        if tok is None:
            return
        S, v = tok
        if S == who:
            return
        k = (who, S)
        if self.waited.get(k, 0) >= v:
            return
        self.eng[who].wait_ge(self.sem[S], v)
        self.waited[k] = v

    def sb(self, name, shape, dt):
        self.uid = getattr(self, "uid", 0) + 1
        return self.es_cur.enter_context(self.nc.sbuf_tensor("%s_%d" % (name, self.uid), shape, dt))

    def barrier(self, pool_tokens=()):
        nc = self.nc
        for t in pool_tokens:
            self.wait("pool", t)
        self.pre_issue("act")
        nc.scalar.copy(out=self.scrA[0:1, 0:1], in_=self.scrA[0:1, 1:2]).then_inc(self.sem["bar"], 1)
        self.pre_issue("dve")
        nc.vector.memset(self.scrV[0:1, 0:1], 0.0).then_inc(self.sem["bar"], 1)
        nc.gpsimd.memset(self.scrP[0:1, 0:1], 0.0).then_inc(self.sem["bar"], 1)
        self.bar_n += 3
        for e in ("act", "dve", "pool"):
            self.eng[e].wait_ge(self.sem["bar"], self.bar_n)
        self.last_ins["act"] = None
        self.last_ins["dve"] = None

    def bank_next(self):
        b = self.banks[self.bank_rr % 8]
        self.bank_rr += 1
        return b

    def wnext(self, kind):
        nc = self.nc
        i = self.w_used
        assert self.wseq[i][0] == kind, (self.wseq[i], kind)
        upto = min(i + NBUF - 1, len(self.wseq) - 1)
        while self.w_issued <= upto:
            j = self.w_issued
            slot = j % NBUF
            if j >= NBUF:
                self.wait("sp", self.wfree_tok[j - NBUF])
            _, l, gi, nk, ncol, first = self.wseq[j]
            if first:
                self.wait("sp", self.conv_tok[l])
            src = self.wscr[l * NWG + gi, :, 0:nk * ncol]
            ins = nc.sync.dma_start(out=self.wbuf[:, slot, 0:nk * ncol], in_=src)
            self.wld_tok[j] = self.dtick("wld%d" % slot, ins)
            self.w_issued += 1
        self.wait("pe", self.wld_tok[i])
        _, l, gi, nk, ncol, _ = self.wseq[i]
        self.w_used += 1
        self.w_cur = i
        return self.wbuf[:, i % NBUF, 0:nk * ncol].rearrange("p (k c) -> p k c", k=nk)

    def wdone(self, ins):
        self.wfree_tok[self.w_cur] = self.tick("pe", ins)
        return self.wfree_tok[self.w_cur]


class EngProxy:
    def __init__(self, k, name, eng):
        self._k, self._name, self._eng = k, name, eng

    def __getattr__(self, attr):
        f = getattr(self._eng, attr)
        if attr in ("wait_ge",):
            return f
        k, name = self._k, self._name

        def wrapped(*a, **kw):
            k.pre_issue(name)
            ins = f(*a, **kw)
            k.post_issue(name, ins)
            return ins
        return wrapped


def _wgroups():
    g = []
    for i in range(11):
        g.append(("gu1", i, 8, 512 if i < 10 else 512))
    for i in range(8):
        g.append(("d1", i, 22, 128))
    for i in range(4):
        g.append(("qk", i, 8, 512))
    for i in range(2):
        g.append(("v", i, 8, 512))
    for i in range(8):
        g.append(("hg", i, 8, 512))
    for i in range(8):
        g.append(("mrg", i, 8, 512))
    for i in range(2):
        g.append(("wo", i, 8, 512))
    for i in range(11):
        g.append(("gu2", i, 8, 512))
    for i in range(8):
        g.append(("d2", i, 22, 128))
    assert len(g) == NWG
    return g


def build(cfg):
    k = B(cfg)
    nc = k.nc
    es = k.es
    SEQ, DEPTH, PAST, TS, NS, TT = cfg.SEQ, cfg.DEPTH, cfg.PAST, cfg.TS, cfg.NS, cfg.TT
    NTS = NS * TS
    NPT = SEQ // TT
    NSQ = 1 + NS

    def din(name, shape):
        return nc.dram_tensor(name, list(shape), F32, kind="ExternalInput").ap()

    def dout(name, shape):
        return nc.dram_tensor(name, list(shape), F32, kind="ExternalOutput").ap()

    xp = din("xp", [SEQ, D]); xs = din("xs", [NTS, D])
    ck = din("ck", [DEPTH, NS, PAST, D]); cv = din("cv", [DEPTH, NS, PAST, D])
    st = din("st", [DEPTH, NS, NH, 128, 128]); cc = din("cc", [NSQ, D])
    w_ada = din("w_ada", [DEPTH, D, 9 * D]); b_ada = din("b_ada", [DEPTH, 9 * D])
    g_ffn1 = din("g_ffn1", [DEPTH, D]); w_gu1 = din("w_ffn1_gu", [DEPTH, D, 2 * DFF]); w_d1 = din("w_ffn1_d", [DEPTH, DFF, D])
    g_mix = din("g_mix", [DEPTH, D]); w_in = din("w_in", [DEPTH, D, DIN])
    att_lambda = din("att_lambda", [DEPTH, 4, 64]); g_att_sub = din("g_att_sub", [DEPTH, 128])
    hg_lb = din("hg_lb_logits", [DEPTH, D]); g_hg_norm = din("g_hg_norm", [DEPTH, 128])
    w_ba = din("w_br_att", [DEPTH, D, D]); w_bh = din("w_br_hg", [DEPTH, D, D]); w_o = din("w_out", [DEPTH, D, D])
    g_ffn2 = din("g_ffn2", [DEPTH, D]); w_gu2 = din("w_ffn2_gu", [DEPTH, D, 2 * DFF]); w_d2 = din("w_ffn2_d", [DEPTH, DFF, D])
    g_final = din("g_final", [D])

    yp = dout("yp", [SEQ, D]); ys = dout("ys", [NTS, D])
    kp = dout("kp", [DEPTH, SEQ, D]); vp = dout("vp", [DEPTH, SEQ, D]); sp_o = dout("sp", [DEPTH, NH, 128, 128])
    ks = dout("ks", [DEPTH, NTS, D]); vs = dout("vs", [DEPTH, NTS, D]); ss_o = dout("ss", [DEPTH, NS, NH, 128, 128])

    if cfg.DEBUG:
        dbg_att = dout("dbg_att", [128, NH, TT]); dbg_h = dout("dbg_h", [128, NH, TT])
        dbg_mod = dout("dbg_mod", [128, DEPTH * 9 * KC * NSQ]); dbg_x = dout("dbg_x", [128, KC * TT]); dbg_h1 = dout("dbg_h1", [128, KC * TT])
        dbg_hid = dout("dbg_hid", [128, FC * TT])
        k.newsem("dbg")
    k.wscr = nc.dram_tensor("wscr", [DEPTH * NWG, 128, WSLOT], BF16, kind="Internal").ap()
    kscr = nc.dram_tensor("kscr", [DEPTH, NPT, 128, NH * TT], BF16, kind="Internal").ap()
    vscr = nc.dram_tensor("vscr", [DEPTH, NPT, 128, NH * TT], BF16, kind="Internal").ap()

    k.es_cur = es
    sb = k.sb
    x_fm = sb("x_fm", [128, KC, TT], F32)
    h_bf = sb("h_bf", [128, KC, TT], BF16)
    o_att = sb("o_att", [128, NH, TT], BF16)
    o_h = sb("o_h", [128, NH, TT], BF16)
    merged = sb("merged", [128, KC, TT], BF16)
    rs = sb("rs", [128, TT], F32)
    S_all = sb("S_all", [128, DEPTH, NH, 128], F32)
    Sb = sb("Sb", [128, NH, 128], BF16)
    k.wbuf = sb("wbuf", [128, NBUF, WSLOT], BF16)
    identF = sb("identF", [128, 128], F32)
    identB = sb("identB", [128, 128], BF16)
    onesB = sb("onesB", [128, 128], BF16)
    triP = sb("triP", [128, 128], F32)
    triS = sb("triS", [128, 128], F32)
    M0p = sb("M0p", [128, TT], F32)
    M0s = sb("M0s", [128, NTS], F32)
    MOD = sb("MOD", [128, DEPTH, 9, KC, NSQ], F32)
    gfin = sb("gfin", [128, KC], F32)
    gatt = sb("gatt", [128, DEPTH], F32)
    ghg = sb("ghg", [128, DEPTH], F32)
    lamt = sb("lamt", [128, DEPTH], F32)
    lbt = sb("lbt", [128, DEPTH, NH], F32)
    omlt = sb("omlt", [128, DEPTH, NH], F32)
    nomlt = sb("nomlt", [128, DEPTH, NH], F32)
    k.scrA = sb("scrA", [128, 4], F32); k.scrV = sb("scrV", [128, 4], F32); k.scrP = sb("scrP", [128, 4], F32)

    pbs = [es.enter_context(nc.psum_tensor("pb%d" % i, [128, 2, 512], F32)) for i in range(4)]

    class Bank:
        pass
    k.banks = []
    for i in range(8):
        b = Bank()
        b.ap = pbs[i // 2][:, i % 2, :]
        b.pair = pbs[i // 2]
        b.ready = None
        b.free = None
        k.banks.append(b)
    k.bank_rr = 0

    for i in range(NBUF):
        k.newsem("wld%d" % i)
    for s in ("wad0", "wad1", "cv0", "ld", "sld", "st0", "st1", "st2", "xld", "kvst", "kv0", "kv1", "kv2", "kv3", "out"):
        k.newsem(s)

    tick, dtick, wait = k.tick, k.dtick, k.wait
    PE, POOL = nc.tensor, nc.gpsimd
    ACT = EngProxy(k, "act", nc.scalar)
    DVE = EngProxy(k, "dve", nc.vector)

    groups = _wgroups()
    k.conv_tok = {}
    for l in range(DEPTH):
        k.newsem("cvl%d" % l)
        gu = {1: w_gu1[l].rearrange("(kc p) c -> p kc c", p=128), 2: w_gu2[l].rearrange("(kc p) c -> p kc c", p=128)}
        dd = {1: w_d1[l].rearrange("(kc p) c -> p kc c", p=128), 2: w_d2[l].rearrange("(kc p) c -> p kc c", p=128)}
        wi = w_in[l].rearrange("(kc p) c -> p kc c", p=128)
        wba = w_ba[l].rearrange("(kc p) c -> p kc c", p=128)
        wbh = w_bh[l].rearrange("(kc p) c -> p kc c", p=128)
        wo = w_o[l].rearrange("(kc p) c -> p kc c", p=128)
        for gi, (kind, i, nk, ncol) in enumerate(groups):
            dst = k.wscr[l * NWG + gi, :, 0:nk * ncol].rearrange("p (k c) -> p k c", k=nk)
            parts = []
            if kind in ("gu1", "gu2"):
                W = gu[1 if kind == "gu1" else 2]
                parts = [(0, 256, W[:, :, 256 * i:256 * i + 256]), (256, 512, W[:, :, DFF + 256 * i:DFF + 256 * i + 256])]
            elif kind in ("d1", "d2"):
                W = dd[1 if kind == "d1" else 2]
                parts = [(0, 128, W[:, :, 128 * i:128 * i + 128])]
            elif kind == "qk":
                parts = [(0, 512, wi[:, :, 512 * i:512 * i + 512])]
            elif kind == "v":
                parts = [(0, 512, wi[:, :, 2048 + 512 * i:2048 + 512 * i + 512])]
            elif kind == "hg":
                parts = [(0, 512, wi[:, :, 3072 + 512 * i:3072 + 512 * i + 512])]
            elif kind == "mrg":
                parts = [(0, 128, wi[:, :, 7168 + 128 * i:7168 + 128 * i + 128]),
                         (128, 256, wi[:, :, 8192 + 128 * i:8192 + 128 * i + 128]),
                         (256, 384, wba[:, :, 128 * i:128 * i + 128]),
                         (384, 512, wbh[:, :, 128 * i:128 * i + 128])]
            elif kind == "wo":
                parts = [(0, 512, wo[:, :, 512 * i:512 * i + 512])]
            for (c0, c1, src) in parts:
                ins = POOL.dma_start(out=dst[:, :, c0:c1], in_=src)
                k.conv_tok[l] = dtick("cvl%d" % l, ins)

    tiles = []
    for t in range(NPT):
        T = Tile(); T.kind = "p"; T.t = t; T.NT = TT; T.nblk = TT // 128; T.bp = 128
        T.segs = [(0, TT, 0)]
        tiles.append(T)
    T = Tile(); T.kind = "s"; T.t = 0; T.NT = NTS; T.nblk = NS; T.bp = TS
    T.segs = [(i * TS, (i + 1) * TS, 1 + i) for i in range(NS)]
    tiles.append(T)
    k.wseq = []
    for ti, T in enumerate(tiles):
        for l in range(DEPTH):
            for gi, (kind, i, nk, ncol) in enumerate(groups):
                k.wseq.append((kind, l, gi, nk, ncol, ti == 0 and gi == 0))
    k.w_used = 0; k.w_issued = 0; k.wld_tok = {}; k.wfree_tok = {}

    POOL.memset(identF[:], 1.0)
    POOL.affine_select(out=identF[:], in_=identF[:], pattern=[[-1, 128]], compare_op=ALU.is_equal, fill=0.0, base=0, channel_multiplier=1)
    POOL.tensor_copy(out=identB[:], in_=identF[:])
    POOL.memset(onesB[:], 1.0)
    POOL.memset(triP[:], 1.0)
    POOL.affine_select(out=triP[:], in_=triP[:], pattern=[[1, 128]], compare_op=ALU.is_ge, fill=0.0, base=0, channel_multiplier=-1)
    POOL.tensor_copy(out=triS[:], in_=triP[:])
    POOL.memset(triP[0:64, 64:128], 0.0)
    POOL.memset(triS[0:32, 32:64], 0.0)
    POOL.memset(M0p[:], 1.0)
    for c in range(TT // 64):
        POOL.memset(M0p[:, 64 * c:64 * c + 1], 0.0)
    POOL.memset(M0s[:], 1.0)
    for c in range(NS):
        POOL.memset(M0s[:, TS * c:TS * c + 1], 0.0)
    POOL.memset(k.scrP[:], 0.0)
    c_tok = tick("pool", POOL.memset(k.scrP[:, 0:1], 0.0))
    DVE.memset(k.scrV[:], 0.0)
    wait("act", c_tok)
    ACT.copy(out=k.scrA[:], in_=identF[:, 0:4])

    with contextlib.ExitStack() as pes:
        k.es_cur = pes
        cT = sb("cT", [128, KC, NSQ], F32)
        cact = sb("cact", [128, KC, NSQ], F32)
        badaT = sb("badaT", [128, DEPTH, 72], F32)
        gT = sb("gT", [128, 3, DEPTH, KC], F32)
        lbl = sb("lbl", [128, DEPTH, NH], F32)
        lam_in = sb("lam_in", [128, DEPTH, 4, 64], F32)
        lam_w = sb("lam_w", [128, DEPTH, 2, 64], F32)
        lam_s = sb("lam_s", [128, DEPTH, 2], F32)
        wad = sb("wad", [128, 2, KC, 1024], F32)
        ld = []
        for s_ in range(NSQ):
            ld.append(dtick("ld", nc.sync.dma_start(out=cT[:, :, s_], in_=cc[s_].rearrange("(c p) -> p c", p=128), allow_slow_non_contiguous=True)))
        for l in range(DEPTH):
            ld.append(dtick("ld", nc.sync.dma_start(out=badaT[:, l, :], in_=b_ada[l].rearrange("(j p) -> p j", p=128), allow_slow_non_contiguous=True)))
            for gi, gsrc in enumerate((g_ffn1, g_mix, g_ffn2)):
                ld.append(dtick("ld", nc.sync.dma_start(out=gT[:, gi, l, :], in_=gsrc[l].rearrange("(c p) -> p c", p=128), allow_slow_non_contiguous=True)))
            ld.append(dtick("ld", nc.sync.dma_start(out=lbl[:, l, :], in_=hg_lb[l].rearrange("(h p) -> p h", p=128), allow_slow_non_contiguous=True)))
        ld.append(dtick("ld", nc.sync.dma_start(out=gfin[:], in_=g_final.rearrange("(c p) -> p c", p=128), allow_slow_non_contiguous=True)))
        ld.append(dtick("ld", nc.sync.dma_start(out=gatt[:], in_=g_att_sub.rearrange("l p -> p l"), allow_slow_non_contiguous=True)))
        ld.append(dtick("ld", nc.sync.dma_start(out=ghg[:], in_=g_hg_norm.rearrange("l p -> p l"), allow_slow_non_contiguous=True)))
        ld.append(dtick("ld", nc.sync.dma_start(out=lam_in[:].rearrange("p l a b -> p (l a b)"),
                                                 in_=att_lambda.rearrange("l a b -> (l a b)").partition_broadcast(128))))
        ldall = ld[-1]
        wait("dve", ldall); wait("act", ldall)
        ct = tick("act", ACT.activation(out=cact[:], in_=cT[:], func=AF.Silu))
        for l in range(DEPTH):
            DVE.tensor_tensor(out=lam_w[:, l, 0, :], in0=lam_in[:, l, 0, :], in1=lam_in[:, l, 1, :], op=ALU.mult)
            DVE.tensor_tensor(out=lam_w[:, l, 1, :], in0=lam_in[:, l, 2, :], in1=lam_in[:, l, 3, :], op=ALU.mult)
        dt_ = tick("dve", DVE.reduce_sum(out=lam_s[:].rearrange("p l a -> p (l a)"), in_=lam_w[:].rearrange("p l a b -> p (l a) b"), axis=mybir.AxisListType.X))
        wait("act", dt_)
        at_ = tick("act", ACT.activation(out=lam_s[:], in_=lam_s[:], func=AF.Exp))
        wait("dve", at_)
        for l in range(DEPTH):
            lam_init = 0.8 - 0.6 * math.exp(-0.3 * l)
            DVE.tensor_tensor(out=lamt[:, l:l + 1], in0=lam_s[:, l, 1:2], in1=lam_s[:, l, 0:1], op=ALU.subtract)
            DVE.tensor_scalar(out=lamt[:, l:l + 1], in0=lamt[:, l:l + 1], scalar1=-lam_init, scalar2=None, op0=ALU.add)
            DVE.tensor_scalar(out=gatt[:, l:l + 1], in0=gatt[:, l:l + 1], scalar1=(1.0 - lam_init) * math.sqrt(128.0), scalar2=None, op0=ALU.mult)
        DVE.tensor_scalar(out=ghg[:], in0=ghg[:], scalar1=math.sqrt(128.0), scalar2=None, op0=ALU.mult)
        DVE.tensor_scalar(out=gfin[:], in0=gfin[:], scalar1=32.0, scalar2=None, op0=ALU.mult)
        at2 = tick("act", ACT.activation(out=lbl[:], in_=lbl[:], func=AF.Exp))
        wait("dve", at2)
        lsum = lam_w[:, 0, 0, 0:NH]
        DVE.tensor_copy(out=lsum, in_=lbl[:, 0, :])
        for l in range(1, DEPTH):
            DVE.tensor_tensor(out=lsum, in0=lsum, in1=lbl[:, l, :], op=ALU.add)
        DVE.reciprocal(out=lsum, in_=lsum)
        DVE.memset(lbt[:, 0, :], 0.0)
        for l in range(1, DEPTH):
            DVE.tensor_tensor(out=lbl[:, l, :], in0=lbl[:, l, :], in1=lsum, op=ALU.mult)
            DVE.tensor_tensor(out=lbt[:, l, :], in0=lbt[:, l - 1, :], in1=lbl[:, l, :], op=ALU.add)
        DVE.tensor_scalar(out=omlt[:], in0=lbt[:], scalar1=-1.0, scalar2=1.0, op0=ALU.mult, op1=ALU.add)
        DVE.tensor_scalar(out=nomlt[:], in0=omlt[:], scalar1=-1.0, scalar2=None, op0=ALU.mult)
        wait("pe", ct)
        wtok = [None, None]
        wfree = [None, None]
        gidx = 0
        for l in range(DEPTH):
            wv = w_ada[l].rearrange("(kc p) c -> p kc c", p=128)
            bk = k.bank_next()
            wait("pe", bk.free)
            outv = bk.ap[:, 0:72 * NSQ].rearrange("p (j s) -> p j s", s=NSQ)
            for g9 in range(9):
                slot = gidx % 2
                wait("sp", wfree[slot])
                wtok[slot] = dtick("wad%d" % slot, nc.sync.dma_start(out=wad[:, slot], in_=wv[:, :, 1024 * g9:1024 * g9 + 1024]))
                wait("pe", wtok[slot])
                for f in range(8):
                    for kc in range(KC):
                        ins = PE.matmul(outv[:, g9 * 8 + f, :], lhsT=wad[:, slot, kc, 128 * f:128 * f + 128], rhs=cact[:, kc, :],
                                        start=(kc == 0), stop=(kc == KC - 1))
                wfree[slot] = tick("pe", ins)
                gidx += 1
            bk.ready = wfree[(gidx - 1) % 2]
            wait("dve", bk.ready)
            ins = DVE.tensor_tensor(out=MOD[:, l].rearrange("p j c s -> p (j c) s"), in0=outv,
                                    in1=badaT[:, l, :].unsqueeze(2).to_broadcast([128, 72, NSQ]), op=ALU.add)
            bk.free = tick("dve", ins)
            for (j_sc, gi_, j_g, half) in ((1, 0, 2, 0.5), (4, 1, 5, 1.0), (7, 2, 8, 0.5)):
                DVE.tensor_scalar(out=MOD[:, l, j_sc], in0=MOD[:, l, j_sc], scalar1=1.0, scalar2=32.0, op0=ALU.add, op1=ALU.mult)
                DVE.tensor_tensor(out=MOD[:, l, j_sc], in0=MOD[:, l, j_sc],
                                  in1=gT[:, gi_, l, :].unsqueeze(2).to_broadcast([128, KC, NSQ]), op=ALU.mult)
                if half != 1.0:
                    DVE.tensor_scalar(out=MOD[:, l, j_g], in0=MOD[:, l, j_g], scalar1=half, scalar2=None, op0=ALU.mult)
        if cfg.DEBUG:
            wait("pool", ("dve", k.cnt["dve"]))
            dmt = dtick("dbg", POOL.dma_start(out=dbg_mod, in_=MOD[:].rearrange("p l j c s -> p (l j c s)")))
            wait("pool", dmt)
        k.barrier()
    k.es_cur = es


    evac_rr = [0]

    def evac_copy(out, in_, bank, scale=None):
        evac_rr[0] += 1
        if scale is not None or evac_rr[0] % 2 == 0:
            wait("act", bank.ready)
            if scale is not None:
                t = tick("act", ACT.mul(out=out, in_=in_, mul=scale))
            else:
                t = tick("act", ACT.copy(out=out, in_=in_))
        else:
            wait("dve", bank.ready)
            t = tick("dve", DVE.tensor_copy(out=out, in_=in_))
        bank.free = t
        return t

    def rstd_act(bank, out_ap, in_ap, eps_total):
        wait("act", bank.ready)
        ACT.activation(out=out_ap, in_=in_ap, func=AF.Ln, bias=eps_total, scale=1.0)
        t = tick("act", ACT.activation(out=out_ap, in_=out_ap, func=AF.Exp, scale=-0.5))
        bank.free = t
        return t

    def norm_phase(T, l, ja, jb, x_tmp):
        NT = T.NT
        sqt = None
        for c in range(KC):
            sqt = tick("act", ACT.activation(out=h_bf[:, c, 0:NT], in_=x_fm[:, c, 0:NT], func=AF.Square))
        bk = k.bank_next()
        wait("pe", bk.free); wait("pe", sqt)
        for c in range(KC):
            ins = PE.matmul(bk.ap[:, 0:NT], lhsT=onesB[:], rhs=h_bf[:, c, 0:NT], start=(c == 0), stop=(c == KC - 1))
        bk.ready = tick("pe", ins)
        wait("dve", rstd_act(bk, rs[:, 0:NT], bk.ap[:, 0:NT], 1024.0 * EPS))
        ht = None
        for c in range(KC):
            for (c0, c1, sq) in T.segs:
                a_ap = gfin[:, c:c + 1] if ja is None else MOD[:, l, ja, c, sq:sq + 1]
                ht = tick("dve", DVE.scalar_tensor_tensor(out=x_tmp[:, c, c0:c1], in0=x_fm[:, c, c0:c1], scalar=a_ap, in1=rs[:, c0:c1], op0=ALU.mult, op1=ALU.mult))
                if jb is not None:
                    ht = tick("dve", DVE.tensor_scalar(out=h_bf[:, c, c0:c1], in0=x_tmp[:, c, c0:c1], scalar1=MOD[:, l, jb, c, sq:sq + 1], scalar2=None, op0=ALU.add))
        return ht

    def norm_mod(T, l, ja, jb):
        with contextlib.ExitStack() as pes:
            k.es_cur = pes
            NT = T.NT
            x_tmp = sb("x_tmp", [128, KC, NT], F32)
            ht = norm_phase(T, l, ja, jb, x_tmp)
            if cfg.DEBUG and T.kind == "p" and T.t == 0 and l == 0 and ja == 1:
                wait("pool", ht)
                wait("pool", dtick("dbg", POOL.dma_start(out=dbg_h1, in_=h_bf[:].rearrange("p c t -> p (c t)"))))
            k.barrier()
        k.es_cur = es
        return ht

    def ffn_phase(T, l, which, h_tok):
        NT = T.NT
        jg = 2 if which == 1 else 8
        with contextlib.ExitStack() as pes:
            k.es_cur = pes
            hid = sb("hid", [128, FC, NT], BF16)
            stmp = sb("stmp", [128, 2, NT], F32)
            st_free = [None, None]
            hid_tok = None
            u = 0
            for g in range(11):
                w = k.wnext("gu%d" % which)
                for j in range(2):
                    jj = 2 * g + j
                    bA = k.bank_next(); bB = k.bank_next()
                    wait("pe", bA.free); wait("pe", bB.free); wait("pe", h_tok)
                    for kc in range(KC):
                        PE.matmul(bA.ap[:, 0:NT], lhsT=w[:, kc, 128 * j:128 * j + 128], rhs=h_bf[:, kc, 0:NT], start=(kc == 0), stop=(kc == KC - 1))
                    for kc in range(KC):
                        ins = PE.matmul(bB.ap[:, 0:NT], lhsT=w[:, kc, 256 + 128 * j:256 + 128 * j + 128], rhs=h_bf[:, kc, 0:NT], start=(kc == 0), stop=(kc == KC - 1))
                    if j == 1:
                        bA.ready = bB.ready = k.wdone(ins)
                    else:
                        bA.ready = bB.ready = tick("pe", ins)
                    s = u % 2
                    wait("act", bA.ready); wait("act", st_free[s])
                    a_t = tick("act", ACT.activation(out=stmp[:, s, 0:NT], in_=bA.ap[:, 0:NT], func=AF.Silu))
                    bA.free = a_t
                    wait("dve", a_t); wait("dve", bB.ready)
                    d_t = tick("dve", DVE.tensor_tensor(out=hid[:, jj, 0:NT], in0=stmp[:, s, 0:NT], in1=bB.ap[:, 0:NT], op=ALU.mult))
                    bB.free = d_t; st_free[s] = d_t; hid_tok = d_t
                    u += 1
            if cfg.DEBUG and T.kind == "p" and T.t == 0 and l == 0 and which == 1:
                wait("pool", hid_tok)
                wait("pool", dtick("dbg", POOL.dma_start(out=dbg_hid, in_=hid[:].rearrange("p c t -> p (c t)"))))
            for m in range(8):
                w = k.wnext("d%d" % which)
                bk = k.bank_next()
                wait("pe", bk.free); wait("pe", hid_tok)
                for kc in range(FC):
                    ins = PE.matmul(bk.ap[:, 0:NT], lhsT=w[:, kc, :], rhs=hid[:, kc, 0:NT], start=(kc == 0), stop=(kc == FC - 1))
                bk.ready = k.wdone(ins)
                wait("dve", bk.ready)
                for (c0, c1, sq) in T.segs:
                    ins = DVE.scalar_tensor_tensor(out=x_fm[:, m, c0:c1], in0=bk.ap[:, c0:c1], scalar=MOD[:, l, jg, m, sq:sq + 1],
                                                   in1=x_fm[:, m, c0:c1], op0=ALU.mult, op1=ALU.add)
                bk.free = tick("dve", ins)
            k.barrier()
        k.es_cur = es

    def load_x(T):
        NT, nblk, bp = T.NT, T.nblk, T.bp
        with contextlib.ExitStack() as pes:
            k.es_cur = pes
            xin = sb("xin", [128, nblk, D], F32)
            if T.kind == "p":
                src = xp[T.t * TT:(T.t + 1) * TT, :].rearrange("(b p) f -> p b f", p=128)
            else:
                src = xs.rearrange("(b p) f -> p b f", p=bp)
            tok = dtick("xld", POOL.dma_start(out=xin[0:bp], in_=src))
            wait("pe", tok)
            for c in range(KC):
                bk = k.bank_next()
                wait("pe", bk.free)
                for b in range(nblk):
                    ins = PE.transpose(out=bk.ap[:, b * bp:(b + 1) * bp], in_=xin[0:bp, b, 128 * c:128 * c + 128], identity=identF[0:bp, 0:bp])
                bk.ready = tick("pe", ins)
                evac_copy(x_fm[:, c, 0:NT], bk.ap[:, 0:NT], bk)
            if cfg.DEBUG and T.kind == "p" and T.t == 0:
                wait("pool", ("dve", k.cnt["dve"])); wait("pool", ("act", k.cnt["act"]))
                wait("pool", dtick("dbg", POOL.dma_start(out=dbg_x, in_=x_fm[:].rearrange("p c t -> p (c t)"))))
            k.barrier()
        k.es_cur = es

    kvst_tok = {}

    def mixing(T, l, h_tok):
        NT, nblk, bp = T.NT, T.nblk, T.bp
        isp = T.kind == "p"
        kout = kp if isp else ks
        vout = vp if isp else vs
        r0 = T.t * TT if isp else 0
        with contextlib.ExitStack() as pes:
            k.es_cur = pes
            qT = sb("qT", [128, NH, NT], BF16)
            kT = sb("kT", [128, NH, NT], BF16)
            v_tm = sb("v_tm", [128, NH, nblk, 128], BF16)
            stage = sb("stage", [128, 3, 512], F32)
            Pb = sb("Pb", [128, 3, 2, NT], BF16)
            fin = sb("fin", [128, 5, NT], F32)
            finc = sb("finc", [128, 4, NT], F32)
            sqb = sb("sqb", [128, NT], BF16)
            rsa = sb("rsa", [128, NT], F32)
            st_tok = [None, None, None]
            st_n = [0]
            last_store = []

            def store_rows(bank, dst_ap, also=None):
                s = st_n[0] % 3
                st_n[0] += 1
                wait("act", bank.ready); wait("act", st_tok[s])
                t1 = tick("act", ACT.copy(out=stage[0:bp, s, :], in_=bank.ap[0:bp, 0:512]))
                if also is not None:
                    wait("dve", bank.ready)
                    wait("dve", t1)
                    bank.free = tick("dve", DVE.tensor_copy(out=also, in_=bank.ap[0:bp, 0:512].rearrange("p (h e) -> p h e", e=128)))
                    ret = bank.free
                else:
                    bank.free = t1
                    ret = None
                wait("pool", t1)
                st_tok[s] = dtick("st%d" % s, POOL.dma_start(out=dst_ap, in_=stage[0:bp, s, :]))
                last_store.append(st_tok[s])
                return ret

            kt_tok = None
            vt_tok = None
            q_tok = None
            for g in range(4):
                w = k.wnext("qk")
                ins = None
                for j in range(4):
                    bk = k.bank_next()
                    wait("pe", bk.free); wait("pe", h_tok)
                    for kc in range(KC):
                        ins = PE.matmul(bk.ap[:, 0:NT], lhsT=w[:, kc, 128 * j:128 * j + 128], rhs=h_bf[:, kc, 0:NT], start=(kc == 0), stop=(kc == KC - 1))
                    bk.ready = tick("pe", ins)
                    if g < 2:
                        q_tok = evac_copy(qT[:, 4 * g + j, 0:NT], bk.ap[:, 0:NT], bk, scale=0.125)
                    else:
                        wait("dve", bk.ready)
                        kt_tok = bk.free = tick("dve", DVE.tensor_copy(out=kT[:, 4 * (g - 2) + j, 0:NT], in_=bk.ap[:, 0:NT]))
                if g >= 2:
                    for b in range(nblk):
                        bk = k.bank_next()
                        wait("pe", bk.free)
                        for kc in range(KC):
                            ins = PE.matmul(bk.ap[0:bp, 0:512], lhsT=h_bf[:, kc, b * bp:(b + 1) * bp], rhs=w[:, kc, :], start=(kc == 0), stop=(kc == KC - 1))
                        bk.ready = tick("pe", ins)
                        store_rows(bk, kout[l, r0 + b * bp:r0 + (b + 1) * bp, 512 * (g - 2):512 * (g - 2) + 512])
                k.wfree_tok[k.w_cur] = bk.ready
            for g in range(2):
                w = k.wnext("v")
                for b in range(nblk):
                    bk = k.bank_next()
                    wait("pe", bk.free); wait("pe", h_tok)
                    for kc in range(KC):
                        ins = PE.matmul(bk.ap[0:bp, 0:512], lhsT=h_bf[:, kc, b * bp:(b + 1) * bp], rhs=w[:, kc, :], start=(kc == 0), stop=(kc == KC - 1))
                    bk.ready = tick("pe", ins)
                    vt_tok = store_rows(bk, vout[l, r0 + b * bp:r0 + (b + 1) * bp, 512 * g:512 * g + 512], also=v_tm[0:bp, 4 * g:4 * g + 4, b, :])
                k.wfree_tok[k.w_cur] = bk.ready
            if isp and T.t < NPT - 1:
                wait("pool", kt_tok); wait("pool", vt_tok)
                dtick("kvst", POOL.dma_start(out=kscr[l, T.t].rearrange("p (h t) -> p h t", h=NH), in_=kT[:, :, :]))
                kvst_tok[(l, T.t)] = dtick("kvst", POOL.dma_start(out=vscr[l, T.t], in_=v_tm[:].rearrange("p h b e -> p (h b e)")))
                last_store.append(kvst_tok[(l, T.t)])

            O = [k.banks[0], k.banks[1]]
            L = [k.banks[2], k.banks[3]]
            Sp = [(k.banks[4], k.banks[5]), (k.banks[6], k.banks[7])]
            if isp:
                kring = sb("kring", [128, 4, TT], BF16)
                vring = sb("vring", [128, 4, TT], BF16)
            ring_free = [None] * 4
            if not isp:
                NBK = PAST // 128
                kc_in = sb("kc_in", [128, 2, NBK, 128], F32)
                vc_in = sb("vc_in", [128, 2, NBK, 128], F32)
                kcT = sb("kcT", [128, 2, PAST], BF16)
                vcb = sb("vcb", [128, 2, NBK, 128], BF16)
            units = []
            if isp:
                for h in range(NH):
                    units.append(dict(h=h, q0=0, N=NT, seq=None))
            else:
                for i in range(NS):
                    for h in range(NH):
                        units.append(dict(h=h, q0=i * TS, N=TS, seq=i))
            loads = []
            if isp:
                for ui, u in enumerate(units):
                    for s in range(T.t):
                        loads.append((ui, s))
            load_tok = {}
            issued = [0]

            def issue_upto(n):
                while issued[0] < min(n, len(loads)):
                    i = issued[0]
                    ui, s = loads[i]
                    slot = i % 4
                    wait("pool", ring_free[slot]); wait("pool", kvst_tok[(l, s)])
                    h = units[ui]["h"]
                    dtick("kv%d" % slot, POOL.dma_start(out=kring[:, slot, :], in_=kscr[l, s][:, h * TT:(h + 1) * TT]))
                    load_tok[i] = dtick("kv%d" % slot, POOL.dma_start(out=vring[:, slot, :], in_=vscr[l, s][:, h * TT:(h + 1) * TT]))
                    issued[0] += 1

            cin_tok = {}
            cin_free = [None, None]
            cprep_tok = {}

            def cache_load(ui):
                u = units[ui]
                s = ui % 2
                wait("pool", cin_free[s])
                h = u["h"]; i = u["seq"]
                dtick("kv%d" % s, POOL.dma_start(out=kc_in[:, s], in_=ck[l, i, :, 128 * h:128 * h + 128].rearrange("(b p) d -> p b d", p=128)))
                cin_tok[ui] = dtick("kv%d" % s, POOL.dma_start(out=vc_in[:, s], in_=cv[l, i, :, 128 * h:128 * h + 128].rearrange("(b p) d -> p b d", p=128)))

            cprep_free = [None, None]

            def cache_prep(ui):
                s = ui % 2
                wait("pe", cin_tok[ui]); wait("dve", cin_tok[ui]); wait("act", cin_tok[ui])
                wait("dve", cprep_free[s]); wait("act", cprep_free[s])
                t = None
                for b4 in range(NBK // 4):
                    bk = k.bank_next()
                    wait("pe", bk.free)
                    for j in range(4):
                        ins = PE.transpose(out=bk.ap[:, 128 * j:128 * j + 128], in_=kc_in[:, s, 4 * b4 + j, :], identity=identF[:])
                    bk.ready = tick("pe", ins)
                    t = evac_copy(kcT[:, s, 512 * b4:512 * b4 + 512], bk.ap[:, :], bk)
                wait("dve", t)
                t2 = tick("dve", DVE.tensor_copy(out=vcb[:, s], in_=vc_in[:, s]))
                wait("act", t2)
                t3 = tick("act", ACT.copy(out=k.scrA[0:1, 2:3], in_=k.scrA[0:1, 1:2]))
                cin_free[s] = t3
                cprep_tok[ui] = t3

            pending = []
            pn = [0]
            sn = [0]
            p_free = [None, None, None]
            wait("pe", q_tok); wait("pe", kt_tok); wait("pe", vt_tok)
            if not isp:
                cache_load(0)
            for ui, u in enumerate(units):
                h, q0, N = u["h"], u["q0"], u["N"]
                blocks = []
                if isp:
                    base = sum(1 for (a, _) in loads if a < ui)
                    for s in range(T.t):
                        li = base + s
                        for b in range(4):
                            blocks.append(dict(kt=kring[:, li % 4, 128 * b:128 * b + 128], v=vring[:, li % 4, 128 * b:128 * b + 128], nk=128, qs=0, zero=False,
                                               li=li, last=(b == 3)))
                    for b in range(nblk):
                        blocks.append(dict(kt=kT[:, h, 128 * b:128 * b + 128], v=v_tm[:, h, b, :], nk=128, qs=128 * b, zero=True, li=None, last=False))
                else:
                    s2 = ui % 2
                    if ui + 1 < len(units):
                        cache_load(ui + 1)
                    cache_prep(ui)
                    wait("pe", cprep_tok[ui])
                    for b in range(NBK):
                        blocks.append(dict(kt=kcT[:, s2, 128 * b:128 * b + 128], v=vcb[:, s2, b, :], nk=128, qs=0, zero=False, li=None, last=False))
                    i = u["seq"]
                    blocks.append(dict(kt=kT[:, h, i * TS:(i + 1) * TS], v=v_tm[0:TS, h, i, :], nk=TS, qs=0, zero=False, li=None, last=False))
                nb = len(blocks)

                def emit_S(j):
                    bl = blocks[j]
                    pr = Sp[sn[0] % 2]
                    bl["pr"] = pr
                    sn[0] += 1
                    if bl["li"] is not None:
                        issue_upto(bl["li"] + 3)
                        wait("pe", load_tok[bl["li"]])
                    wait("pe", pr[0].free); wait("pe", pr[1].free)
                    nk, qs = bl["nk"], bl["qs"]
                    PE.matmul(pr[0].ap[0:nk, qs:N], lhsT=bl["kt"][0:64, :], rhs=qT[0:64, h, q0 + qs:q0 + N], start=True, stop=True)
                    ins = PE.matmul(pr[1].ap[0:nk, qs:N], lhsT=bl["kt"][64:128, :], rhs=qT[64:128, h, q0 + qs:q0 + N], start=True, stop=True)
                    pr[0].ready = pr[1].ready = tick("pe", ins)

                wait("pe", O[0].free); wait("pe", O[1].free); wait("pe", L[0].free); wait("pe", L[1].free)
                emit_S(0)
                for j in range(nb):
                    bl = blocks[j]
                    nk, qs = bl["nk"], bl["qs"]
                    pr = bl["pr"]
                    ps = pn[0] % 3
                    pn[0] += 1
                    wait("act", pr[0].ready); wait("act", p_free[ps])
                    e_t = tick("act", ACT.activation(out=Pb[0:nk, ps, :, qs:N], in_=pr[0].pair[0:nk, :, qs:N], func=AF.Exp))
                    if bl["zero"]:
                        e_t = tick("act", ACT.mul(out=Pb[64:128, ps, :, qs:qs + 64], in_=Pb[64:128, ps, :, qs:qs + 64], mul=0.0))
                    pr[0].free = pr[1].free = e_t
                    if j + 1 < nb:
                        emit_S(j + 1)
                    wait("pe", e_t)
                    for m in range(2):
                        PE.matmul(O[m].ap[:, qs:N], lhsT=bl["v"], rhs=Pb[0:nk, ps, m, qs:N], start=(j == 0), stop=(j == nb - 1))
                    for m in range(2):
                        ins = PE.matmul(L[m].ap[:, qs:N], lhsT=onesB[0:nk, :], rhs=Pb[0:nk, ps, m, qs:N], start=(j == 0), stop=(j == nb - 1))
                    pv_t = tick("pe", ins)
                    p_free[ps] = pv_t
                    if bl["last"]:
                        ring_free[bl["li"] % 4] = pv_t
                    if j == 1 and pending:
                        pending.pop(0)()
                while pending:
                    pending.pop(0)()
                wait("act", pv_t); wait("dve", pv_t)
                ACT.activation(out=finc[:, 2, 0:N], in_=L[0].ap[:, 0:N], func=AF.Ln)
                a_t = tick("act", ACT.activation(out=finc[:, 3, 0:N], in_=L[1].ap[:, 0:N], func=AF.Ln))
                a2_t = tick("act", ACT.activation(out=finc[:, 2:4, 0:N], in_=finc[:, 2:4, 0:N], func=AF.Exp, scale=-1.0))
                DVE.tensor_copy(out=finc[:, 0, 0:N], in_=O[0].ap[:, 0:N])
                d_t = tick("dve", DVE.tensor_copy(out=finc[:, 1, 0:N], in_=O[1].ap[:, 0:N]))
                L[0].free = L[1].free = a_t
                O[0].free = O[1].free = d_t
                wait("dve", a2_t)
                DVE.tensor_tensor(out=fin[:, 2, 0:N], in0=finc[:, 0, 0:N], in1=finc[:, 2, 0:N], op=ALU.mult)
                DVE.tensor_tensor(out=fin[:, 3, 0:N], in0=finc[:, 1, 0:N], in1=finc[:, 3, 0:N], op=ALU.mult)
                DVE.scalar_tensor_tensor(out=fin[:, 4, 0:N], in0=fin[:, 3, 0:N], scalar=lamt[:, l:l + 1], in1=fin[:, 2, 0:N], op0=ALU.mult, op1=ALU.add)
                sq_t = tick("dve", DVE.tensor_tensor(out=sqb[:, 0:N], in0=fin[:, 4, 0:N], in1=fin[:, 4, 0:N], op=ALU.mult))

                def fin2(h=h, q0=q0, N=N, sq_t=sq_t):
                    pr = Sp[sn[0] % 2]
                    sn[0] += 1
                    bk = pr[0]
                    wait("pe", pr[0].free); wait("pe", pr[1].free); wait("pe", sq_t)
                    ins = PE.matmul(bk.ap[:, 0:N], lhsT=onesB[:], rhs=sqb[:, 0:N], start=True, stop=True)
                    pr[0].ready = pr[1].ready = tick("pe", ins)
                    t = rstd_act(bk, rsa[:, 0:N], bk.ap[:, 0:N], 128.0 * EPS)
                    pr[0].free = pr[1].free = t
                    wait("dve", t)
                    return tick("dve", DVE.scalar_tensor_tensor(out=o_att[:, h, q0:q0 + N], in0=fin[:, 4, 0:N], scalar=gatt[:, l:l + 1], in1=rsa[:, 0:N], op0=ALU.mult, op1=ALU.mult))
                pending.append(fin2)
            oa_tok = None
            while pending:
                oa_tok = pending.pop(0)()
            if cfg.DEBUG and isp and T.t == 0 and l == 0:
                wait("pool", oa_tok)
                last_store.append(dtick("dbg", POOL.dma_start(out=dbg_att, in_=o_att[:])))
            k.barrier(pool_tokens=last_store)
        k.es_cur = es

        with contextlib.ExitStack() as pes:
            k.es_cur = pes
            if isp:
                Lc = 64; h2 = 32
                nch = NT // 64
                M0 = M0p
            else:
                Lc = TS; h2 = TS
                nch = NS
                M0 = M0s
            two = h2 < Lc
            hq = sb("hq", [128, NH, NT], F32)
            sigf = sb("sigf", [128, NH, NT], F32)
            hgt = sb("hgt", [128, NH, NT], BF16)
            hi_tm = sb("hi_tm", [64, NH, nch, 128], BF16)
            fw = sb("fw", [128, 10, NT], F32)
            qk4 = sb("qk4", [128, 2, 5, NT], BF16)
            kd_tm = sb("kd_tm", [64, 2, nch, 128], BF16)
            Am = sb("Am", [64, 2, nch, 64], BF16)
            sm = sb("sm", [128, 4, TT // 32], F32)
            osq = sb("osq", [128, NT], BF16)
            ot = sb("ot", [128, NT], F32)
            rsh = sb("rsh", [128, NT], F32)
            if not isp:
                Ssm = sb("Ssm", [128, NS, NH, 128], F32)
                Sbs = sb("Sbs", [128, NS, NH, 128], BF16)
            if two:
                DVE.memset(Am[h2:Lc, :, :, 0:h2], 0.0)
            hq_tok = sig_tok = hg_tok = hi_tok = None
            for i in range(8):
                w = k.wnext("hg")
                if i in (4, 5):
                    for b in range(nch):
                        bk = k.bank_next()
                        wait("pe", bk.free); wait("pe", h_tok)
                        for kc in range(KC):
                            ins = PE.matmul(bk.ap[0:Lc, 0:512], lhsT=h_bf[:, kc, b * Lc:(b + 1) * Lc], rhs=w[:, kc, :], start=(kc == 0), stop=(kc == KC - 1))
                        bk.ready = tick("pe", ins)
                        hi_tok = evac_copy(hi_tm[0:Lc, 4 * (i - 4):4 * (i - 4) + 4, b, :], bk.ap[0:Lc, 0:512].rearrange("p (h e) -> p h e", e=128), bk)
                else:
                    for j in range(4):
                        bk = k.bank_next()
                        wait("pe", bk.free); wait("pe", h_tok)
                        for kc in range(KC):
                            ins = PE.matmul(bk.ap[:, 0:NT], lhsT=w[:, kc, 128 * j:128 * j + 128], rhs=h_bf[:, kc, 0:NT], start=(kc == 0), stop=(kc == KC - 1))
                        bk.ready = tick("pe", ins)
                        wait("act", bk.ready)
                        if i < 2:
                            hq_tok = bk.free = tick("act", ACT.activation(out=hq[:, 4 * i + j, 0:NT], in_=bk.ap[:, 0:NT], func=AF.Silu))
                        elif i < 4:
                            sig_tok = bk.free = tick("act", ACT.activation(out=sigf[:, 4 * (i - 2) + j, 0:NT], in_=bk.ap[:, 0:NT], func=AF.Sigmoid))
                        else:
                            hg_tok = bk.free = tick("act", ACT.activation(out=hgt[:, 4 * (i - 6) + j, 0:NT], in_=bk.ap[:, 0:NT], func=AF.Silu))
                k.wfree_tok[k.w_cur] = bk.ready
            hi_toks = [("act", k.cnt["act"]), ("dve", k.cnt["dve"])]
            if isp:
                sb_tok = tick("dve", DVE.tensor_copy(out=Sb[:], in_=S_all[:, l]))
            else:
                s_ld = dtick("sld", POOL.dma_start(out=Ssm[:].rearrange("p i h v -> p (i h) v"), in_=st[l].rearrange("i h kk v -> kk (i h) v")))
                wait("dve", s_ld)
                sb_tok = tick("dve", DVE.tensor_copy(out=Sbs[:], in_=Ssm[:]))
            wait("dve", sig_tok); wait("dve", hq_tok)
            for t_ in hi_toks:
                wait("pe", t_)
            pe_use = [None, None]
            last_o = None
            act_rd = [None]
            dve_rd = [None]
            dve_use = [None, None]

            def c3(ap):
                return ap.rearrange("p (c t) -> p c t", t=Lc)
            def prep(h):
                    hs = h % 2
                    fA, fK, fB, fC, fD, fE, gA, gB, gC, gD = [fw[:, i, 0:NT] for i in range(10)]
                    qe, qa2, ka0, ka1, kd = [qk4[:, hs, i, 0:NT] for i in range(5)]
                    PL = POOL
                    wait("pool", sig_tok); wait("pool", hq_tok); wait("pool", act_rd[0])
                    PL.tensor_scalar(out=fA, in0=sigf[:, h, 0:NT], scalar1=omlt[:, l, h:h + 1], scalar2=lbt[:, l, h:h + 1], op0=ALU.mult, op1=ALU.add)
                    t1 = tick("pool", PL.tensor_scalar(out=fA, in0=fA, scalar1=TINY, scalar2=None, op0=ALU.max))
                    wait("act", t1)
                    t2 = tick("act", ACT.activation(out=fB, in_=fA, func=AF.Ln))
                    wait("pool", dve_rd[0])
                    PL.tensor_scalar(out=fK, in0=sigf[:, h, 0:NT], scalar1=nomlt[:, l, h:h + 1], scalar2=omlt[:, l, h:h + 1], op0=ALU.mult, op1=ALU.add)
                    wait("pool", t2)
                    wait("dve", t2)
                    DVE.tensor_tensor_scan(out=fC, data0=M0[:, 0:NT], data1=fB, initial=0.0, op0=ALU.mult, op1=ALU.add)
                    sc_t = tick("dve", k.last_ins["dve"])
                    wait("pool", sc_t)
                    b3 = c3(fC)
                    if two:
                        PL.tensor_tensor(out=c3(fD), in0=b3, in1=b3[:, :, h2 - 1:h2].to_broadcast([128, nch, Lc]), op=ALU.subtract)
                    PL.tensor_tensor(out=c3(fE), in0=b3[:, :, Lc - 1:Lc].to_broadcast([128, nch, Lc]), in1=b3, op=ALU.subtract)
                    t3 = tick("pool", PL.tensor_copy(out=sm[:, 0, 0:nch], in_=b3[:, :, Lc - 1]))
                    wait("act", t3)
                    ACT.activation(out=gC, in_=fC, func=AF.Exp)
                    ACT.activation(out=c3(gA)[:, :, 0:h2], in_=c3(fC)[:, :, 0:h2], func=AF.Exp, scale=-1.0)
                    if two:
                        ACT.activation(out=c3(gA)[:, :, h2:Lc], in_=c3(fD)[:, :, h2:Lc], func=AF.Exp)
                        ACT.activation(out=gB, in_=fD, func=AF.Exp, scale=-1.0)
                    ACT.activation(out=gD, in_=fE, func=AF.Exp)
                    t4 = tick("act", ACT.activation(out=sm[:, 1, 0:nch], in_=sm[:, 0, 0:nch], func=AF.Exp))
                    act_rd[0] = t4
                    wait("pool", t4); wait("pool", pe_use[hs]); wait("pool", dve_use[hs])
                    PL.tensor_tensor(out=qe, in0=hq[:, h, 0:NT], in1=gC, op=ALU.mult)
                    PL.tensor_tensor(out=c3(ka0)[:, :, 0:h2], in0=c3(fK)[:, :, 0:h2], in1=c3(gA)[:, :, 0:h2], op=ALU.mult)
                    if two:
                        PL.tensor_tensor(out=c3(qa2)[:, :, h2:Lc], in0=c3(hq[:, h, 0:NT])[:, :, h2:Lc], in1=c3(gA)[:, :, h2:Lc], op=ALU.mult)
                        PL.tensor_tensor(out=ka1, in0=fK, in1=gB, op=ALU.mult)
                    PL.tensor_copy(out=sm[:, 2 + hs, 0:nch], in_=sm[:, 1, 0:nch])
                    t5 = tick("pool", PL.tensor_tensor(out=kd, in0=fK, in1=gD, op=ALU.mult))
                    wait("dve", t5)
                    return t5

            def pe_part(h, t5):
                    hs = h % 2
                    sb_tok = nonloc["sb_tok"]
                    qe, qa2, ka0, ka1, kd = [qk4[:, hs, i, 0:NT] for i in range(5)]
                    ebl = sm[:, 2 + hs, :]
                    bA = k.banks[0]; bT = k.banks[1]
                    wait("pe", t5); wait("pe", bA.free); wait("pe", bT.free)
                    for ci in range(nch):
                        cs = ci * Lc
                        ins = PE.matmul(bA.ap[0:h2, cs:cs + h2], lhsT=ka0[:, cs:cs + h2], rhs=qe[:, cs:cs + h2], start=True, stop=True)
                        if two:
                            ins = PE.matmul(bA.ap[0:Lc, cs + h2:cs + Lc], lhsT=ka1[:, cs:cs + Lc], rhs=qa2[:, cs + h2:cs + Lc], start=True, stop=True)
                    bA.ready = tick("pe", ins)
                    bTv = bT.ap.bitcast(BF16)
                    for ci in range(nch):
                        ins = PE.transpose(out=bTv[0:Lc, 128 * ci:128 * ci + 128], in_=kd[:, ci * Lc:(ci + 1) * Lc], identity=identB[:])
                    bT.ready = tick("pe", ins)
                    wait("dve", bA.ready)
                    bA3 = bA.ap[:, 0:nch * Lc].rearrange("p (c t) -> p c t", t=Lc)
                    am_t = tick("dve", DVE.tensor_tensor(out=Am[0:h2, hs, :, 0:Lc], in0=bA3[0:h2], in1=triP[0:h2, 0:Lc].unsqueeze(1).to_broadcast([h2, nch, Lc]), op=ALU.mult))
                    if two:
                        am_t = tick("dve", DVE.tensor_tensor(out=Am[h2:Lc, hs, :, h2:Lc], in0=bA3[h2:Lc, :, h2:Lc],
                                                             in1=triP[h2:Lc, h2:Lc].unsqueeze(1).to_broadcast([Lc - h2, nch, Lc - h2]), op=ALU.mult))
                    bA.free = am_t
                    wait("act", bT.ready); wait("act", pe_use[hs])
                    kt_t = bT.free = tick("act", ACT.copy(out=kd_tm[0:Lc, hs], in_=bTv[0:Lc, 0:nch * 128].rearrange("p (b e) -> p b e", e=128)))
                    bO = k.banks[2 + (h % 2)]
                    wait("pe", bO.free); wait("pe", am_t); wait("pe", kt_t)
                    for ci in range(nch):
                        cs = ci * Lc
                        if isp:
                            Sst = S_all[:, l, h, :]; Sbh = Sb[:, h, :]
                        else:
                            Sst = Ssm[:, ci, h, :]; Sbh = Sbs[:, ci, h, :]
                        wait("pe", sb_tok)
                        PE.matmul(bO.ap[:, cs:cs + Lc], lhsT=Sbh, rhs=qe[:, cs:cs + Lc], start=True, stop=False)
                        PE.matmul(bO.ap[:, cs:cs + Lc], lhsT=hi_tm[0:Lc, h, ci, :], rhs=Am[0:Lc, hs, ci, 0:Lc], start=False, stop=True)
                        bS = k.banks[4 + (ci % 2)]
                        wait("pe", bS.free)
                        ins = PE.matmul(bS.ap[:, 0:128], lhsT=kd_tm[0:Lc, hs, ci, :], rhs=hi_tm[0:Lc, h, ci, :], start=True, stop=True)
                        bS.ready = tick("pe", ins)
                        wait("dve", bS.ready)
                        bS.free = tick("dve", DVE.scalar_tensor_tensor(out=Sst, in0=Sst, scalar=ebl[:, ci:ci + 1], in1=bS.ap[:, 0:128], op0=ALU.mult, op1=ALU.add))
                        sb_tok = tick("dve", DVE.tensor_copy(out=Sbh, in_=Sst))
                    bO.ready = bS.ready
                    pe_use[hs] = bS.ready
                    dve_use[hs] = sb_tok
                    wait("act", bO.ready)
                    q_t = tick("act", ACT.activation(out=osq[:, 0:NT], in_=bO.ap[:, 0:NT], func=AF.Square))
                    bq = k.banks[6 + (h % 2)]
                    wait("pe", bq.free); wait("pe", q_t)
                    bq.ready = tick("pe", PE.matmul(bq.ap[:, 0:NT], lhsT=onesB[:], rhs=osq[:, 0:NT], start=True, stop=True))
                    wait("dve", rstd_act(bq, rsh[:, 0:NT], bq.ap[:, 0:NT], 128.0 * EPS)); wait("dve", hg_tok)
                    bO.free = tick("dve", DVE.scalar_tensor_tensor(out=ot[:, 0:NT], in0=bO.ap[:, 0:NT], scalar=ghg[:, l:l + 1], in1=rsh[:, 0:NT], op0=ALU.mult, op1=ALU.mult))
                    last_o = tick("dve", DVE.tensor_tensor(out=o_h[:, h, 0:NT], in0=ot[:, 0:NT], in1=hgt[:, h, 0:NT], op=ALU.mult))
                    if cfg.DEBUG and isp and T.t == 0 and l == 0:
                        wait("pool", last_o)
                        dd_ = dtick("dbg", POOL.dma_start(out=dbg_h[:, h, :], in_=ot[:, :]))
                        wait("dve", dd_)
                    nonloc["sb_tok"] = sb_tok; nonloc["last_o"] = last_o

            nonloc = {"sb_tok": sb_tok, "last_o": None}
            t5s = {0: prep(0)}
            for h in range(NH):
                if h + 1 < NH:
                    t5s[h + 1] = prep(h + 1)
                pe_part(h, t5s[h])
            sb_tok = nonloc["sb_tok"]; last_o = nonloc["last_o"]
            ptoks = []
            if isp and T.t == NPT - 1:
                wait("pool", sb_tok)
                ptoks.append(dtick("out", POOL.dma_start(out=sp_o[l].rearrange("h kk v -> kk h v"), in_=S_all[:, l])))
            if not isp:
                wait("pool", sb_tok)
                ptoks.append(dtick("out", POOL.dma_start(out=ss_o[l].rearrange("i h kk v -> kk (i h) v"), in_=Ssm[:].rearrange("p i h v -> p (i h) v"))))
            k.barrier(pool_tokens=ptoks)
        k.es_cur = es

        with contextlib.ExitStack() as pes:
            k.es_cur = pes
            mt = sb("mt", [128, 2, 4, NT], F32)
            mt_free = [None, None]
            m_tok = None
            for m in range(8):
                w = k.wnext("mrg")
                bks = [k.bank_next() for _ in range(4)]
                srcs = [h_bf, h_bf, o_att, o_h]
                for bi in range(4):
                    wait("pe", bks[bi].free)
                wait("pe", h_tok); wait("pe", oa_tok); wait("pe", last_o)
                for bi in range(4):
                    for kc in range(KC):
                        ins = PE.matmul(bks[bi].ap[:, 0:NT], lhsT=w[:, kc, 128 * bi:128 * bi + 128], rhs=srcs[bi][:, kc, 0:NT], start=(kc == 0), stop=(kc == KC - 1))
                r_t = k.wdone(ins)
                for bi in range(4):
                    bks[bi].ready = r_t
                s = m % 2
                wait("act", r_t); wait("act", mt_free[s])
                ACT.activation(out=mt[:, s, 0, 0:NT], in_=bks[0].ap[:, 0:NT], func=AF.Sigmoid)
                a_t = tick("act", ACT.activation(out=mt[:, s, 1, 0:NT], in_=bks[1].ap[:, 0:NT], func=AF.Sigmoid))
                bks[0].free = bks[1].free = a_t
                wait("dve", a_t); wait("dve", r_t)
                DVE.tensor_tensor(out=mt[:, s, 2, 0:NT], in0=mt[:, s, 0, 0:NT], in1=bks[2].ap[:, 0:NT], op=ALU.mult)
                d_t = tick("dve", DVE.tensor_tensor(out=mt[:, s, 3, 0:NT], in0=mt[:, s, 1, 0:NT], in1=bks[3].ap[:, 0:NT], op=ALU.mult))
                bks[2].free = bks[3].free = d_t
                m_tok = tick("dve", DVE.tensor_tensor(out=merged[:, m, 0:NT], in0=mt[:, s, 2, 0:NT], in1=mt[:, s, 3, 0:NT], op=ALU.add))
                mt_free[s] = m_tok
            for g in range(2):
                w = k.wnext("wo")
                for j in range(4):
                    m = 4 * g + j
                    bk = k.bank_next()
                    wait("pe", bk.free); wait("pe", m_tok)
                    for kc in range(KC):
                        ins = PE.matmul(bk.ap[:, 0:NT], lhsT=w[:, kc, 128 * j:128 * j + 128], rhs=merged[:, kc, 0:NT], start=(kc == 0), stop=(kc == KC - 1))
                    bk.ready = tick("pe", ins)
                    wait("dve", bk.ready)
                    for (c0, c1, sq) in T.segs:
                        ins = DVE.scalar_tensor_tensor(out=x_fm[:, m, c0:c1], in0=bk.ap[:, c0:c1], scalar=MOD[:, l, 5, m, sq:sq + 1],
                                                       in1=x_fm[:, m, c0:c1], op0=ALU.mult, op1=ALU.add)
                    bk.free = tick("dve", ins)
                k.wfree_tok[k.w_cur] = bk.ready
            k.barrier()
        k.es_cur = es

    def final_out(T):
        NT, nblk, bp = T.NT, T.nblk, T.bp
        with contextlib.ExitStack() as pes:
            k.es_cur = pes
            x_tmp = sb("x_tmp", [128, KC, NT], F32)
            ystage = sb("ystage", [128, nblk, D], F32)
            yt = norm_phase(T, 0, None, None, x_tmp)
            wait("pe", yt)
            et = None
            for b in range(nblk):
                for cg in range(2):
                    bk = k.bank_next()
                    wait("pe", bk.free)
                    for j in range(4):
                        ins = PE.transpose(out=bk.ap[0:bp, 128 * j:128 * j + 128], in_=x_tmp[:, 4 * cg + j, b * bp:(b + 1) * bp], identity=identF[:])
                    bk.ready = tick("pe", ins)
                    et = evac_copy(ystage[0:bp, b, 512 * cg:512 * cg + 512], bk.ap[0:bp, 0:512], bk)
                    wait("pool", et)
            wait("pool", ("act", k.cnt["act"])) if k.cnt["act"] else None
            wait("pool", ("dve", k.cnt["dve"])) if k.cnt["dve"] else None
            if T.kind == "p":
                dst = yp[T.t * TT:(T.t + 1) * TT, :].rearrange("(b p) f -> p b f", p=128)
            else:
                dst = ys.rearrange("(b p) f -> p b f", p=bp)
            ot_ = dtick("out", POOL.dma_start(out=dst, in_=ystage[0:bp]))
            k.barrier(pool_tokens=[ot_])
        k.es_cur = es

    DVE.memset(S_all[:], 0.0)
    for T in tiles:
        load_x(T)
        for l in range(DEPTH):
            ht = norm_mod(T, l, 1, 0)
            ffn_phase(T, l, 1, ht)
            ht = norm_mod(T, l, 4, 3)
            mixing(T, l, ht)
            ht = norm_mod(T, l, 7, 6)
            ffn_phase(T, l, 2, ht)
        final_out(T)
    for s in ("out", "st0", "st1", "st2", "kvst") + (("dbg",) if cfg.DEBUG else ()):
        if k.cnt[s]:
            POOL.wait_ge(k.sem[s], k.cnt[s])
    assert k.w_used == len(k.wseq)
    es.close()
    return nc


_WNAMES = ["w_ada", "b_ada", "g_ffn1", "w_ffn1_gu", "w_ffn1_d", "g_mix", "w_in", "att_lambda", "g_att_sub",
           "hg_lb_logits", "g_hg_norm", "w_br_att", "w_br_hg", "w_out", "g_ffn2", "w_ffn2_gu", "w_ffn2_d", "g_final"]


def run(cfg, inputs, n_cores=8):
    nc = build(cfg)
    NS, TS = cfg.NS, cfg.TS
    f = lambda a: np.ascontiguousarray(np.asarray(a, dtype=np.float32))
    in_maps = []
    nb = inputs["x_prompt"].shape[0]
    for c in range(n_cores):
        sq = (c * nb) // n_cores
        s0 = c * NS
        m = {
            "xp": f(inputs["x_prompt"][sq]),
            "xs": f(inputs["x_sample"][s0:s0 + NS]).reshape(NS * TS, D),
            "ck": f(np.asarray(inputs["cache_k"])[:, s0:s0 + NS].reshape(cfg.DEPTH, NS, cfg.PAST, D)),
            "cv": f(np.asarray(inputs["cache_v"])[:, s0:s0 + NS].reshape(cfg.DEPTH, NS, cfg.PAST, D)),
            "st": f(np.asarray(inputs["state_hgrn"])[:, s0:s0 + NS]),
            "cc": f(np.concatenate([np.asarray(inputs["c_prompt"])[sq:sq + 1], np.asarray(inputs["c_sample"])[s0:s0 + NS]], 0)),
        }
        for n in _WNAMES:
            m[n] = f(inputs[n])
        in_maps.append(m)
    res = run_bass_kernel_spmd(nc, in_maps, core_ids=list(range(n_cores)))
    R = res.results
    run.last = R
    per = n_cores // nb
    DEPTH = cfg.DEPTH
    y_prompt = np.stack([R[per * b]["yp"] for b in range(nb)], 0)
    k_prompt = np.stack([R[per * b]["kp"] for b in range(nb)], 1).reshape(DEPTH, nb, cfg.SEQ, NH, 128)
    v_prompt = np.stack([R[per * b]["vp"] for b in range(nb)], 1).reshape(DEPTH, nb, cfg.SEQ, NH, 128)
    s_prompt = np.stack([R[per * b]["sp"] for b in range(nb)], 1)
    y_sample = np.concatenate([R[c]["ys"].reshape(NS, TS, D) for c in range(n_cores)], 0)
    k_sample = np.concatenate([R[c]["ks"].reshape(DEPTH, NS, TS, NH, 128) for c in range(n_cores)], 1)
    v_sample = np.concatenate([R[c]["vs"].reshape(DEPTH, NS, TS, NH, 128) for c in range(n_cores)], 1)
    s_sample = np.concatenate([R[c]["ss"] for c in range(n_cores)], 1)
    return tuple(np.ascontiguousarray(a, dtype=np.float32) for a in
                 (y_prompt, y_sample, k_prompt, v_prompt, s_prompt, k_sample, v_sample, s_sample))


def kernel(**inputs):
    cfg = Cfg()
    return run(cfg, inputs, 8)
```

```python
import contextlib
import math
import numpy as np
import concourse.bass as bass
import concourse.mybir as mybir
from concourse.bass_utils import run_bass_kernel_spmd

F32 = mybir.dt.float32
BF16 = mybir.dt.bfloat16
AF = mybir.ActivationFunctionType
ALU = mybir.AluOpType

D = 1024
DFF = 2816
NH = 8
KC = 8
FC = 22
DIN = 9216
EPS = 1e-6
TINY = 1e-30
NWG = 62
WSLOT = 4096
NBUF = 4


class Cfg:
    def __init__(self, SEQ=8192, DEPTH=4, PAST=2048, TS=32, NS=2, TT=512):
        self.SEQ, self.DEPTH, self.PAST, self.TS, self.NS, self.TT = SEQ, DEPTH, PAST, TS, NS, TT
        self.DEBUG = False


class Tile:
    pass


class B:
    def __init__(self, cfg):
        self.cfg = cfg
        self.nc = bass.Bass("TRN2", target_bir_lowering=False)
        self.es = contextlib.ExitStack()
        self.cnt = {}
        self.sem = {}
        self.waited = {}
        self.eng = {"pe": self.nc.tensor, "act": self.nc.scalar, "dve": self.nc.vector,
                    "pool": self.nc.gpsimd, "sp": self.nc.sync}
        for e in ("pe", "act", "dve", "pool"):
            self.newsem(e)
        self.bar_n = 0
        self.newsem("bar")
        self.last_tok = {}
        self.last_ins = {}

    def newsem(self, name):
        self.sem[name] = self.es.enter_context(self.nc.semaphore(name))
        self.cnt[name] = 0
        return name

    def tick(self, E, ins):
        if self.last_ins.get(E) is ins and self.last_tok.get(E) is not None:
            return self.last_tok[E]
        ins.then_inc(self.sem[E], 1)
        self.cnt[E] += 1
        tok = (E, self.cnt[E])
        if self.last_ins.get(E) is ins:
            self.last_tok[E] = tok
        return tok

    def pre_issue(self, E):
        li = self.last_ins.get(E)
        if li is None:
            return
        tok = self.tick(E, li)
        k = (E, E)
        if self.waited.get(k, 0) < tok[1]:
            self.eng[E].wait_ge(self.sem[E], tok[1])
            self.waited[k] = tok[1]

    def post_issue(self, E, ins):
        self.last_ins[E] = ins
        self.last_tok[E] = None

    def dtick(self, S, ins):
        ins.then_inc(self.sem[S], 16)
        self.cnt[S] += 16
        return (S, self.cnt[S])

    def wait(self, who, tok):
        if tok is None:
            return
        S, v = tok
        if S == who:
            return
        k = (who, S)
        if self.waited.get(k, 0) >= v:
            return
        self.eng[who].wait_ge(self.sem[S], v)
        self.waited[k] = v

    def sb(self, name, shape, dt):
        self.uid = getattr(self, "uid", 0) + 1
        return self.es_cur.enter_context(self.nc.sbuf_tensor("%s_%d" % (name, self.uid), shape, dt))

    def barrier(self, pool_tokens=()):
        nc = self.nc
        for t in pool_tokens:
            self.wait("pool", t)
        self.pre_issue("act")
        nc.scalar.copy(out=self.scrA[0:1, 0:1], in_=self.scrA[0:1, 1:2]).then_inc(self.sem["bar"], 1)
        self.pre_issue("dve")
        nc.vector.memset(self.scrV[0:1, 0:1], 0.0).then_inc(self.sem["bar"], 1)
        nc.gpsimd.memset(self.scrP[0:1, 0:1], 0.0).then_inc(self.sem["bar"], 1)
        self.bar_n += 3
        for e in ("act", "dve", "pool"):
            self.eng[e].wait_ge(self.sem["bar"], self.bar_n)
        self.last_ins["act"] = None
        self.last_ins["dve"] = None

    def bank_next(self):
        b = self.banks[self.bank_rr % 8]
        self.bank_rr += 1
        return b

    def wnext(self, kind):
        nc = self.nc
        i = self.w_used
        assert self.wseq[i][0] == kind, (self.wseq[i], kind)
        upto = min(i + NBUF - 1, len(self.wseq) - 1)
        while self.w_issued <= upto:
            j = self.w_issued
            slot = j % NBUF
            if j >= NBUF:
                self.wait("sp", self.wfree_tok[j - NBUF])
            _, l, gi, nk, ncol, first = self.wseq[j]
            if first:
                self.wait("sp", self.conv_tok[l])
            src = self.wscr[l * NWG + gi, :, 0:nk * ncol]
            ins = nc.sync.dma_start(out=self.wbuf[:, slot, 0:nk * ncol], in_=src)
            self.wld_tok[j] = self.dtick("wld%d" % slot, ins)
            self.w_issued += 1
        self.wait("pe", self.wld_tok[i])
        _, l, gi, nk, ncol, _ = self.wseq[i]
        self.w_used += 1
        self.w_cur = i
        return self.wbuf[:, i % NBUF, 0:nk * ncol].rearrange("p (k c) -> p k c", k=nk)

    def wdone(self, ins):
        self.wfree_tok[self.w_cur] = self.tick("pe", ins)
        return self.wfree_tok[self.w_cur]


class EngProxy:
    def __init__(self, k, name, eng):
        self._k, self._name, self._eng = k, name, eng

    def __getattr__(self, attr):
        f = getattr(self._eng, attr)
        if attr in ("wait_ge",):
            return f
        k, name = self._k, self._name

        def wrapped(*a, **kw):
            k.pre_issue(name)
            ins = f(*a, **kw)
            k.post_issue(name, ins)
            return ins
        return wrapped


def _wgroups():
    g = []
    for i in range(11):
        g.append(("gu1", i, 8, 512 if i < 10 else 512))
    for i in range(8):
        g.append(("d1", i, 22, 128))
    for i in range(4):
        g.append(("qk", i, 8, 512))
    for i in range(2):
        g.append(("v", i, 8, 512))
    for i in range(8):
        g.append(("hg", i, 8, 512))
    for i in range(8):
        g.append(("mrg", i, 8, 512))
    for i in range(2):
        g.append(("wo", i, 8, 512))
    for i in range(11):
        g.append(("gu2", i, 8, 512))
    for i in range(8):
        g.append(("d2", i, 22, 128))
    assert len(g) == NWG
    return g


def build(cfg):
    k = B(cfg)
    nc = k.nc
    es = k.es
    SEQ, DEPTH, PAST, TS, NS, TT = cfg.SEQ, cfg.DEPTH, cfg.PAST, cfg.TS, cfg.NS, cfg.TT
    NTS = NS * TS
    NPT = SEQ // TT
    NSQ = 1 + NS

    def din(name, shape):
        return nc.dram_tensor(name, list(shape), F32, kind="ExternalInput").ap()

    def dout(name, shape):
        return nc.dram_tensor(name, list(shape), F32, kind="ExternalOutput").ap()

    xp = din("xp", [SEQ, D]); xs = din("xs", [NTS, D])
    ck = din("ck", [DEPTH, NS, PAST, D]); cv = din("cv", [DEPTH, NS, PAST, D])
    st = din("st", [DEPTH, NS, NH, 128, 128]); cc = din("cc", [NSQ, D])
    w_ada = din("w_ada", [DEPTH, D, 9 * D]); b_ada = din("b_ada", [DEPTH, 9 * D])
    g_ffn1 = din("g_ffn1", [DEPTH, D]); w_gu1 = din("w_ffn1_gu", [DEPTH, D, 2 * DFF]); w_d1 = din("w_ffn1_d", [DEPTH, DFF, D])
    g_mix = din("g_mix", [DEPTH, D]); w_in = din("w_in", [DEPTH, D, DIN])
    att_lambda = din("att_lambda", [DEPTH, 4, 64]); g_att_sub = din("g_att_sub", [DEPTH, 128])
    hg_lb = din("hg_lb_logits", [DEPTH, D]); g_hg_norm = din("g_hg_norm", [DEPTH, 128])
    w_ba = din("w_br_att", [DEPTH, D, D]); w_bh = din("w_br_hg", [DEPTH, D, D]); w_o = din("w_out", [DEPTH, D, D])
    g_ffn2 = din("g_ffn2", [DEPTH, D]); w_gu2 = din("w_ffn2_gu", [DEPTH, D, 2 * DFF]); w_d2 = din("w_ffn2_d", [DEPTH, DFF, D])
    g_final = din("g_final", [D])

    yp = dout("yp", [SEQ, D]); ys = dout("ys", [NTS, D])
    kp = dout("kp", [DEPTH, SEQ, D]); vp = dout("vp", [DEPTH, SEQ, D]); sp_o = dout("sp", [DEPTH, NH, 128, 128])
    ks = dout("ks", [DEPTH, NTS, D]); vs = dout("vs", [DEPTH, NTS, D]); ss_o = dout("ss", [DEPTH, NS, NH, 128, 128])

    if cfg.DEBUG:
        dbg_att = dout("dbg_att", [128, NH, TT]); dbg_h = dout("dbg_h", [128, NH, TT])
        dbg_mod = dout("dbg_mod", [128, DEPTH * 9 * KC * NSQ]); dbg_x = dout("dbg_x", [128, KC * TT]); dbg_h1 = dout("dbg_h1", [128, KC * TT])
        dbg_hid = dout("dbg_hid", [128, FC * TT])
        k.newsem("dbg")
    k.wscr = nc.dram_tensor("wscr", [DEPTH * NWG, 128, WSLOT], BF16, kind="Internal").ap()
    kscr = nc.dram_tensor("kscr", [DEPTH, NPT, 128, NH * TT], BF16, kind="Internal").ap()
    vscr = nc.dram_tensor("vscr", [DEPTH, NPT, 128, NH * TT], BF16, kind="Internal").ap()

    k.es_cur = es
    sb = k.sb
    x_fm = sb("x_fm", [128, KC, TT], F32)
    h_bf = sb("h_bf", [128, KC, TT], BF16)
    o_att = sb("o_att", [128, NH, TT], BF16)
    o_h = sb("o_h", [128, NH, TT], BF16)
    merged = sb("merged", [128, KC, TT], BF16)
    rs = sb("rs", [128, TT], F32)
    S_all = sb("S_all", [128, DEPTH, NH, 128], F32)
    Sb = sb("Sb", [128, NH, 128], BF16)
    k.wbuf = sb("wbuf", [128, NBUF, WSLOT], BF16)
    identF = sb("identF", [128, 128], F32)
    identB = sb("identB", [128, 128], BF16)
    onesB = sb("onesB", [128, 128], BF16)
    triP = sb("triP", [128, 128], F32)
    triS = sb("triS", [128, 128], F32)
    M0p = sb("M0p", [128, TT], F32)
    M0s = sb("M0s", [128, NTS], F32)
    MOD = sb("MOD", [128, DEPTH, 9, KC, NSQ], F32)
    gfin = sb("gfin", [128, KC], F32)
    gatt = sb("gatt", [128, DEPTH], F32)
    ghg = sb("ghg", [128, DEPTH], F32)
    lamt = sb("lamt", [128, DEPTH], F32)
    lbt = sb("lbt", [128, DEPTH, NH], F32)
    omlt = sb("omlt", [128, DEPTH, NH], F32)
    nomlt = sb("nomlt", [128, DEPTH, NH], F32)
    k.scrA = sb("scrA", [128, 4], F32); k.scrV = sb("scrV", [128, 4], F32); k.scrP = sb("scrP", [128, 4], F32)

    pbs = [es.enter_context(nc.psum_tensor("pb%d" % i, [128, 2, 512], F32)) for i in range(4)]

    class Bank:
        pass
    k.banks = []
    for i in range(8):
        b = Bank()
        b.ap = pbs[i // 2][:, i % 2, :]
        b.pair = pbs[i // 2]
        b.ready = None
        b.free = None
        k.banks.append(b)
    k.bank_rr = 0

    for i in range(NBUF):
        k.newsem("wld%d" % i)
    for s in ("wad0", "wad1", "cv0", "ld", "sld", "st0", "st1", "st2", "xld", "kvst", "kv0", "kv1", "kv2", "kv3", "out"):
        k.newsem(s)

    tick, dtick, wait = k.tick, k.dtick, k.wait
    PE, POOL = nc.tensor, nc.gpsimd
    ACT = EngProxy(k, "act", nc.scalar)
    DVE = EngProxy(k, "dve", nc.vector)

    groups = _wgroups()
    k.conv_tok = {}
    for l in range(DEPTH):
        k.newsem("cvl%d" % l)
        gu = {1: w_gu1[l].rearrange("(kc p) c -> p kc c", p=128), 2: w_gu2[l].rearrange("(kc p) c -> p kc c", p=128)}
        dd = {1: w_d1[l].rearrange("(kc p) c -> p kc c", p=128), 2: w_d2[l].rearrange("(kc p) c -> p kc c", p=128)}
        wi = w_in[l].rearrange("(kc p) c -> p kc c", p=128)
        wba = w_ba[l].rearrange("(kc p) c -> p kc c", p=128)
        wbh = w_bh[l].rearrange("(kc p) c -> p kc c", p=128)
        wo = w_o[l].rearrange("(kc p) c -> p kc c", p=128)
        for gi, (kind, i, nk, ncol) in enumerate(groups):
            dst = k.wscr[l * NWG + gi, :, 0:nk * ncol].rearrange("p (k c) -> p k c", k=nk)
            parts = []
            if kind in ("gu1", "gu2"):
                W = gu[1 if kind == "gu1" else 2]
                parts = [(0, 256, W[:, :, 256 * i:256 * i + 256]), (256, 512, W[:, :, DFF + 256 * i:DFF + 256 * i + 256])]
            elif kind in ("d1", "d2"):
                W = dd[1 if kind == "d1" else 2]
                parts = [(0, 128, W[:, :, 128 * i:128 * i + 128])]
            elif kind == "qk":
                parts = [(0, 512, wi[:, :, 512 * i:512 * i + 512])]
            elif kind == "v":
                parts = [(0, 512, wi[:, :, 2048 + 512 * i:2048 + 512 * i + 512])]
            elif kind == "hg":
                parts = [(0, 512, wi[:, :, 3072 + 512 * i:3072 + 512 * i + 512])]
            elif kind == "mrg":
                parts = [(0, 128, wi[:, :, 7168 + 128 * i:7168 + 128 * i + 128]),
                         (128, 256, wi[:, :, 8192 + 128 * i:8192 + 128 * i + 128]),
                         (256, 384, wba[:, :, 128 * i:128 * i + 128]),
                         (384, 512, wbh[:, :, 128 * i:128 * i + 128])]
            elif kind == "wo":
                parts = [(0, 512, wo[:, :, 512 * i:512 * i + 512])]
            for (c0, c1, src) in parts:
                ins = POOL.dma_start(out=dst[:, :, c0:c1], in_=src)
                k.conv_tok[l] = dtick("cvl%d" % l, ins)

    tiles = []
    for t in range(NPT):
        T = Tile(); T.kind = "p"; T.t = t; T.NT = TT; T.nblk = TT // 128; T.bp = 128
        T.segs = [(0, TT, 0)]
        tiles.append(T)
    T = Tile(); T.kind = "s"; T.t = 0; T.NT = NTS; T.nblk = NS; T.bp = TS
    T.segs = [(i * TS, (i + 1) * TS, 1 + i) for i in range(NS)]
    tiles.append(T)
    k.wseq = []
    for ti, T in enumerate(tiles):
        for l in range(DEPTH):
            for gi, (kind, i, nk, ncol) in enumerate(groups):
                k.wseq.append((kind, l, gi, nk, ncol, ti == 0 and gi == 0))
    k.w_used = 0; k.w_issued = 0; k.wld_tok = {}; k.wfree_tok = {}

    POOL.memset(identF[:], 1.0)
    POOL.affine_select(out=identF[:], in_=identF[:], pattern=[[-1, 128]], compare_op=ALU.is_equal, fill=0.0, base=0, channel_multiplier=1)
    POOL.tensor_copy(out=identB[:], in_=identF[:])
    POOL.memset(onesB[:], 1.0)
    POOL.memset(triP[:], 1.0)
    POOL.affine_select(out=triP[:], in_=triP[:], pattern=[[1, 128]], compare_op=ALU.is_ge, fill=0.0, base=0, channel_multiplier=-1)
    POOL.tensor_copy(out=triS[:], in_=triP[:])
    POOL.memset(triP[0:64, 64:128], 0.0)
    POOL.memset(triS[0:32, 32:64], 0.0)
    POOL.memset(M0p[:], 1.0)
    for c in range(TT // 64):
        POOL.memset(M0p[:, 64 * c:64 * c + 1], 0.0)
    POOL.memset(M0s[:], 1.0)
    for c in range(NS):
        POOL.memset(M0s[:, TS * c:TS * c + 1], 0.0)
    POOL.memset(k.scrP[:], 0.0)
    c_tok = tick("pool", POOL.memset(k.scrP[:, 0:1], 0.0))
    DVE.memset(k.scrV[:], 0.0)
    wait("act", c_tok)
    ACT.copy(out=k.scrA[:], in_=identF[:, 0:4])

    with contextlib.ExitStack() as pes:
        k.es_cur = pes
        cT = sb("cT", [128, KC, NSQ], F32)
        cact = sb("cact", [128, KC, NSQ], F32)
        badaT = sb("badaT", [128, DEPTH, 72], F32)
        gT = sb("gT", [128, 3, DEPTH, KC], F32)
        lbl = sb("lbl", [128, DEPTH, NH], F32)
        lam_in = sb("lam_in", [128, DEPTH, 4, 64], F32)
        lam_w = sb("lam_w", [128, DEPTH, 2, 64], F32)
        lam_s = sb("lam_s", [128, DEPTH, 2], F32)
        wad = sb("wad", [128, 2, KC, 1024], F32)
        ld = []
        for s_ in range(NSQ):
            ld.append(dtick("ld", nc.sync.dma_start(out=cT[:, :, s_], in_=cc[s_].rearrange("(c p) -> p c", p=128), allow_slow_non_contiguous=True)))
        for l in range(DEPTH):
            ld.append(dtick("ld", nc.sync.dma_start(out=badaT[:, l, :], in_=b_ada[l].rearrange("(j p) -> p j", p=128), allow_slow_non_contiguous=True)))
            for gi, gsrc in enumerate((g_ffn1, g_mix, g_ffn2)):
                ld.append(dtick("ld", nc.sync.dma_start(out=gT[:, gi, l, :], in_=gsrc[l].rearrange("(c p) -> p c", p=128), allow_slow_non_contiguous=True)))
            ld.append(dtick("ld", nc.sync.dma_start(out=lbl[:, l, :], in_=hg_lb[l].rearrange("(h p) -> p h", p=128), allow_slow_non_contiguous=True)))
        ld.append(dtick("ld", nc.sync.dma_start(out=gfin[:], in_=g_final.rearrange("(c p) -> p c", p=128), allow_slow_non_contiguous=True)))
        ld.append(dtick("ld", nc.sync.dma_start(out=gatt[:], in_=g_att_sub.rearrange("l p -> p l"), allow_slow_non_contiguous=True)))
        ld.append(dtick("ld", nc.sync.dma_start(out=ghg[:], in_=g_hg_norm.rearrange("l p -> p l"), allow_slow_non_contiguous=True)))
        ld.append(dtick("ld", nc.sync.dma_start(out=lam_in[:].rearrange("p l a b -> p (l a b)"),
                                                 in_=att_lambda.rearrange("l a b -> (l a b)").partition_broadcast(128))))
        ldall = ld[-1]
        wait("dve", ldall); wait("act", ldall)
        ct = tick("act", ACT.activation(out=cact[:], in_=cT[:], func=AF.Silu))
        for l in range(DEPTH):
            DVE.tensor_tensor(out=lam_w[:, l, 0, :], in0=lam_in[:, l, 0, :], in1=lam_in[:, l, 1, :], op=ALU.mult)
            DVE.tensor_tensor(out=lam_w[:, l, 1, :], in0=lam_in[:, l, 2, :], in1=lam_in[:, l, 3, :], op=ALU.mult)
        dt_ = tick("dve", DVE.reduce_sum(out=lam_s[:].rearrange("p l a -> p (l a)"), in_=lam_w[:].rearrange("p l a b -> p (l a) b"), axis=mybir.AxisListType.X))
        wait("act", dt_)
        at_ = tick("act", ACT.activation(out=lam_s[:], in_=lam_s[:], func=AF.Exp))
        wait("dve", at_)
        for l in range(DEPTH):
            lam_init = 0.8 - 0.6 * math.exp(-0.3 * l)
            DVE.tensor_tensor(out=lamt[:, l:l + 1], in0=lam_s[:, l, 1:2], in1=lam_s[:, l, 0:1], op=ALU.subtract)
            DVE.tensor_scalar(out=lamt[:, l:l + 1], in0=lamt[:, l:l + 1], scalar1=-lam_init, scalar2=None, op0=ALU.add)
            DVE.tensor_scalar(out=gatt[:, l:l + 1], in0=gatt[:, l:l + 1], scalar1=(1.0 - lam_init) * math.sqrt(128.0), scalar2=None, op0=ALU.mult)
        DVE.tensor_scalar(out=ghg[:], in0=ghg[:], scalar1=math.sqrt(128.0), scalar2=None, op0=ALU.mult)
        DVE.tensor_scalar(out=gfin[:], in0=gfin[:], scalar1=32.0, scalar2=None, op0=ALU.mult)
        at2 = tick("act", ACT.activation(out=lbl[:], in_=lbl[:], func=AF.Exp))
        wait("dve", at2)
        lsum = lam_w[:, 0, 0, 0:NH]
        DVE.tensor_copy(out=lsum, in_=lbl[:, 0, :])
        for l in range(1, DEPTH):
            DVE.tensor_tensor(out=lsum, in0=lsum, in1=lbl[:, l, :], op=ALU.add)
        DVE.reciprocal(out=lsum, in_=lsum)
        DVE.memset(lbt[:, 0, :], 0.0)
        for l in range(1, DEPTH):
            DVE.tensor_tensor(out=lbl[:, l, :], in0=lbl[:, l, :], in1=lsum, op=ALU.mult)
            DVE.tensor_tensor(out=lbt[:, l, :], in0=lbt[:, l - 1, :], in1=lbl[:, l, :], op=ALU.add)
        DVE.tensor_scalar(out=omlt[:], in0=lbt[:], scalar1=-1.0, scalar2=1.0, op0=ALU.mult, op1=ALU.add)
        DVE.tensor_scalar(out=nomlt[:], in0=omlt[:], scalar1=-1.0, scalar2=None, op0=ALU.mult)
        wait("pe", ct)
        wtok = [None, None]
        wfree = [None, None]
        gidx = 0
        for l in range(DEPTH):
            wv = w_ada[l].rearrange("(kc p) c -> p kc c", p=128)
            bk = k.bank_next()
            wait("pe", bk.free)
            outv = bk.ap[:, 0:72 * NSQ].rearrange("p (j s) -> p j s", s=NSQ)
            for g9 in range(9):
                slot = gidx % 2
                wait("sp", wfree[slot])
                wtok[slot] = dtick("wad%d" % slot, nc.sync.dma_start(out=wad[:, slot], in_=wv[:, :, 1024 * g9:1024 * g9 + 1024]))
                wait("pe", wtok[slot])
                for f in range(8):
                    for kc in range(KC):
                        ins = PE.matmul(outv[:, g9 * 8 + f, :], lhsT=wad[:, slot, kc, 128 * f:128 * f + 128], rhs=cact[:, kc, :],
                                        start=(kc == 0), stop=(kc == KC - 1))
                wfree[slot] = tick("pe", ins)
                gidx += 1
            bk.ready = wfree[(gidx - 1) % 2]
            wait("dve", bk.ready)
            ins = DVE.tensor_tensor(out=MOD[:, l].rearrange("p j c s -> p (j c) s"), in0=outv,
                                    in1=badaT[:, l, :].unsqueeze(2).to_broadcast([128, 72, NSQ]), op=ALU.add)
            bk.free = tick("dve", ins)
            for (j_sc, gi_, j_g, half) in ((1, 0, 2, 0.5), (4, 1, 5, 1.0), (7, 2, 8, 0.5)):
                DVE.tensor_scalar(out=MOD[:, l, j_sc], in0=MOD[:, l, j_sc], scalar1=1.0, scalar2=32.0, op0=ALU.add, op1=ALU.mult)
                DVE.tensor_tensor(out=MOD[:, l, j_sc], in0=MOD[:, l, j_sc],
                                  in1=gT[:, gi_, l, :].unsqueeze(2).to_broadcast([128, KC, NSQ]), op=ALU.mult)
                if half != 1.0:
                    DVE.tensor_scalar(out=MOD[:, l, j_g], in0=MOD[:, l, j_g], scalar1=half, scalar2=None, op0=ALU.mult)
        if cfg.DEBUG:
            wait("pool", ("dve", k.cnt["dve"]))
            dmt = dtick("dbg", POOL.dma_start(out=dbg_mod, in_=MOD[:].rearrange("p l j c s -> p (l j c s)")))
            wait("pool", dmt)
        k.barrier()
    k.es_cur = es


    evac_rr = [0]

    def evac_copy(out, in_, bank, scale=None):
        evac_rr[0] += 1
        if scale is not None or evac_rr[0] % 2 == 0:
            wait("act", bank.ready)
            if scale is not None:
                t = tick("act", ACT.mul(out=out, in_=in_, mul=scale))
            else:
                t = tick("act", ACT.copy(out=out, in_=in_))
        else:
            wait("dve", bank.ready)
            t = tick("dve", DVE.tensor_copy(out=out, in_=in_))
        bank.free = t
        return t

    def rstd_act(bank, out_ap, in_ap, eps_total):
        wait("act", bank.ready)
        ACT.activation(out=out_ap, in_=in_ap, func=AF.Ln, bias=eps_total, scale=1.0)
        t = tick("act", ACT.activation(out=out_ap, in_=out_ap, func=AF.Exp, scale=-0.5))
        bank.free = t
        return t

    def norm_phase(T, l, ja, jb, x_tmp):
        NT = T.NT
        sqt = None
        for c in range(KC):
            sqt = tick("act", ACT.activation(out=h_bf[:, c, 0:NT], in_=x_fm[:, c, 0:NT], func=AF.Square))
        bk = k.bank_next()
        wait("pe", bk.free); wait("pe", sqt)
        for c in range(KC):
            ins = PE.matmul(bk.ap[:, 0:NT], lhsT=onesB[:], rhs=h_bf[:, c, 0:NT], start=(c == 0), stop=(c == KC - 1))
        bk.ready = tick("pe", ins)
        wait("dve", rstd_act(bk, rs[:, 0:NT], bk.ap[:, 0:NT], 1024.0 * EPS))
        ht = None
        for c in range(KC):
            for (c0, c1, sq) in T.segs:
                a_ap = gfin[:, c:c + 1] if ja is None else MOD[:, l, ja, c, sq:sq + 1]
                ht = tick("dve", DVE.scalar_tensor_tensor(out=x_tmp[:, c, c0:c1], in0=x_fm[:, c, c0:c1], scalar=a_ap, in1=rs[:, c0:c1], op0=ALU.mult, op1=ALU.mult))
                if jb is not None:
                    ht = tick("dve", DVE.tensor_scalar(out=h_bf[:, c, c0:c1], in0=x_tmp[:, c, c0:c1], scalar1=MOD[:, l, jb, c, sq:sq + 1], scalar2=None, op0=ALU.add))
        return ht

    def norm_mod(T, l, ja, jb):
        with contextlib.ExitStack() as pes:
            k.es_cur = pes
            NT = T.NT
            x_tmp = sb("x_tmp", [128, KC, NT], F32)
            ht = norm_phase(T, l, ja, jb, x_tmp)
            if cfg.DEBUG and T.kind == "p" and T.t == 0 and l == 0 and ja == 1:
                wait("pool", ht)
                wait("pool", dtick("dbg", POOL.dma_start(out=dbg_h1, in_=h_bf[:].rearrange("p c t -> p (c t)"))))
            k.barrier()
        k.es_cur = es
        return ht

    def ffn_phase(T, l, which, h_tok):
        NT = T.NT
        jg = 2 if which == 1 else 8
        with contextlib.ExitStack() as pes:
            k.es_cur = pes
            hid = sb("hid", [128, FC, NT], BF16)
            stmp = sb("stmp", [128, 2, NT], F32)
            st_free = [None, None]
            hid_tok = None
            u = 0
            for g in range(11):
                w = k.wnext("gu%d" % which)
                for j in range(2):
                    jj = 2 * g + j
                    bA = k.bank_next(); bB = k.bank_next()
                    wait("pe", bA.free); wait("pe", bB.free); wait("pe", h_tok)
                    for kc in range(KC):
                        PE.matmul(bA.ap[:, 0:NT], lhsT=w[:, kc, 128 * j:128 * j + 128], rhs=h_bf[:, kc, 0:NT], start=(kc == 0), stop=(kc == KC - 1))
                    for kc in range(KC):
                        ins = PE.matmul(bB.ap[:, 0:NT], lhsT=w[:, kc, 256 + 128 * j:256 + 128 * j + 128], rhs=h_bf[:, kc, 0:NT], start=(kc == 0), stop=(kc == KC - 1))
                    if j == 1:
                        bA.ready = bB.ready = k.wdone(ins)
                    else:
                        bA.ready = bB.ready = tick("pe", ins)
                    s = u % 2
                    wait("act", bA.ready); wait("act", st_free[s])
                    a_t = tick("act", ACT.activation(out=stmp[:, s, 0:NT], in_=bA.ap[:, 0:NT], func=AF.Silu))
                    bA.free = a_t
                    wait("dve", a_t); wait("dve", bB.ready)
                    d_t = tick("dve", DVE.tensor_tensor(out=hid[:, jj, 0:NT], in0=stmp[:, s, 0:NT], in1=bB.ap[:, 0:NT], op=ALU.mult))
                    bB.free = d_t; st_free[s] = d_t; hid_tok = d_t
                    u += 1
            if cfg.DEBUG and T.kind == "p" and T.t == 0 and l == 0 and which == 1:
                wait("pool", hid_tok)
                wait("pool", dtick("dbg", POOL.dma_start(out=dbg_hid, in_=hid[:].rearrange("p c t -> p (c t)"))))
            for m in range(8):
                w = k.wnext("d%d" % which)
                bk = k.bank_next()
                wait("pe", bk.free); wait("pe", hid_tok)
                for kc in range(FC):
                    ins = PE.matmul(bk.ap[:, 0:NT], lhsT=w[:, kc, :], rhs=hid[:, kc, 0:NT], start=(kc == 0), stop=(kc == FC - 1))
                bk.ready = k.wdone(ins)
                wait("dve", bk.ready)
                for (c0, c1, sq) in T.segs:
                    ins = DVE.scalar_tensor_tensor(out=x_fm[:, m, c0:c1], in0=bk.ap[:, c0:c1], scalar=MOD[:, l, jg, m, sq:sq + 1],
                                                   in1=x_fm[:, m, c0:c1], op0=ALU.mult, op1=ALU.add)
                bk.free = tick("dve", ins)
            k.barrier()
        k.es_cur = es

    def load_x(T):
        NT, nblk, bp = T.NT, T.nblk, T.bp
        with contextlib.ExitStack() as pes:
            k.es_cur = pes
            xin = sb("xin", [128, nblk, D], F32)
            if T.kind == "p":
                src = xp[T.t * TT:(T.t + 1) * TT, :].rearrange("(b p) f -> p b f", p=128)
            else:
                src = xs.rearrange("(b p) f -> p b f", p=bp)
            tok = dtick("xld", POOL.dma_start(out=xin[0:bp], in_=src))
            wait("pe", tok)
            for c in range(KC):
                bk = k.bank_next()
                wait("pe", bk.free)
                for b in range(nblk):
                    ins = PE.transpose(out=bk.ap[:, b * bp:(b + 1) * bp], in_=xin[0:bp, b, 128 * c:128 * c + 128], identity=identF[0:bp, 0:bp])
                bk.ready = tick("pe", ins)
                evac_copy(x_fm[:, c, 0:NT], bk.ap[:, 0:NT], bk)
            if cfg.DEBUG and T.kind == "p" and T.t == 0:
                wait("pool", ("dve", k.cnt["dve"])); wait("pool", ("act", k.cnt["act"]))
                wait("pool", dtick("dbg", POOL.dma_start(out=dbg_x, in_=x_fm[:].rearrange("p c t -> p (c t)"))))
            k.barrier()
        k.es_cur = es

    kvst_tok = {}

    def mixing(T, l, h_tok):
        NT, nblk, bp = T.NT, T.nblk, T.bp
        isp = T.kind == "p"
        kout = kp if isp else ks
        vout = vp if isp else vs
        r0 = T.t * TT if isp else 0
        with contextlib.ExitStack() as pes:
            k.es_cur = pes
            qT = sb("qT", [128, NH, NT], BF16)
            kT = sb("kT", [128, NH, NT], BF16)
            v_tm = sb("v_tm", [128, NH, nblk, 128], BF16)
            stage = sb("stage", [128, 3, 512], F32)
            Pb = sb("Pb", [128, 3, 2, NT], BF16)
            fin = sb("fin", [128, 5, NT], F32)
            finc = sb("finc", [128, 4, NT], F32)
            sqb = sb("sqb", [128, NT], BF16)
            rsa = sb("rsa", [128, NT], F32)
            st_tok = [None, None, None]
            st_n = [0]
            last_store = []

            def store_rows(bank, dst_ap, also=None):
                s = st_n[0] % 3
                st_n[0] += 1
                wait("act", bank.ready); wait("act", st_tok[s])
                t1 = tick("act", ACT.copy(out=stage[0:bp, s, :], in_=bank.ap[0:bp, 0:512]))
                if also is not None:
                    wait("dve", bank.ready)
                    wait("dve", t1)
                    bank.free = tick("dve", DVE.tensor_copy(out=also, in_=bank.ap[0:bp, 0:512].rearrange("p (h e) -> p h e", e=128)))
                    ret = bank.free
                else:
                    bank.free = t1
                    ret = None
                wait("pool", t1)
                st_tok[s] = dtick("st%d" % s, POOL.dma_start(out=dst_ap, in_=stage[0:bp, s, :]))
                last_store.append(st_tok[s])
                return ret

            kt_tok = None
            vt_tok = None
            q_tok = None
            for g in range(4):
                w = k.wnext("qk")
                ins = None
                for j in range(4):
                    bk = k.bank_next()
                    wait("pe", bk.free); wait("pe", h_tok)
                    for kc in range(KC):
                        ins = PE.matmul(bk.ap[:, 0:NT], lhsT=w[:, kc, 128 * j:128 * j + 128], rhs=h_bf[:, kc, 0:NT], start=(kc == 0), stop=(kc == KC - 1))
                    bk.ready = tick("pe", ins)
                    if g < 2:
                        q_tok = evac_copy(qT[:, 4 * g + j, 0:NT], bk.ap[:, 0:NT], bk, scale=0.125)
                    else:
                        wait("dve", bk.ready)
                        kt_tok = bk.free = tick("dve", DVE.tensor_copy(out=kT[:, 4 * (g - 2) + j, 0:NT], in_=bk.ap[:, 0:NT]))
                if g >= 2:
                    for b in range(nblk):
                        bk = k.bank_next()
                        wait("pe", bk.free)
                        for kc in range(KC):
                            ins = PE.matmul(bk.ap[0:bp, 0:512], lhsT=h_bf[:, kc, b * bp:(b + 1) * bp], rhs=w[:, kc, :], start=(kc == 0), stop=(kc == KC - 1))
                        bk.ready = tick("pe", ins)
                        store_rows(bk, kout[l, r0 + b * bp:r0 + (b + 1) * bp, 512 * (g - 2):512 * (g - 2) + 512])
                k.wfree_tok[k.w_cur] = bk.ready
            for g in range(2):
                w = k.wnext("v")
                for b in range(nblk):
                    bk = k.bank_next()
                    wait("pe", bk.free); wait("pe", h_tok)
                    for kc in range(KC):
                        ins = PE.matmul(bk.ap[0:bp, 0:512], lhsT=h_bf[:, kc, b * bp:(b + 1) * bp], rhs=w[:, kc, :], start=(kc == 0), stop=(kc == KC - 1))
                    bk.ready = tick("pe", ins)
                    vt_tok = store_rows(bk, vout[l, r0 + b * bp:r0 + (b + 1) * bp, 512 * g:512 * g + 512], also=v_tm[0:bp, 4 * g:4 * g + 4, b, :])
                k.wfree_tok[k.w_cur] = bk.ready
            if isp and T.t < NPT - 1:
                wait("pool", kt_tok); wait("pool", vt_tok)
                dtick("kvst", POOL.dma_start(out=kscr[l, T.t].rearrange("p (h t) -> p h t", h=NH), in_=kT[:, :, :]))
                kvst_tok[(l, T.t)] = dtick("kvst", POOL.dma_start(out=vscr[l, T.t], in_=v_tm[:].rearrange("p h b e -> p (h b e)")))
                last_store.append(kvst_tok[(l, T.t)])

            O = [k.banks[0], k.banks[1]]
            L = [k.banks[2], k.banks[3]]
            Sp = [(k.banks[4], k.banks[5]), (k.banks[6], k.banks[7])]
            if isp:
                kring = sb("kring", [128, 4, TT], BF16)
                vring = sb("vring", [128, 4, TT], BF16)
            ring_free = [None] * 4
            if not isp:
                NBK = PAST // 128
                kc_in = sb("kc_in", [128, 2, NBK, 128], F32)
                vc_in = sb("vc_in", [128, 2, NBK, 128], F32)
                kcT = sb("kcT", [128, 2, PAST], BF16)
                vcb = sb("vcb", [128, 2, NBK, 128], BF16)
            units = []
            if isp:
                for h in range(NH):
                    units.append(dict(h=h, q0=0, N=NT, seq=None))
            else:
                for i in range(NS):
                    for h in range(NH):
                        units.append(dict(h=h, q0=i * TS, N=TS, seq=i))
            loads = []
            if isp:
                for ui, u in enumerate(units):
                    for s in range(T.t):
                        loads.append((ui, s))
            load_tok = {}
            issued = [0]

            def issue_upto(n):
                while issued[0] < min(n, len(loads)):
                    i = issued[0]
                    ui, s = loads[i]
                    slot = i % 4
                    wait("pool", ring_free[slot]); wait("pool", kvst_tok[(l, s)])
                    h = units[ui]["h"]
                    dtick("kv%d" % slot, POOL.dma_start(out=kring[:, slot, :], in_=kscr[l, s][:, h * TT:(h + 1) * TT]))
                    load_tok[i] = dtick("kv%d" % slot, POOL.dma_start(out=vring[:, slot, :], in_=vscr[l, s][:, h * TT:(h + 1) * TT]))
                    issued[0] += 1

            cin_tok = {}
            cin_free = [None, None]
            cprep_tok = {}

            def cache_load(ui):
                u = units[ui]
                s = ui % 2
                wait("pool", cin_free[s])
                h = u["h"]; i = u["seq"]
                dtick("kv%d" % s, POOL.dma_start(out=kc_in[:, s], in_=ck[l, i, :, 128 * h:128 * h + 128].rearrange("(b p) d -> p b d", p=128)))
                cin_tok[ui] = dtick("kv%d" % s, POOL.dma_start(out=vc_in[:, s], in_=cv[l, i, :, 128 * h:128 * h + 128].rearrange("(b p) d -> p b d", p=128)))

            cprep_free = [None, None]

            def cache_prep(ui):
                s = ui % 2
                wait("pe", cin_tok[ui]); wait("dve", cin_tok[ui]); wait("act", cin_tok[ui])
                wait("dve", cprep_free[s]); wait("act", cprep_free[s])
                t = None
                for b4 in range(NBK // 4):
                    bk = k.bank_next()
                    wait("pe", bk.free)
                    for j in range(4):
                        ins = PE.transpose(out=bk.ap[:, 128 * j:128 * j + 128], in_=kc_in[:, s, 4 * b4 + j, :], identity=identF[:])
                    bk.ready = tick("pe", ins)
                    t = evac_copy(kcT[:, s, 512 * b4:512 * b4 + 512], bk.ap[:, :], bk)
                wait("dve", t)
                t2 = tick("dve", DVE.tensor_copy(out=vcb[:, s], in_=vc_in[:, s]))
                wait("act", t2)
                t3 = tick("act", ACT.copy(out=k.scrA[0:1, 2:3], in_=k.scrA[0:1, 1:2]))
                cin_free[s] = t3
                cprep_tok[ui] = t3

            pending = []
            pn = [0]
            sn = [0]
            p_free = [None, None, None]
            wait("pe", q_tok); wait("pe", kt_tok); wait("pe", vt_tok)
            if not isp:
                cache_load(0)
            for ui, u in enumerate(units):
                h, q0, N = u["h"], u["q0"], u["N"]
                blocks = []
                if isp:
                    base = sum(1 for (a, _) in loads if a < ui)
                    for s in range(T.t):
                        li = base + s
                        for b in range(4):
                            blocks.append(dict(kt=kring[:, li % 4, 128 * b:128 * b + 128], v=vring[:, li % 4, 128 * b:128 * b + 128], nk=128, qs=0, zero=False,
                                               li=li, last=(b == 3)))
                    for b in range(nblk):
                        blocks.append(dict(kt=kT[:, h, 128 * b:128 * b + 128], v=v_tm[:, h, b, :], nk=128, qs=128 * b, zero=True, li=None, last=False))
                else:
                    s2 = ui % 2
                    if ui + 1 < len(units):
                        cache_load(ui + 1)
                    cache_prep(ui)
                    wait("pe", cprep_tok[ui])
                    for b in range(NBK):
                        blocks.append(dict(kt=kcT[:, s2, 128 * b:128 * b + 128], v=vcb[:, s2, b, :], nk=128, qs=0, zero=False, li=None, last=False))
                    i = u["seq"]
                    blocks.append(dict(kt=kT[:, h, i * TS:(i + 1) * TS], v=v_tm[0:TS, h, i, :], nk=TS, qs=0, zero=False, li=None, last=False))
                nb = len(blocks)

                def emit_S(j):
                    bl = blocks[j]
                    pr = Sp[sn[0] % 2]
                    bl["pr"] = pr
                    sn[0] += 1
                    if bl["li"] is not None:
                        issue_upto(bl["li"] + 3)
                        wait("pe", load_tok[bl["li"]])
                    wait("pe", pr[0].free); wait("pe", pr[1].free)
                    nk, qs = bl["nk"], bl["qs"]
                    PE.matmul(pr[0].ap[0:nk, qs:N], lhsT=bl["kt"][0:64, :], rhs=qT[0:64, h, q0 + qs:q0 + N], start=True, stop=True)
                    ins = PE.matmul(pr[1].ap[0:nk, qs:N], lhsT=bl["kt"][64:128, :], rhs=qT[64:128, h, q0 + qs:q0 + N], start=True, stop=True)
                    pr[0].ready = pr[1].ready = tick("pe", ins)

                wait("pe", O[0].free); wait("pe", O[1].free); wait("pe", L[0].free); wait("pe", L[1].free)
                emit_S(0)
                for j in range(nb):
                    bl = blocks[j]
                    nk, qs = bl["nk"], bl["qs"]
                    pr = bl["pr"]
                    ps = pn[0] % 3
                    pn[0] += 1
                    wait("act", pr[0].ready); wait("act", p_free[ps])
                    e_t = tick("act", ACT.activation(out=Pb[0:nk, ps, :, qs:N], in_=pr[0].pair[0:nk, :, qs:N], func=AF.Exp))
                    if bl["zero"]:
                        e_t = tick("act", ACT.mul(out=Pb[64:128, ps, :, qs:qs + 64], in_=Pb[64:128, ps, :, qs:qs + 64], mul=0.0))
                    pr[0].free = pr[1].free = e_t
                    if j + 1 < nb:
                        emit_S(j + 1)
                    wait("pe", e_t)
                    for m in range(2):
                        PE.matmul(O[m].ap[:, qs:N], lhsT=bl["v"], rhs=Pb[0:nk, ps, m, qs:N], start=(j == 0), stop=(j == nb - 1))
                    for m in range(2):
                        ins = PE.matmul(L[m].ap[:, qs:N], lhsT=onesB[0:nk, :], rhs=Pb[0:nk, ps, m, qs:N], start=(j == 0), stop=(j == nb - 1))
                    pv_t = tick("pe", ins)
                    p_free[ps] = pv_t
                    if bl["last"]:
                        ring_free[bl["li"] % 4] = pv_t
                    if j == 1 and pending:
                        pending.pop(0)()
                while pending:
                    pending.pop(0)()
                wait("act", pv_t); wait("dve", pv_t)
                ACT.activation(out=finc[:, 2, 0:N], in_=L[0].ap[:, 0:N], func=AF.Ln)
                a_t = tick("act", ACT.activation(out=finc[:, 3, 0:N], in_=L[1].ap[:, 0:N], func=AF.Ln))
                a2_t = tick("act", ACT.activation(out=finc[:, 2:4, 0:N], in_=finc[:, 2:4, 0:N], func=AF.Exp, scale=-1.0))
                DVE.tensor_copy(out=finc[:, 0, 0:N], in_=O[0].ap[:, 0:N])
                d_t = tick("dve", DVE.tensor_copy(out=finc[:, 1, 0:N], in_=O[1].ap[:, 0:N]))
                L[0].free = L[1].free = a_t
                O[0].free = O[1].free = d_t
                wait("dve", a2_t)
                DVE.tensor_tensor(out=fin[:, 2, 0:N], in0=finc[:, 0, 0:N], in1=finc[:, 2, 0:N], op=ALU.mult)
                DVE.tensor_tensor(out=fin[:, 3, 0:N], in0=finc[:, 1, 0:N], in1=finc[:, 3, 0:N], op=ALU.mult)
                DVE.scalar_tensor_tensor(out=fin[:, 4, 0:N], in0=fin[:, 3, 0:N], scalar=lamt[:, l:l + 1], in1=fin[:, 2, 0:N], op0=ALU.mult, op1=ALU.add)
                sq_t = tick("dve", DVE.tensor_tensor(out=sqb[:, 0:N], in0=fin[:, 4, 0:N], in1=fin[:, 4, 0:N], op=ALU.mult))

                def fin2(h=h, q0=q0, N=N, sq_t=sq_t):
                    pr = Sp[sn[0] % 2]
                    sn[0] += 1
                    bk = pr[0]
                    wait("pe", pr[0].free); wait("pe", pr[1].free); wait("pe", sq_t)
                    ins = PE.matmul(bk.ap[:, 0:N], lhsT=onesB[:], rhs=sqb[:, 0:N], start=True, stop=True)
                    pr[0].ready = pr[1].ready = tick("pe", ins)
                    t = rstd_act(bk, rsa[:, 0:N], bk.ap[:, 0:N], 128.0 * EPS)
                    pr[0].free = pr[1].free = t
                    wait("dve", t)
                    return tick("dve", DVE.scalar_tensor_tensor(out=o_att[:, h, q0:q0 + N], in0=fin[:, 4, 0:N], scalar=gatt[:, l:l + 1], in1=rsa[:, 0:N], op0=ALU.mult, op1=ALU.mult))
                pending.append(fin2)
            oa_tok = None
            while pending:
                oa_tok = pending.pop(0)()
            if cfg.DEBUG and isp and T.t == 0 and l == 0:
                wait("pool", oa_tok)
                last_store.append(dtick("dbg", POOL.dma_start(out=dbg_att, in_=o_att[:])))
            k.barrier(pool_tokens=last_store)
        k.es_cur = es

        with contextlib.ExitStack() as pes:
            k.es_cur = pes
            if isp:
                Lc = 64; h2 = 32
                nch = NT // 64
                M0 = M0p
            else:
                Lc = TS; h2 = TS
                nch = NS
                M0 = M0s
            two = h2 < Lc
            hq = sb("hq", [128, NH, NT], F32)
            sigf = sb("sigf", [128, NH, NT], F32)
            hgt = sb("hgt", [128, NH, NT], BF16)
            hi_tm = sb("hi_tm", [64, NH, nch, 128], BF16)
            fw = sb("fw", [128, 10, NT], F32)
            qk4 = sb("qk4", [128, 2, 5, NT], BF16)
            kd_tm = sb("kd_tm", [64, 2, nch, 128], BF16)
            Am = sb("Am", [64, 2, nch, 64], BF16)
            sm = sb("sm", [128, 4, TT // 32], F32)
            osq = sb("osq", [128, NT], BF16)
            ot = sb("ot", [128, NT], F32)
            rsh = sb("rsh", [128, NT], F32)
            if not isp:
                Ssm = sb("Ssm", [128, NS, NH, 128], F32)
                Sbs = sb("Sbs", [128, NS, NH, 128], BF16)
            if two:
                DVE.memset(Am[h2:Lc, :, :, 0:h2], 0.0)
            hq_tok = sig_tok = hg_tok = hi_tok = None
            for i in range(8):
                w = k.wnext("hg")
                if i in (4, 5):
                    for b in range(nch):
                        bk = k.bank_next()
                        wait("pe", bk.free); wait("pe", h_tok)
                        for kc in range(KC):
                            ins = PE.matmul(bk.ap[0:Lc, 0:512], lhsT=h_bf[:, kc, b * Lc:(b + 1) * Lc], rhs=w[:, kc, :], start=(kc == 0), stop=(kc == KC - 1))
                        bk.ready = tick("pe", ins)
                        hi_tok = evac_copy(hi_tm[0:Lc, 4 * (i - 4):4 * (i - 4) + 4, b, :], bk.ap[0:Lc, 0:512].rearrange("p (h e) -> p h e", e=128), bk)
                else:
                    for j in range(4):
                        bk = k.bank_next()
                        wait("pe", bk.free); wait("pe", h_tok)
                        for kc in range(KC):
                            ins = PE.matmul(bk.ap[:, 0:NT], lhsT=w[:, kc, 128 * j:128 * j + 128], rhs=h_bf[:, kc, 0:NT], start=(kc == 0), stop=(kc == KC - 1))
                        bk.ready = tick("pe", ins)
                        wait("act", bk.ready)
                        if i < 2:
                            hq_tok = bk.free = tick("act", ACT.activation(out=hq[:, 4 * i + j, 0:NT], in_=bk.ap[:, 0:NT], func=AF.Silu))
                        elif i < 4:
                            sig_tok = bk.free = tick("act", ACT.activation(out=sigf[:, 4 * (i - 2) + j, 0:NT], in_=bk.ap[:, 0:NT], func=AF.Sigmoid))
                        else:
                            hg_tok = bk.free = tick("act", ACT.activation(out=hgt[:, 4 * (i - 6) + j, 0:NT], in_=bk.ap[:, 0:NT], func=AF.Silu))
                k.wfree_tok[k.w_cur] = bk.ready
            hi_toks = [("act", k.cnt["act"]), ("dve", k.cnt["dve"])]
            if isp:
                sb_tok = tick("dve", DVE.tensor_copy(out=Sb[:], in_=S_all[:, l]))
            else:
                s_ld = dtick("sld", POOL.dma_start(out=Ssm[:].rearrange("p i h v -> p (i h) v"), in_=st[l].rearrange("i h kk v -> kk (i h) v")))
                wait("dve", s_ld)
                sb_tok = tick("dve", DVE.tensor_copy(out=Sbs[:], in_=Ssm[:]))
            wait("dve", sig_tok); wait("dve", hq_tok)
            for t_ in hi_toks:
                wait("pe", t_)
            pe_use = [None, None]
            last_o = None
            act_rd = [None]
            dve_rd = [None]
            dve_use = [None, None]

            def c3(ap):
                return ap.rearrange("p (c t) -> p c t", t=Lc)
            def prep(h):
                    hs = h % 2
                    fA, fK, fB, fC, fD, fE, gA, gB, gC, gD = [fw[:, i, 0:NT] for i in range(10)]
                    qe, qa2, ka0, ka1, kd = [qk4[:, hs, i, 0:NT] for i in range(5)]
                    PL = POOL
                    wait("pool", sig_tok); wait("pool", hq_tok); wait("pool", act_rd[0])
                    t0_ = tick("pool", PL.tensor_scalar(out=fA, in0=sigf[:, h, 0:NT], scalar1=omlt[:, l, h:h + 1], scalar2=lbt[:, l, h:h + 1], op0=ALU.mult, op1=ALU.add))
                    wait("dve", t0_)
                    t1 = tick("dve", DVE.tensor_scalar(out=fA, in0=fA, scalar1=TINY, scalar2=None, op0=ALU.max))
                    wait("act", t1)
                    t2 = tick("act", ACT.activation(out=fB, in_=fA, func=AF.Ln))
                    wait("pool", dve_rd[0])
                    PL.tensor_scalar(out=fK, in0=sigf[:, h, 0:NT], scalar1=nomlt[:, l, h:h + 1], scalar2=omlt[:, l, h:h + 1], op0=ALU.mult, op1=ALU.add)
                    wait("pool", t2)
                    wait("dve", t2)
                    DVE.tensor_tensor_scan(out=fC, data0=M0[:, 0:NT], data1=fB, initial=0.0, op0=ALU.mult, op1=ALU.add)
                    sc_t = tick("dve", k.last_ins["dve"])
                    wait("pool", sc_t)
                    b3 = c3(fC)
                    if two:
                        PL.tensor_tensor(out=c3(fD), in0=b3, in1=b3[:, :, h2 - 1:h2].to_broadcast([128, nch, Lc]), op=ALU.subtract)
                    PL.tensor_tensor(out=c3(fE), in0=b3[:, :, Lc - 1:Lc].to_broadcast([128, nch, Lc]), in1=b3, op=ALU.subtract)
                    t3 = tick("pool", PL.tensor_copy(out=sm[:, 0, 0:nch], in_=b3[:, :, Lc - 1]))
                    wait("act", t3)
                    ACT.activation(out=gC, in_=fC, func=AF.Exp)
                    ACT.activation(out=c3(gA)[:, :, 0:h2], in_=c3(fC)[:, :, 0:h2], func=AF.Exp, scale=-1.0)
                    if two:
                        ACT.activation(out=c3(gA)[:, :, h2:Lc], in_=c3(fD)[:, :, h2:Lc], func=AF.Exp)
                        ACT.activation(out=gB, in_=fD, func=AF.Exp, scale=-1.0)
                    ACT.activation(out=gD, in_=fE, func=AF.Exp)
                    t4 = tick("act", ACT.activation(out=sm[:, 1, 0:nch], in_=sm[:, 0, 0:nch], func=AF.Exp))
                    act_rd[0] = t4
                    wait("pool", t4); wait("pool", pe_use[hs]); wait("pool", dve_use[hs])
                    PL.tensor_tensor(out=qe, in0=hq[:, h, 0:NT], in1=gC, op=ALU.mult)
                    PL.tensor_tensor(out=c3(ka0)[:, :, 0:h2], in0=c3(fK)[:, :, 0:h2], in1=c3(gA)[:, :, 0:h2], op=ALU.mult)
                    if two:
                        PL.tensor_tensor(out=c3(qa2)[:, :, h2:Lc], in0=c3(hq[:, h, 0:NT])[:, :, h2:Lc], in1=c3(gA)[:, :, h2:Lc], op=ALU.mult)
                        PL.tensor_tensor(out=ka1, in0=fK, in1=gB, op=ALU.mult)
                    PL.tensor_copy(out=sm[:, 2 + hs, 0:nch], in_=sm[:, 1, 0:nch])
                    t5 = tick("pool", PL.tensor_tensor(out=kd, in0=fK, in1=gD, op=ALU.mult))
                    wait("dve", t5)
                    return t5

            HD = {}
            osq_free = [None]
            rsh_free = [None]

            def stageA(h, t5):
                hs = h % 2
                qe, qa2, ka0, ka1, kd = [qk4[:, hs, i, 0:NT] for i in range(5)]
                bA = k.banks[0]; bT = k.banks[1]
                wait("pe", t5); wait("pe", bA.free); wait("pe", bT.free)
                for ci in range(nch):
                    cs = ci * Lc
                    ins = PE.matmul(bA.ap[0:h2, cs:cs + h2], lhsT=ka0[:, cs:cs + h2], rhs=qe[:, cs:cs + h2], start=True, stop=True)
                    if two:
                        ins = PE.matmul(bA.ap[0:Lc, cs + h2:cs + Lc], lhsT=ka1[:, cs:cs + Lc], rhs=qa2[:, cs + h2:cs + Lc], start=True, stop=True)
                bA.ready = tick("pe", ins)
                bTv = bT.ap.bitcast(BF16)
                for ci in range(nch):
                    ins = PE.transpose(out=bTv[0:Lc, 128 * ci:128 * ci + 128], in_=kd[:, ci * Lc:(ci + 1) * Lc], identity=identB[:])
                bT.ready = tick("pe", ins)
                wait("dve", bA.ready); wait("dve", pe_use[hs])
                bA3 = bA.ap[:, 0:nch * Lc].rearrange("p (c t) -> p c t", t=Lc)
                am_t = tick("dve", DVE.tensor_tensor(out=Am[0:h2, hs, :, 0:Lc], in0=bA3[0:h2], in1=triP[0:h2, 0:Lc].unsqueeze(1).to_broadcast([h2, nch, Lc]), op=ALU.mult))
                if two:
                    am_t = tick("dve", DVE.tensor_tensor(out=Am[h2:Lc, hs, :, h2:Lc], in0=bA3[h2:Lc, :, h2:Lc],
                                                         in1=triP[h2:Lc, h2:Lc].unsqueeze(1).to_broadcast([Lc - h2, nch, Lc - h2]), op=ALU.mult))
                bA.free = am_t
                wait("act", bT.ready); wait("act", pe_use[hs])
                kt_t = bT.free = tick("act", ACT.copy(out=kd_tm[0:Lc, hs], in_=bTv[0:Lc, 0:nch * 128].rearrange("p (b e) -> p b e", e=128)))
                HD[h] = dict(am_t=am_t, kt_t=kt_t, t5=t5)

            def chunk(h, ci):
                hs = h % 2
                qe = qk4[:, hs, 0, 0:NT]
                ebl = sm[:, 2 + hs, :]
                bO = k.banks[2 + (h % 2)]
                if ci == 0:
                    wait("pe", bO.free); wait("pe", HD[h]["am_t"]); wait("pe", HD[h]["kt_t"]); wait("dve", HD[h]["t5"])
                cs = ci * Lc
                if isp:
                    Sst = S_all[:, l, h, :]; Sbh = Sb[:, h, :]
                else:
                    Sst = Ssm[:, ci, h, :]; Sbh = Sbs[:, ci, h, :]
                wait("pe", nonloc["sb_tok"])
                PE.matmul(bO.ap[:, cs:cs + Lc], lhsT=Sbh, rhs=qe[:, cs:cs + Lc], start=True, stop=False)
                PE.matmul(bO.ap[:, cs:cs + Lc], lhsT=hi_tm[0:Lc, h, ci, :], rhs=Am[0:Lc, hs, ci, 0:Lc], start=False, stop=True)
                bS = k.banks[4 + (ci % 2)]
                wait("pe", bS.free)
                ins = PE.matmul(bS.ap[:, 0:128], lhsT=kd_tm[0:Lc, hs, ci, :], rhs=hi_tm[0:Lc, h, ci, :], start=True, stop=True)
                bS.ready = tick("pe", ins)
                wait("dve", bS.ready)
                bS.free = tick("dve", DVE.scalar_tensor_tensor(out=Sst, in0=Sst, scalar=ebl[:, ci:ci + 1], in1=bS.ap[:, 0:128], op0=ALU.mult, op1=ALU.add))
                nonloc["sb_tok"] = tick("dve", DVE.tensor_copy(out=Sbh, in_=Sst))
                if ci == nch - 1:
                    bO.ready = bS.ready
                    pe_use[hs] = bS.ready
                    dve_use[hs] = nonloc["sb_tok"]

            def fin_a(h):
                bO = k.banks[2 + (h % 2)]
                wait("act", bO.ready); wait("act", osq_free[0])
                HD[h]["q_t"] = tick("act", ACT.activation(out=osq[:, 0:NT], in_=bO.ap[:, 0:NT], func=AF.Square))

            def fin_b(h):
                bO = k.banks[2 + (h % 2)]
                bq = k.banks[6 + (h % 2)]
                wait("pe", bq.free); wait("pe", HD[h]["q_t"])
                bq.ready = tick("pe", PE.matmul(bq.ap[:, 0:NT], lhsT=onesB[:], rhs=osq[:, 0:NT], start=True, stop=True))
                osq_free[0] = bq.ready
                wait("act", rsh_free[0])
                wait("dve", rstd_act(bq, rsh[:, 0:NT], bq.ap[:, 0:NT], 128.0 * EPS)); wait("dve", hg_tok)
                bO.free = tick("dve", DVE.scalar_tensor_tensor(out=ot[:, 0:NT], in0=bO.ap[:, 0:NT], scalar=ghg[:, l:l + 1], in1=rsh[:, 0:NT], op0=ALU.mult, op1=ALU.mult))
                rsh_free[0] = bO.free
                nonloc["last_o"] = tick("dve", DVE.tensor_tensor(out=o_h[:, h, 0:NT], in0=ot[:, 0:NT], in1=hgt[:, h, 0:NT], op=ALU.mult))
                if cfg.DEBUG and isp and T.t == 0 and l == 0:
                    wait("pool", nonloc["last_o"])
                    dd_ = dtick("dbg", POOL.dma_start(out=dbg_h[:, h, :], in_=ot[:, :]))
                    wait("dve", dd_)

            nonloc = {"sb_tok": sb_tok, "last_o": None}
            pend = []
            t5_0 = prep(0)
            stageA(0, t5_0)
            for h in range(NH):
                t5n = prep(h + 1) if h + 1 < NH else None
                for ci in range(nch):
                    chunk(h, ci)
                    if ci == min(2, nch - 1) and pend:
                        pend.pop(0)()
                if h + 1 < NH:
                    stageA(h + 1, t5n)
                fin_a(h)
                pend.append(lambda h=h: fin_b(h))
            while pend:
                pend.pop(0)()
            sb_tok = nonloc["sb_tok"]; last_o = nonloc["last_o"]
            ptoks = []
            if isp and T.t == NPT - 1:
                wait("pool", sb_tok)
                ptoks.append(dtick("out", POOL.dma_start(out=sp_o[l].rearrange("h kk v -> kk h v"), in_=S_all[:, l])))
            if not isp:
                wait("pool", sb_tok)
                ptoks.append(dtick("out", POOL.dma_start(out=ss_o[l].rearrange("i h kk v -> kk (i h) v"), in_=Ssm[:].rearrange("p i h v -> p (i h) v"))))
            k.barrier(pool_tokens=ptoks)
        k.es_cur = es

        with contextlib.ExitStack() as pes:
            k.es_cur = pes
            mt = sb("mt", [128, 2, 4, NT], F32)
            mt_free = [None, None]
            m_tok = None
            for m in range(8):
                w = k.wnext("mrg")
                bks = [k.bank_next() for _ in range(4)]
                srcs = [h_bf, h_bf, o_att, o_h]
                for bi in range(4):
                    wait("pe", bks[bi].free)
                wait("pe", h_tok); wait("pe", oa_tok); wait("pe", last_o)
                for bi in range(4):
                    for kc in range(KC):
                        ins = PE.matmul(bks[bi].ap[:, 0:NT], lhsT=w[:, kc, 128 * bi:128 * bi + 128], rhs=srcs[bi][:, kc, 0:NT], start=(kc == 0), stop=(kc == KC - 1))
                r_t = k.wdone(ins)
                for bi in range(4):
                    bks[bi].ready = r_t
                s = m % 2
                wait("act", r_t); wait("act", mt_free[s])
                ACT.activation(out=mt[:, s, 0, 0:NT], in_=bks[0].ap[:, 0:NT], func=AF.Sigmoid)
                a_t = tick("act", ACT.activation(out=mt[:, s, 1, 0:NT], in_=bks[1].ap[:, 0:NT], func=AF.Sigmoid))
                bks[0].free = bks[1].free = a_t
                wait("dve", a_t); wait("dve", r_t)
                DVE.tensor_tensor(out=mt[:, s, 2, 0:NT], in0=mt[:, s, 0, 0:NT], in1=bks[2].ap[:, 0:NT], op=ALU.mult)
                d_t = tick("dve", DVE.tensor_tensor(out=mt[:, s, 3, 0:NT], in0=mt[:, s, 1, 0:NT], in1=bks[3].ap[:, 0:NT], op=ALU.mult))
                bks[2].free = bks[3].free = d_t
                m_tok = tick("dve", DVE.tensor_tensor(out=merged[:, m, 0:NT], in0=mt[:, s, 2, 0:NT], in1=mt[:, s, 3, 0:NT], op=ALU.add))
                mt_free[s] = m_tok
            for g in range(2):
                w = k.wnext("wo")
                for j in range(4):
                    m = 4 * g + j
                    bk = k.bank_next()
                    wait("pe", bk.free); wait("pe", m_tok)
                    for kc in range(KC):
                        ins = PE.matmul(bk.ap[:, 0:NT], lhsT=w[:, kc, 128 * j:128 * j + 128], rhs=merged[:, kc, 0:NT], start=(kc == 0), stop=(kc == KC - 1))
                    bk.ready = tick("pe", ins)
                    wait("dve", bk.ready)
                    for (c0, c1, sq) in T.segs:
                        ins = DVE.scalar_tensor_tensor(out=x_fm[:, m, c0:c1], in0=bk.ap[:, c0:c1], scalar=MOD[:, l, 5, m, sq:sq + 1],
                                                       in1=x_fm[:, m, c0:c1], op0=ALU.mult, op1=ALU.add)
                    bk.free = tick("dve", ins)
                k.wfree_tok[k.w_cur] = bk.ready
            k.barrier()
        k.es_cur = es

    def final_out(T):
        NT, nblk, bp = T.NT, T.nblk, T.bp
        with contextlib.ExitStack() as pes:
            k.es_cur = pes
            x_tmp = sb("x_tmp", [128, KC, NT], F32)
            ystage = sb("ystage", [128, nblk, D], F32)
            yt = norm_phase(T, 0, None, None, x_tmp)
            wait("pe", yt)
            et = None
            for b in range(nblk):
                for cg in range(2):
                    bk = k.bank_next()
                    wait("pe", bk.free)
                    for j in range(4):
                        ins = PE.transpose(out=bk.ap[0:bp, 128 * j:128 * j + 128], in_=x_tmp[:, 4 * cg + j, b * bp:(b + 1) * bp], identity=identF[:])
                    bk.ready = tick("pe", ins)
                    et = evac_copy(ystage[0:bp, b, 512 * cg:512 * cg + 512], bk.ap[0:bp, 0:512], bk)
                    wait("pool", et)
            wait("pool", ("act", k.cnt["act"])) if k.cnt["act"] else None
            wait("pool", ("dve", k.cnt["dve"])) if k.cnt["dve"] else None
            if T.kind == "p":
                dst = yp[T.t * TT:(T.t + 1) * TT, :].rearrange("(b p) f -> p b f", p=128)
            else:
                dst = ys.rearrange("(b p) f -> p b f", p=bp)
            ot_ = dtick("out", POOL.dma_start(out=dst, in_=ystage[0:bp]))
            k.barrier(pool_tokens=[ot_])
        k.es_cur = es

    DVE.memset(S_all[:], 0.0)
    for T in tiles:
        load_x(T)
        for l in range(DEPTH):
            ht = norm_mod(T, l, 1, 0)
            ffn_phase(T, l, 1, ht)
            ht = norm_mod(T, l, 4, 3)
            mixing(T, l, ht)
            ht = norm_mod(T, l, 7, 6)
            ffn_phase(T, l, 2, ht)
        final_out(T)
    for s in ("out", "st0", "st1", "st2", "kvst") + (("dbg",) if cfg.DEBUG else ()):
        if k.cnt[s]:
            POOL.wait_ge(k.sem[s], k.cnt[s])
    assert k.w_used == len(k.wseq)
    es.close()
    return nc


_WNAMES = ["w_ada", "b_ada", "g_ffn1", "w_ffn1_gu", "w_ffn1_d", "g_mix", "w_in", "att_lambda", "g_att_sub",
           "hg_lb_logits", "g_hg_norm", "w_br_att", "w_br_hg", "w_out", "g_ffn2", "w_ffn2_gu", "w_ffn2_d", "g_final"]


def run(cfg, inputs, n_cores=8):
    nc = build(cfg)
    NS, TS = cfg.NS, cfg.TS
    f = lambda a: np.ascontiguousarray(np.asarray(a, dtype=np.float32))
    in_maps = []
    nb = inputs["x_prompt"].shape[0]
    for c in range(n_cores):
        sq = (c * nb) // n_cores
        s0 = c * NS
        m = {
            "xp": f(inputs["x_prompt"][sq]),
            "xs": f(inputs["x_sample"][s0:s0 + NS]).reshape(NS * TS, D),
            "ck": f(np.asarray(inputs["cache_k"])[:, s0:s0 + NS].reshape(cfg.DEPTH, NS, cfg.PAST, D)),
            "cv": f(np.asarray(inputs["cache_v"])[:, s0:s0 + NS].reshape(cfg.DEPTH, NS, cfg.PAST, D)),
            "st": f(np.asarray(inputs["state_hgrn"])[:, s0:s0 + NS]),
            "cc": f(np.concatenate([np.asarray(inputs["c_prompt"])[sq:sq + 1], np.asarray(inputs["c_sample"])[s0:s0 + NS]], 0)),
        }
        for n in _WNAMES:
            m[n] = f(inputs[n])
        in_maps.append(m)
    res = run_bass_kernel_spmd(nc, in_maps, core_ids=list(range(n_cores)))
    R = res.results
    run.last = R
    per = n_cores // nb
    DEPTH = cfg.DEPTH
    y_prompt = np.stack([R[per * b]["yp"] for b in range(nb)], 0)
    k_prompt = np.stack([R[per * b]["kp"] for b in range(nb)], 1).reshape(DEPTH, nb, cfg.SEQ, NH, 128)
    v_prompt = np.stack([R[per * b]["vp"] for b in range(nb)], 1).reshape(DEPTH, nb, cfg.SEQ, NH, 128)
    s_prompt = np.stack([R[per * b]["sp"] for b in range(nb)], 1)
    y_sample = np.concatenate([R[c]["ys"].reshape(NS, TS, D) for c in range(n_cores)], 0)
    k_sample = np.concatenate([R[c]["ks"].reshape(DEPTH, NS, TS, NH, 128) for c in range(n_cores)], 1)
    v_sample = np.concatenate([R[c]["vs"].reshape(DEPTH, NS, TS, NH, 128) for c in range(n_cores)], 1)
    s_sample = np.concatenate([R[c]["ss"] for c in range(n_cores)], 1)
    return tuple(np.ascontiguousarray(a, dtype=np.float32) for a in
                 (y_prompt, y_sample, k_prompt, v_prompt, s_prompt, k_sample, v_sample, s_sample))


def kernel(**inputs):
    cfg = Cfg()
    return run(cfg, inputs, 8)
```

```python
import contextlib
import math
import numpy as np
import concourse.bass as bass
import concourse.mybir as mybir
from concourse.bass_utils import run_bass_kernel_spmd

F32 = mybir.dt.float32
BF16 = mybir.dt.bfloat16
AF = mybir.ActivationFunctionType
ALU = mybir.AluOpType

D = 1024
DFF = 2816
NH = 8
KC = 8
FC = 22
DIN = 9216
EPS = 1e-6
TINY = 1e-30
NWG = 62
WSLOT = 4096
NBUF = 4


class Cfg:
    def __init__(self, SEQ=8192, DEPTH=4, PAST=2048, TS=32, NS=2, TT=512):
        self.SEQ, self.DEPTH, self.PAST, self.TS, self.NS, self.TT = SEQ, DEPTH, PAST, TS, NS, TT
        self.DEBUG = False


class Tile:
    pass


class B:
    def __init__(self, cfg):
        self.cfg = cfg
        self.nc = bass.Bass("TRN2", target_bir_lowering=False)
        self.es = contextlib.ExitStack()
        self.cnt = {}
        self.sem = {}
        self.waited = {}
        self.eng = {"pe": self.nc.tensor, "act": self.nc.scalar, "dve": self.nc.vector,
                    "pool": self.nc.gpsimd, "sp": self.nc.sync}
        for e in ("pe", "act", "dve", "pool"):
            self.newsem(e)
        self.bar_n = 0
        self.newsem("bar")
        self.last_tok = {}
        self.last_ins = {}
        self.skip_sync = {}

    def newsem(self, name):
        self.sem[name] = self.es.enter_context(self.nc.semaphore(name))
        self.cnt[name] = 0
        return name

    def tick(self, E, ins):
        if self.last_ins.get(E) is ins and self.last_tok.get(E) is not None:
            return self.last_tok[E]
        ins.then_inc(self.sem[E], 1)
        self.cnt[E] += 1
        tok = (E, self.cnt[E])
        if self.last_ins.get(E) is ins:
            self.last_tok[E] = tok
        return tok

    def pre_issue(self, E):
        li = self.last_ins.get(E)
        if li is None:
            return
        tok = self.tick(E, li)
        k = (E, E)
        if self.waited.get(k, 0) < tok[1]:
            self.eng[E].wait_ge(self.sem[E], tok[1])
            self.waited[k] = tok[1]

    def post_issue(self, E, ins):
        self.last_ins[E] = ins
        self.last_tok[E] = None

    def dtick(self, S, ins):
        ins.then_inc(self.sem[S], 16)
        self.cnt[S] += 16
        return (S, self.cnt[S])

    def wait(self, who, tok):
        if tok is None:
            return
        S, v = tok
        if S == who:
            return
        k = (who, S)
        if self.waited.get(k, 0) >= v:
            return
        self.eng[who].wait_ge(self.sem[S], v)
        self.waited[k] = v

    def sb(self, name, shape, dt):
        self.uid = getattr(self, "uid", 0) + 1
        return self.es_cur.enter_context(self.nc.sbuf_tensor("%s_%d" % (name, self.uid), shape, dt))

    def barrier(self, pool_tokens=()):
        nc = self.nc
        for t in pool_tokens:
            self.wait("pool", t)
        self.pre_issue("act")
        nc.scalar.copy(out=self.scrA[0:1, 0:1], in_=self.scrA[0:1, 1:2]).then_inc(self.sem["bar"], 1)
        self.pre_issue("dve")
        nc.vector.memset(self.scrV[0:1, 0:1], 0.0).then_inc(self.sem["bar"], 1)
        nc.gpsimd.memset(self.scrP[0:1, 0:1], 0.0).then_inc(self.sem["bar"], 1)
        self.bar_n += 3
        for e in ("act", "dve", "pool"):
            self.eng[e].wait_ge(self.sem["bar"], self.bar_n)
        self.last_ins["act"] = None
        self.last_ins["dve"] = None

    def bank_next(self):
        b = self.banks[self.bank_rr % 8]
        self.bank_rr += 1
        return b

    def wnext(self, kind):
        nc = self.nc
        i = self.w_used
        assert self.wseq[i][0] == kind, (self.wseq[i], kind)
        upto = min(i + NBUF - 1, len(self.wseq) - 1)
        while self.w_issued <= upto:
            j = self.w_issued
            slot = j % NBUF
            if j >= NBUF:
                self.wait("sp", self.wfree_tok[j - NBUF])
            _, l, gi, nk, ncol, first = self.wseq[j]
            if first:
                self.wait("sp", self.conv_tok[l])
            src = self.wscr[l * NWG + gi, :, 0:nk * ncol]
            ins = nc.sync.dma_start(out=self.wbuf[:, slot, 0:nk * ncol], in_=src)
            self.wld_tok[j] = self.dtick("wld%d" % slot, ins)
            self.w_issued += 1
        self.wait("pe", self.wld_tok[i])
        _, l, gi, nk, ncol, _ = self.wseq[i]
        self.w_used += 1
        self.w_cur = i
        return self.wbuf[:, i % NBUF, 0:nk * ncol].rearrange("p (k c) -> p k c", k=nk)

    def wdone(self, ins):
        self.wfree_tok[self.w_cur] = self.tick("pe", ins)
        return self.wfree_tok[self.w_cur]


class EngProxy:
    def __init__(self, k, name, eng):
        self._k, self._name, self._eng = k, name, eng

    def __getattr__(self, attr):
        f = getattr(self._eng, attr)
        if attr in ("wait_ge",):
            return f
        k, name = self._k, self._name

        def wrapped(*a, **kw):
            if k.skip_sync.get(name):
                k.skip_sync[name] = False
            else:
                k.pre_issue(name)
            ins = f(*a, **kw)
            k.post_issue(name, ins)
            return ins
        return wrapped


def _wgroups():
    g = []
    for i in range(11):
        g.append(("gu1", i, 8, 512 if i < 10 else 512))
    for i in range(8):
        g.append(("d1", i, 22, 128))
    for i in range(4):
        g.append(("qk", i, 8, 512))
    for i in range(2):
        g.append(("v", i, 8, 512))
    for i in range(8):
        g.append(("hg", i, 8, 512))
    for i in range(8):
        g.append(("mrg", i, 8, 512))
    for i in range(2):
        g.append(("wo", i, 8, 512))
    for i in range(11):
        g.append(("gu2", i, 8, 512))
    for i in range(8):
        g.append(("d2", i, 22, 128))
    assert len(g) == NWG
    return g


def build(cfg):
    k = B(cfg)
    nc = k.nc
    es = k.es
    SEQ, DEPTH, PAST, TS, NS, TT = cfg.SEQ, cfg.DEPTH, cfg.PAST, cfg.TS, cfg.NS, cfg.TT
    NTS = NS * TS
    NPT = SEQ // TT
    NSQ = 1 + NS

    def din(name, shape):
        return nc.dram_tensor(name, list(shape), F32, kind="ExternalInput").ap()

    def dout(name, shape):
        return nc.dram_tensor(name, list(shape), F32, kind="ExternalOutput").ap()

    xp = din("xp", [SEQ, D]); xs = din("xs", [NTS, D])
    ck = din("ck", [DEPTH, NS, PAST, D]); cv = din("cv", [DEPTH, NS, PAST, D])
    st = din("st", [DEPTH, NS, NH, 128, 128]); cc = din("cc", [NSQ, D])
    w_ada = din("w_ada", [DEPTH, D, 9 * D]); b_ada = din("b_ada", [DEPTH, 9 * D])
    g_ffn1 = din("g_ffn1", [DEPTH, D]); w_gu1 = din("w_ffn1_gu", [DEPTH, D, 2 * DFF]); w_d1 = din("w_ffn1_d", [DEPTH, DFF, D])
    g_mix = din("g_mix", [DEPTH, D]); w_in = din("w_in", [DEPTH, D, DIN])
    att_lambda = din("att_lambda", [DEPTH, 4, 64]); g_att_sub = din("g_att_sub", [DEPTH, 128])
    hg_lb = din("hg_lb_logits", [DEPTH, D]); g_hg_norm = din("g_hg_norm", [DEPTH, 128])
    w_ba = din("w_br_att", [DEPTH, D, D]); w_bh = din("w_br_hg", [DEPTH, D, D]); w_o = din("w_out", [DEPTH, D, D])
    g_ffn2 = din("g_ffn2", [DEPTH, D]); w_gu2 = din("w_ffn2_gu", [DEPTH, D, 2 * DFF]); w_d2 = din("w_ffn2_d", [DEPTH, DFF, D])
    g_final = din("g_final", [D])

    yp = dout("yp", [SEQ, D]); ys = dout("ys", [NTS, D])
    kp = dout("kp", [DEPTH, SEQ, D]); vp = dout("vp", [DEPTH, SEQ, D]); sp_o = dout("sp", [DEPTH, NH, 128, 128])
    ks = dout("ks", [DEPTH, NTS, D]); vs = dout("vs", [DEPTH, NTS, D]); ss_o = dout("ss", [DEPTH, NS, NH, 128, 128])

    if cfg.DEBUG:
        dbg_att = dout("dbg_att", [128, NH, TT]); dbg_h = dout("dbg_h", [128, NH, TT])
        dbg_mod = dout("dbg_mod", [128, DEPTH * 9 * KC * NSQ]); dbg_x = dout("dbg_x", [128, KC * TT]); dbg_h1 = dout("dbg_h1", [128, KC * TT])
        dbg_hid = dout("dbg_hid", [128, FC * TT])
        k.newsem("dbg")
    k.wscr = nc.dram_tensor("wscr", [DEPTH * NWG, 128, WSLOT], BF16, kind="Internal").ap()
    kscr = nc.dram_tensor("kscr", [DEPTH, NPT, 128, NH * TT], BF16, kind="Internal").ap()
    vscr = nc.dram_tensor("vscr", [DEPTH, NPT, 128, NH * TT], BF16, kind="Internal").ap()

    k.es_cur = es
    sb = k.sb
    x_fm = sb("x_fm", [128, KC, TT], F32)
    h_bf = sb("h_bf", [128, KC, TT], BF16)
    o_att = sb("o_att", [128, NH, TT], BF16)
    o_h = sb("o_h", [128, NH, TT], BF16)
    merged = sb("merged", [128, KC, TT], BF16)
    rs = sb("rs", [128, TT], F32)
    S_all = sb("S_all", [128, DEPTH, NH, 128], F32)
    Sb = sb("Sb", [128, NH, 128], BF16)
    k.wbuf = sb("wbuf", [128, NBUF, WSLOT], BF16)
    identF = sb("identF", [128, 128], F32)
    identB = sb("identB", [128, 128], BF16)
    onesB = sb("onesB", [128, 128], BF16)
    onesF = sb("onesF", [128, 128], F32)
    triP = sb("triP", [128, 128], F32)
    triS = sb("triS", [128, 128], F32)
    M0p = sb("M0p", [128, TT], F32)
    M0s = sb("M0s", [128, NTS], F32)
    MOD = sb("MOD", [128, DEPTH, 9, KC, NSQ], F32)
    gfin = sb("gfin", [128, KC], F32)
    gatt = sb("gatt", [128, DEPTH], F32)
    ghg = sb("ghg", [128, DEPTH], F32)
    lamt = sb("lamt", [128, DEPTH], F32)
    lbt = sb("lbt", [128, DEPTH, NH], F32)
    omlt = sb("omlt", [128, DEPTH, NH], F32)
    nomlt = sb("nomlt", [128, DEPTH, NH], F32)
    k.scrA = sb("scrA", [128, 4], F32); k.scrV = sb("scrV", [128, 4], F32); k.scrP = sb("scrP", [128, 4], F32)

    pbs = [es.enter_context(nc.psum_tensor("pb%d" % i, [128, 2, 512], F32)) for i in range(4)]

    class Bank:
        pass
    k.banks = []
    for i in range(8):
        b = Bank()
        b.ap = pbs[i // 2][:, i % 2, :]
        b.pair = pbs[i // 2]
        b.ready = None
        b.free = None
        k.banks.append(b)
    k.bank_rr = 0

    for i in range(NBUF):
        k.newsem("wld%d" % i)
    for s in ("wad0", "wad1", "cv0", "ld", "sld", "st0", "st1", "st2", "xld", "kvst", "kv0", "kv1", "kv2", "kv3", "out"):
        k.newsem(s)

    tick, dtick, wait = k.tick, k.dtick, k.wait
    PE, POOL = nc.tensor, nc.gpsimd
    ACT = EngProxy(k, "act", nc.scalar)
    DVE = EngProxy(k, "dve", nc.vector)

    groups = _wgroups()
    k.conv_tok = {}
    for l in range(DEPTH):
        k.newsem("cvl%d" % l)
        gu = {1: w_gu1[l].rearrange("(kc p) c -> p kc c", p=128), 2: w_gu2[l].rearrange("(kc p) c -> p kc c", p=128)}
        dd = {1: w_d1[l].rearrange("(kc p) c -> p kc c", p=128), 2: w_d2[l].rearrange("(kc p) c -> p kc c", p=128)}
        wi = w_in[l].rearrange("(kc p) c -> p kc c", p=128)
        wba = w_ba[l].rearrange("(kc p) c -> p kc c", p=128)
        wbh = w_bh[l].rearrange("(kc p) c -> p kc c", p=128)
        wo = w_o[l].rearrange("(kc p) c -> p kc c", p=128)
        for gi, (kind, i, nk, ncol) in enumerate(groups):
            dst = k.wscr[l * NWG + gi, :, 0:nk * ncol].rearrange("p (k c) -> p k c", k=nk)
            parts = []
            if kind in ("gu1", "gu2"):
                W = gu[1 if kind == "gu1" else 2]
                parts = [(0, 256, W[:, :, 256 * i:256 * i + 256]), (256, 512, W[:, :, DFF + 256 * i:DFF + 256 * i + 256])]
            elif kind in ("d1", "d2"):
                W = dd[1 if kind == "d1" else 2]
                parts = [(0, 128, W[:, :, 128 * i:128 * i + 128])]
            elif kind == "qk":
                parts = [(0, 512, wi[:, :, 512 * i:512 * i + 512])]
            elif kind == "v":
                parts = [(0, 512, wi[:, :, 2048 + 512 * i:2048 + 512 * i + 512])]
            elif kind == "hg":
                parts = [(0, 512, wi[:, :, 3072 + 512 * i:3072 + 512 * i + 512])]
            elif kind == "mrg":
                parts = [(0, 128, wi[:, :, 7168 + 128 * i:7168 + 128 * i + 128]),
                         (128, 256, wi[:, :, 8192 + 128 * i:8192 + 128 * i + 128]),
                         (256, 384, wba[:, :, 128 * i:128 * i + 128]),
                         (384, 512, wbh[:, :, 128 * i:128 * i + 128])]
            elif kind == "wo":
                parts = [(0, 512, wo[:, :, 512 * i:512 * i + 512])]
            for (c0, c1, src) in parts:
                ins = POOL.dma_start(out=dst[:, :, c0:c1], in_=src)
                k.conv_tok[l] = dtick("cvl%d" % l, ins)

    tiles = []
    for t in range(NPT):
        T = Tile(); T.kind = "p"; T.t = t; T.NT = TT; T.nblk = TT // 128; T.bp = 128
        T.segs = [(0, TT, 0)]
        tiles.append(T)
    T = Tile(); T.kind = "s"; T.t = 0; T.NT = NTS; T.nblk = NS; T.bp = TS
    T.segs = [(i * TS, (i + 1) * TS, 1 + i) for i in range(NS)]
    tiles.append(T)
    k.wseq = []
    for ti, T in enumerate(tiles):
        for l in range(DEPTH):
            for gi, (kind, i, nk, ncol) in enumerate(groups):
                k.wseq.append((kind, l, gi, nk, ncol, ti == 0 and gi == 0))
    k.w_used = 0; k.w_issued = 0; k.wld_tok = {}; k.wfree_tok = {}

    POOL.memset(identF[:], 1.0)
    POOL.affine_select(out=identF[:], in_=identF[:], pattern=[[-1, 128]], compare_op=ALU.is_equal, fill=0.0, base=0, channel_multiplier=1)
    POOL.tensor_copy(out=identB[:], in_=identF[:])
    POOL.memset(onesB[:], 1.0)
    POOL.memset(onesF[:], 1.0)
    POOL.memset(triP[:], 1.0)
    POOL.affine_select(out=triP[:], in_=triP[:], pattern=[[1, 128]], compare_op=ALU.is_ge, fill=0.0, base=0, channel_multiplier=-1)
    POOL.tensor_copy(out=triS[:], in_=triP[:])
    POOL.memset(triP[0:64, 64:128], 0.0)
    POOL.memset(triS[0:32, 32:64], 0.0)
    POOL.memset(M0p[:], 1.0)
    for c in range(TT // 64):
        POOL.memset(M0p[:, 64 * c:64 * c + 1], 0.0)
    POOL.memset(M0s[:], 1.0)
    for c in range(NS):
        POOL.memset(M0s[:, TS * c:TS * c + 1], 0.0)
    POOL.memset(k.scrP[:], 0.0)
    c_tok = tick("pool", POOL.memset(k.scrP[:, 0:1], 0.0))
    DVE.memset(k.scrV[:], 0.0)
    wait("act", c_tok)
    ACT.copy(out=k.scrA[:], in_=identF[:, 0:4])

    with contextlib.ExitStack() as pes:
        k.es_cur = pes
        cT = sb("cT", [128, KC, NSQ], F32)
        cact = sb("cact", [128, KC, NSQ], F32)
        badaT = sb("badaT", [128, DEPTH, 72], F32)
        gT = sb("gT", [128, 3, DEPTH, KC], F32)
        lbl = sb("lbl", [128, DEPTH, NH], F32)
        lam_in = sb("lam_in", [128, DEPTH, 4, 64], F32)
        lam_w = sb("lam_w", [128, DEPTH, 2, 64], F32)
        lam_s = sb("lam_s", [128, DEPTH, 2], F32)
        wad = sb("wad", [128, 2, KC, 1024], F32)
        ld = []
        for s_ in range(NSQ):
            ld.append(dtick("ld", nc.sync.dma_start(out=cT[:, :, s_], in_=cc[s_].rearrange("(c p) -> p c", p=128), allow_slow_non_contiguous=True)))
        for l in range(DEPTH):
            ld.append(dtick("ld", nc.sync.dma_start(out=badaT[:, l, :], in_=b_ada[l].rearrange("(j p) -> p j", p=128), allow_slow_non_contiguous=True)))
            for gi, gsrc in enumerate((g_ffn1, g_mix, g_ffn2)):
                ld.append(dtick("ld", nc.sync.dma_start(out=gT[:, gi, l, :], in_=gsrc[l].rearrange("(c p) -> p c", p=128), allow_slow_non_contiguous=True)))
            ld.append(dtick("ld", nc.sync.dma_start(out=lbl[:, l, :], in_=hg_lb[l].rearrange("(h p) -> p h", p=128), allow_slow_non_contiguous=True)))
        ld.append(dtick("ld", nc.sync.dma_start(out=gfin[:], in_=g_final.rearrange("(c p) -> p c", p=128), allow_slow_non_contiguous=True)))
        ld.append(dtick("ld", nc.sync.dma_start(out=gatt[:], in_=g_att_sub.rearrange("l p -> p l"), allow_slow_non_contiguous=True)))
        ld.append(dtick("ld", nc.sync.dma_start(out=ghg[:], in_=g_hg_norm.rearrange("l p -> p l"), allow_slow_non_contiguous=True)))
        ld.append(dtick("ld", nc.sync.dma_start(out=lam_in[:].rearrange("p l a b -> p (l a b)"),
                                                 in_=att_lambda.rearrange("l a b -> (l a b)").partition_broadcast(128))))
        ldall = ld[-1]
        wait("dve", ldall); wait("act", ldall)
        ct = tick("act", ACT.activation(out=cact[:], in_=cT[:], func=AF.Silu))
        for l in range(DEPTH):
            DVE.tensor_tensor(out=lam_w[:, l, 0, :], in0=lam_in[:, l, 0, :], in1=lam_in[:, l, 1, :], op=ALU.mult)
            DVE.tensor_tensor(out=lam_w[:, l, 1, :], in0=lam_in[:, l, 2, :], in1=lam_in[:, l, 3, :], op=ALU.mult)
        dt_ = tick("dve", DVE.reduce_sum(out=lam_s[:].rearrange("p l a -> p (l a)"), in_=lam_w[:].rearrange("p l a b -> p (l a) b"), axis=mybir.AxisListType.X))
        wait("act", dt_)
        at_ = tick("act", ACT.activation(out=lam_s[:], in_=lam_s[:], func=AF.Exp))
        wait("dve", at_)
        for l in range(DEPTH):
            lam_init = 0.8 - 0.6 * math.exp(-0.3 * l)
            DVE.tensor_tensor(out=lamt[:, l:l + 1], in0=lam_s[:, l, 1:2], in1=lam_s[:, l, 0:1], op=ALU.subtract)
            DVE.tensor_scalar(out=lamt[:, l:l + 1], in0=lamt[:, l:l + 1], scalar1=-lam_init, scalar2=None, op0=ALU.add)
            DVE.tensor_scalar(out=gatt[:, l:l + 1], in0=gatt[:, l:l + 1], scalar1=(1.0 - lam_init) * math.sqrt(128.0), scalar2=None, op0=ALU.mult)
        DVE.tensor_scalar(out=ghg[:], in0=ghg[:], scalar1=math.sqrt(128.0), scalar2=None, op0=ALU.mult)
        DVE.tensor_scalar(out=gfin[:], in0=gfin[:], scalar1=32.0, scalar2=None, op0=ALU.mult)
        at2 = tick("act", ACT.activation(out=lbl[:], in_=lbl[:], func=AF.Exp))
        wait("dve", at2)
        lsum = lam_w[:, 0, 0, 0:NH]
        DVE.tensor_copy(out=lsum, in_=lbl[:, 0, :])
        for l in range(1, DEPTH):
            DVE.tensor_tensor(out=lsum, in0=lsum, in1=lbl[:, l, :], op=ALU.add)
        DVE.reciprocal(out=lsum, in_=lsum)
        DVE.memset(lbt[:, 0, :], 0.0)
        for l in range(1, DEPTH):
            DVE.tensor_tensor(out=lbl[:, l, :], in0=lbl[:, l, :], in1=lsum, op=ALU.mult)
            DVE.tensor_tensor(out=lbt[:, l, :], in0=lbt[:, l - 1, :], in1=lbl[:, l, :], op=ALU.add)
        DVE.tensor_scalar(out=omlt[:], in0=lbt[:], scalar1=-1.0, scalar2=1.0, op0=ALU.mult, op1=ALU.add)
        DVE.tensor_scalar(out=nomlt[:], in0=omlt[:], scalar1=-1.0, scalar2=None, op0=ALU.mult)
        wait("pe", ct)
        wtok = [None, None]
        wfree = [None, None]
        gidx = 0
        for l in range(DEPTH):
            wv = w_ada[l].rearrange("(kc p) c -> p kc c", p=128)
            bk = k.bank_next()
            wait("pe", bk.free)
            outv = bk.ap[:, 0:72 * NSQ].rearrange("p (j s) -> p j s", s=NSQ)
            for g9 in range(9):
                slot = gidx % 2
                wait("sp", wfree[slot])
                wtok[slot] = dtick("wad%d" % slot, nc.sync.dma_start(out=wad[:, slot], in_=wv[:, :, 1024 * g9:1024 * g9 + 1024]))
                wait("pe", wtok[slot])
                for f in range(8):
                    for kc in range(KC):
                        ins = PE.matmul(outv[:, g9 * 8 + f, :], lhsT=wad[:, slot, kc, 128 * f:128 * f + 128], rhs=cact[:, kc, :],
                                        start=(kc == 0), stop=(kc == KC - 1))
                wfree[slot] = tick("pe", ins)
                gidx += 1
            bk.ready = wfree[(gidx - 1) % 2]
            wait("dve", bk.ready)
            ins = DVE.tensor_tensor(out=MOD[:, l].rearrange("p j c s -> p (j c) s"), in0=outv,
                                    in1=badaT[:, l, :].unsqueeze(2).to_broadcast([128, 72, NSQ]), op=ALU.add)
            bk.free = tick("dve", ins)
            for (j_sc, gi_, j_g, half) in ((1, 0, 2, 0.5), (4, 1, 5, 1.0), (7, 2, 8, 0.5)):
                DVE.tensor_scalar(out=MOD[:, l, j_sc], in0=MOD[:, l, j_sc], scalar1=1.0, scalar2=32.0, op0=ALU.add, op1=ALU.mult)
                DVE.tensor_tensor(out=MOD[:, l, j_sc], in0=MOD[:, l, j_sc],
                                  in1=gT[:, gi_, l, :].unsqueeze(2).to_broadcast([128, KC, NSQ]), op=ALU.mult)
                if half != 1.0:
                    DVE.tensor_scalar(out=MOD[:, l, j_g], in0=MOD[:, l, j_g], scalar1=half, scalar2=None, op0=ALU.mult)
        if cfg.DEBUG:
            wait("pool", ("dve", k.cnt["dve"]))
            dmt = dtick("dbg", POOL.dma_start(out=dbg_mod, in_=MOD[:].rearrange("p l j c s -> p (l j c s)")))
            wait("pool", dmt)
        k.barrier()
    k.es_cur = es


    evac_rr = [0]

    def evac_copy(out, in_, bank, scale=None):
        evac_rr[0] += 1
        if scale is not None or evac_rr[0] % 2 == 0:
            wait("act", bank.ready)
            if scale is not None:
                t = tick("act", ACT.mul(out=out, in_=in_, mul=scale))
            else:
                t = tick("act", ACT.copy(out=out, in_=in_))
        else:
            wait("dve", bank.ready)
            t = tick("dve", DVE.tensor_copy(out=out, in_=in_))
        bank.free = t
        return t

    def rstd_act(bank, out_ap, in_ap, eps_total):
        wait("act", bank.ready)
        ACT.activation(out=out_ap, in_=in_ap, func=AF.Ln, bias=eps_total, scale=1.0)
        t = tick("act", ACT.activation(out=out_ap, in_=out_ap, func=AF.Exp, scale=-0.5))
        bank.free = t
        return t

    def norm_phase(T, l, ja, jb, x_tmp):
        NT = T.NT
        sqt = None
        for c in range(KC):
            sqt = tick("act", ACT.activation(out=h_bf[:, c, 0:NT], in_=x_fm[:, c, 0:NT], func=AF.Square))
        bk = k.bank_next()
        wait("pe", bk.free); wait("pe", sqt)
        for c in range(KC):
            ins = PE.matmul(bk.ap[:, 0:NT], lhsT=onesB[:], rhs=h_bf[:, c, 0:NT], start=(c == 0), stop=(c == KC - 1))
        bk.ready = tick("pe", ins)
        wait("dve", rstd_act(bk, rs[:, 0:NT], bk.ap[:, 0:NT], 1024.0 * EPS))
        ht = None
        for c in range(KC):
            for (c0, c1, sq) in T.segs:
                a_ap = gfin[:, c:c + 1] if ja is None else MOD[:, l, ja, c, sq:sq + 1]
                ht = tick("dve", DVE.scalar_tensor_tensor(out=x_tmp[:, c, c0:c1], in0=x_fm[:, c, c0:c1], scalar=a_ap, in1=rs[:, c0:c1], op0=ALU.mult, op1=ALU.mult))
                if jb is not None:
                    ht = tick("dve", DVE.tensor_scalar(out=h_bf[:, c, c0:c1], in0=x_tmp[:, c, c0:c1], scalar1=MOD[:, l, jb, c, sq:sq + 1], scalar2=None, op0=ALU.add))
        return ht

    def norm_mod(T, l, ja, jb):
        with contextlib.ExitStack() as pes:
            k.es_cur = pes
            NT = T.NT
            x_tmp = sb("x_tmp", [128, KC, NT], F32)
            ht = norm_phase(T, l, ja, jb, x_tmp)
            if cfg.DEBUG and T.kind == "p" and T.t == 0 and l == 0 and ja == 1:
                wait("pool", ht)
                wait("pool", dtick("dbg", POOL.dma_start(out=dbg_h1, in_=h_bf[:].rearrange("p c t -> p (c t)"))))
            k.barrier()
        k.es_cur = es
        return ht

    def ffn_phase(T, l, which, h_tok):
        NT = T.NT
        jg = 2 if which == 1 else 8
        with contextlib.ExitStack() as pes:
            k.es_cur = pes
            hid = sb("hid", [128, FC, NT], BF16)
            stmp = sb("stmp", [128, 2, NT], F32)
            st_free = [None, None]
            hid_tok = None
            u = 0
            for g in range(11):
                w = k.wnext("gu%d" % which)
                for j in range(2):
                    jj = 2 * g + j
                    bA = k.bank_next(); bB = k.bank_next()
                    wait("pe", bA.free); wait("pe", bB.free); wait("pe", h_tok)
                    for kc in range(KC):
                        PE.matmul(bA.ap[:, 0:NT], lhsT=w[:, kc, 128 * j:128 * j + 128], rhs=h_bf[:, kc, 0:NT], start=(kc == 0), stop=(kc == KC - 1))
                    for kc in range(KC):
                        ins = PE.matmul(bB.ap[:, 0:NT], lhsT=w[:, kc, 256 + 128 * j:256 + 128 * j + 128], rhs=h_bf[:, kc, 0:NT], start=(kc == 0), stop=(kc == KC - 1))
                    if j == 1:
                        bA.ready = bB.ready = k.wdone(ins)
                    else:
                        bA.ready = bB.ready = tick("pe", ins)
                    s = u % 2
                    wait("act", bA.ready); wait("act", st_free[s])
                    a_t = tick("act", ACT.activation(out=stmp[:, s, 0:NT], in_=bA.ap[:, 0:NT], func=AF.Silu))
                    bA.free = a_t
                    wait("dve", a_t); wait("dve", bB.ready)
                    d_t = tick("dve", DVE.tensor_tensor(out=hid[:, jj, 0:NT], in0=stmp[:, s, 0:NT], in1=bB.ap[:, 0:NT], op=ALU.mult))
                    bB.free = d_t; st_free[s] = d_t; hid_tok = d_t
                    u += 1
            if cfg.DEBUG and T.kind == "p" and T.t == 0 and l == 0 and which == 1:
                wait("pool", hid_tok)
                wait("pool", dtick("dbg", POOL.dma_start(out=dbg_hid, in_=hid[:].rearrange("p c t -> p (c t)"))))
            for m in range(8):
                w = k.wnext("d%d" % which)
                bk = k.bank_next()
                wait("pe", bk.free); wait("pe", hid_tok)
                for kc in range(FC):
                    ins = PE.matmul(bk.ap[:, 0:NT], lhsT=w[:, kc, :], rhs=hid[:, kc, 0:NT], start=(kc == 0), stop=(kc == FC - 1))
                bk.ready = k.wdone(ins)
                wait("dve", bk.ready)
                for (c0, c1, sq) in T.segs:
                    ins = DVE.scalar_tensor_tensor(out=x_fm[:, m, c0:c1], in0=bk.ap[:, c0:c1], scalar=MOD[:, l, jg, m, sq:sq + 1],
                                                   in1=x_fm[:, m, c0:c1], op0=ALU.mult, op1=ALU.add)
                bk.free = tick("dve", ins)
            k.barrier()
        k.es_cur = es

    def load_x(T):
        NT, nblk, bp = T.NT, T.nblk, T.bp
        with contextlib.ExitStack() as pes:
            k.es_cur = pes
            xin = sb("xin", [128, nblk, D], F32)
            if T.kind == "p":
                src = xp[T.t * TT:(T.t + 1) * TT, :].rearrange("(b p) f -> p b f", p=128)
            else:
                src = xs.rearrange("(b p) f -> p b f", p=bp)
            tok = dtick("xld", POOL.dma_start(out=xin[0:bp], in_=src))
            wait("pe", tok)
            for c in range(KC):
                bk = k.bank_next()
                wait("pe", bk.free)
                for b in range(nblk):
                    ins = PE.transpose(out=bk.ap[:, b * bp:(b + 1) * bp], in_=xin[0:bp, b, 128 * c:128 * c + 128], identity=identF[0:bp, 0:bp])
                bk.ready = tick("pe", ins)
                evac_copy(x_fm[:, c, 0:NT], bk.ap[:, 0:NT], bk)
            if cfg.DEBUG and T.kind == "p" and T.t == 0:
                wait("pool", ("dve", k.cnt["dve"])); wait("pool", ("act", k.cnt["act"]))
                wait("pool", dtick("dbg", POOL.dma_start(out=dbg_x, in_=x_fm[:].rearrange("p c t -> p (c t)"))))
            k.barrier()
        k.es_cur = es

    kvst_tok = {}

    def mixing(T, l, h_tok):
        NT, nblk, bp = T.NT, T.nblk, T.bp
        isp = T.kind == "p"
        kout = kp if isp else ks
        vout = vp if isp else vs
        r0 = T.t * TT if isp else 0
        with contextlib.ExitStack() as pes:
            k.es_cur = pes
            qT = sb("qT", [128, NH, NT], BF16)
            kT = sb("kT", [128, NH, NT], BF16)
            v_tm = sb("v_tm", [128, NH, nblk, 128], BF16)
            stage = sb("stage", [128, 3, 512], F32)
            Pb = sb("Pb", [128, 3, 2, NT], BF16)
            fin = sb("fin", [128, 5, NT], F32)
            finc = sb("finc", [128, 4, NT], F32)
            lacc = sb("lacc", [128, NT], F32)
            lacc_free = [None]
            p_free2 = [None, None, None]
            sqb = sb("sqb", [128, NT], BF16)
            rsa = sb("rsa", [128, NT], F32)
            st_tok = [None, None, None]
            st_n = [0]
            last_store = []

            def store_rows(bank, dst_ap, also=None):
                s = st_n[0] % 3
                st_n[0] += 1
                wait("act", bank.ready); wait("act", st_tok[s])
                t1 = tick("act", ACT.copy(out=stage[0:bp, s, :], in_=bank.ap[0:bp, 0:512]))
                if also is not None:
                    wait("dve", bank.ready)
                    wait("dve", t1)
                    bank.free = tick("dve", DVE.tensor_copy(out=also, in_=bank.ap[0:bp, 0:512].rearrange("p (h e) -> p h e", e=128)))
                    ret = bank.free
                else:
                    bank.free = t1
                    ret = None
                wait("pool", t1)
                st_tok[s] = dtick("st%d" % s, POOL.dma_start(out=dst_ap, in_=stage[0:bp, s, :]))
                last_store.append(st_tok[s])
                return ret

            kt_tok = None
            vt_tok = None
            q_tok = None
            for g in range(4):
                w = k.wnext("qk")
                ins = None
                for j in range(4):
                    bk = k.bank_next()
                    wait("pe", bk.free); wait("pe", h_tok)
                    for kc in range(KC):
                        ins = PE.matmul(bk.ap[:, 0:NT], lhsT=w[:, kc, 128 * j:128 * j + 128], rhs=h_bf[:, kc, 0:NT], start=(kc == 0), stop=(kc == KC - 1))
                    bk.ready = tick("pe", ins)
                    if g < 2:
                        q_tok = evac_copy(qT[:, 4 * g + j, 0:NT], bk.ap[:, 0:NT], bk, scale=0.125)
                    else:
                        wait("dve", bk.ready)
                        kt_tok = bk.free = tick("dve", DVE.tensor_copy(out=kT[:, 4 * (g - 2) + j, 0:NT], in_=bk.ap[:, 0:NT]))
                if g >= 2:
                    for b in range(nblk):
                        bk = k.bank_next()
                        wait("pe", bk.free)
                        for kc in range(KC):
                            ins = PE.matmul(bk.ap[0:bp, 0:512], lhsT=h_bf[:, kc, b * bp:(b + 1) * bp], rhs=w[:, kc, :], start=(kc == 0), stop=(kc == KC - 1))
                        bk.ready = tick("pe", ins)
                        store_rows(bk, kout[l, r0 + b * bp:r0 + (b + 1) * bp, 512 * (g - 2):512 * (g - 2) + 512])
                k.wfree_tok[k.w_cur] = bk.ready
            for g in range(2):
                w = k.wnext("v")
                for b in range(nblk):
                    bk = k.bank_next()
                    wait("pe", bk.free); wait("pe", h_tok)
                    for kc in range(KC):
                        ins = PE.matmul(bk.ap[0:bp, 0:512], lhsT=h_bf[:, kc, b * bp:(b + 1) * bp], rhs=w[:, kc, :], start=(kc == 0), stop=(kc == KC - 1))
                    bk.ready = tick("pe", ins)
                    vt_tok = store_rows(bk, vout[l, r0 + b * bp:r0 + (b + 1) * bp, 512 * g:512 * g + 512], also=v_tm[0:bp, 4 * g:4 * g + 4, b, :])
                k.wfree_tok[k.w_cur] = bk.ready
            if isp and T.t < NPT - 1:
                wait("pool", kt_tok); wait("pool", vt_tok)
                dtick("kvst", POOL.dma_start(out=kscr[l, T.t].rearrange("p (h t) -> p h t", h=NH), in_=kT[:, :, :]))
                kvst_tok[(l, T.t)] = dtick("kvst", POOL.dma_start(out=vscr[l, T.t], in_=v_tm[:].rearrange("p h b e -> p (h b e)")))
                last_store.append(kvst_tok[(l, T.t)])

            O = [k.banks[0], k.banks[1]]
            L = [k.banks[2], k.banks[3]]
            Sp = [(k.banks[4], k.banks[5]), (k.banks[6], k.banks[7])]
            if isp:
                kring = sb("kring", [128, 4, TT], BF16)
                vring = sb("vring", [128, 4, TT], BF16)
            ring_free = [None] * 4
            if not isp:
                NBK = PAST // 128
                kc_in = sb("kc_in", [128, 2, NBK, 128], F32)
                vc_in = sb("vc_in", [128, 2, NBK, 128], F32)
                kcT = sb("kcT", [128, 2, PAST], BF16)
                vcb = sb("vcb", [128, 2, NBK, 128], BF16)
            units = []
            if isp:
                for h in range(NH):
                    units.append(dict(h=h, q0=0, N=NT, seq=None))
            else:
                for i in range(NS):
                    for h in range(NH):
                        units.append(dict(h=h, q0=i * TS, N=TS, seq=i))
            loads = []
            if isp:
                for ui, u in enumerate(units):
                    for s in range(T.t):
                        loads.append((ui, s))
            load_tok = {}
            issued = [0]

            def issue_upto(n):
                while issued[0] < min(n, len(loads)):
                    i = issued[0]
                    ui, s = loads[i]
                    slot = i % 4
                    wait("pool", ring_free[slot]); wait("pool", kvst_tok[(l, s)])
                    h = units[ui]["h"]
                    dtick("kv%d" % slot, POOL.dma_start(out=kring[:, slot, :], in_=kscr[l, s][:, h * TT:(h + 1) * TT]))
                    load_tok[i] = dtick("kv%d" % slot, POOL.dma_start(out=vring[:, slot, :], in_=vscr[l, s][:, h * TT:(h + 1) * TT]))
                    issued[0] += 1

            cin_tok = {}
            cin_free = [None, None]
            cprep_tok = {}

            def cache_load(ui):
                u = units[ui]
                s = ui % 2
                wait("pool", cin_free[s])
                h = u["h"]; i = u["seq"]
                dtick("kv%d" % s, POOL.dma_start(out=kc_in[:, s], in_=ck[l, i, :, 128 * h:128 * h + 128].rearrange("(b p) d -> p b d", p=128)))
                cin_tok[ui] = dtick("kv%d" % s, POOL.dma_start(out=vc_in[:, s], in_=cv[l, i, :, 128 * h:128 * h + 128].rearrange("(b p) d -> p b d", p=128)))

            cprep_free = [None, None]

            def cache_prep(ui):
                s = ui % 2
                wait("pe", cin_tok[ui]); wait("dve", cin_tok[ui]); wait("act", cin_tok[ui])
                wait("dve", cprep_free[s]); wait("act", cprep_free[s])
                t = None
                for b4 in range(NBK // 4):
                    bk = k.bank_next()
                    wait("pe", bk.free)
                    for j in range(4):
                        ins = PE.transpose(out=bk.ap[:, 128 * j:128 * j + 128], in_=kc_in[:, s, 4 * b4 + j, :], identity=identF[:])
                    bk.ready = tick("pe", ins)
                    t = evac_copy(kcT[:, s, 512 * b4:512 * b4 + 512], bk.ap[:, :], bk)
                wait("dve", t)
                t2 = tick("dve", DVE.tensor_copy(out=vcb[:, s], in_=vc_in[:, s]))
                wait("act", t2)
                t3 = tick("act", ACT.copy(out=k.scrA[0:1, 2:3], in_=k.scrA[0:1, 1:2]))
                cin_free[s] = t3
                cprep_tok[ui] = t3

            pending = []
            pn = [0]
            sn = [0]
            p_free = [None, None, None]
            wait("pe", q_tok); wait("pe", kt_tok); wait("pe", vt_tok)
            if not isp:
                cache_load(0)
            for ui, u in enumerate(units):
                h, q0, N = u["h"], u["q0"], u["N"]
                blocks = []
                if isp:
                    base = sum(1 for (a, _) in loads if a < ui)
                    for s in range(T.t):
                        li = base + s
                        for b in range(4):
                            blocks.append(dict(kt=kring[:, li % 4, 128 * b:128 * b + 128], v=vring[:, li % 4, 128 * b:128 * b + 128], nk=128, qs=0, zero=False,
                                               li=li, last=(b == 3)))
                    for b in range(nblk):
                        blocks.append(dict(kt=kT[:, h, 128 * b:128 * b + 128], v=v_tm[:, h, b, :], nk=128, qs=128 * b, zero=True, li=None, last=False))
                else:
                    s2 = ui % 2
                    if ui + 1 < len(units):
                        cache_load(ui + 1)
                    cache_prep(ui)
                    wait("pe", cprep_tok[ui])
                    for b in range(NBK):
                        blocks.append(dict(kt=kcT[:, s2, 128 * b:128 * b + 128], v=vcb[:, s2, b, :], nk=128, qs=0, zero=False, li=None, last=False))
                    i = u["seq"]
                    blocks.append(dict(kt=kT[:, h, i * TS:(i + 1) * TS], v=v_tm[0:TS, h, i, :], nk=TS, qs=0, zero=False, li=None, last=False))
                nb = len(blocks)

                def emit_S(j):
                    bl = blocks[j]
                    pr = Sp[sn[0] % 2]
                    bl["pr"] = pr
                    sn[0] += 1
                    if bl["li"] is not None:
                        issue_upto(bl["li"] + 3)
                        wait("pe", load_tok[bl["li"]])
                    wait("pe", pr[0].free); wait("pe", pr[1].free)
                    nk, qs = bl["nk"], bl["qs"]
                    PE.matmul(pr[0].ap[0:nk, qs:N], lhsT=bl["kt"][0:64, :], rhs=qT[0:64, h, q0 + qs:q0 + N], start=True, stop=True)
                    ins = PE.matmul(pr[1].ap[0:nk, qs:N], lhsT=bl["kt"][64:128, :], rhs=qT[64:128, h, q0 + qs:q0 + N], start=True, stop=True)
                    pr[0].ready = pr[1].ready = tick("pe", ins)

                wait("pe", O[0].free); wait("pe", O[1].free); wait("pe", L[0].free); wait("pe", L[1].free)
                emit_S(0)
                for j in range(nb):
                    bl = blocks[j]
                    nk, qs = bl["nk"], bl["qs"]
                    pr = bl["pr"]
                    ps = pn[0] % 3
                    pn[0] += 1
                    wait("act", pr[0].ready); wait("act", p_free[ps]); wait("act", p_free2[ps])
                    if isp:
                        k.skip_sync["act"] = True
                    e_t = tick("act", ACT.activation(out=Pb[0:nk, ps, :, qs:N], in_=pr[0].pair[0:nk, :, qs:N], func=AF.Exp))
                    if bl["zero"]:
                        e_t = tick("act", ACT.mul(out=Pb[64:128, ps, :, qs:qs + 64], in_=Pb[64:128, ps, :, qs:qs + 64], mul=0.0))
                    pr[0].free = pr[1].free = e_t
                    if j + 1 < nb:
                        emit_S(j + 1)
                    wait("pe", e_t)
                    for m in range(2):
                        PE.matmul(O[m].ap[:, qs:N], lhsT=bl["v"], rhs=Pb[0:nk, ps, m, qs:N], start=(j == 0), stop=(j == nb - 1))
                    for m in ((1,) if isp else (0, 1)):
                        ins = PE.matmul(L[m].ap[:, qs:N], lhsT=onesB[0:nk, :], rhs=Pb[0:nk, ps, m, qs:N], start=(j == 0), stop=(j == nb - 1))
                    pv_t = tick("pe", ins)
                    p_free[ps] = pv_t
                    if isp:
                        wait("dve", e_t)
                        if j == 0:
                            wait("dve", lacc_free[0])
                            dacc_t = tick("dve", DVE.tensor_copy(out=lacc[:, 0:N], in_=Pb[:, ps, 0, 0:N]))
                        else:
                            dacc_t = tick("dve", DVE.tensor_tensor(out=lacc[:, qs:N], in0=lacc[:, qs:N], in1=Pb[:, ps, 0, qs:N], op=ALU.add))
                        p_free2[ps] = dacc_t
                    if bl["last"]:
                        ring_free[bl["li"] % 4] = pv_t
                    if j == 1 and pending:
                        pending.pop(0)()
                while pending:
                    pending.pop(0)()
                if isp:
                    wait("pe", dacc_t)
                    pv_t = tick("pe", PE.matmul(L[0].ap[:, 0:N], lhsT=onesF[:], rhs=lacc[:, 0:N], start=True, stop=True))
                    lacc_free[0] = pv_t
                wait("act", pv_t); wait("dve", pv_t)
                ACT.activation(out=finc[:, 2, 0:N], in_=L[0].ap[:, 0:N], func=AF.Ln)
                a_t = tick("act", ACT.activation(out=finc[:, 3, 0:N], in_=L[1].ap[:, 0:N], func=AF.Ln))
                a2_t = tick("act", ACT.activation(out=finc[:, 2:4, 0:N], in_=finc[:, 2:4, 0:N], func=AF.Exp, scale=-1.0))
                DVE.tensor_copy(out=finc[:, 0, 0:N], in_=O[0].ap[:, 0:N])
                d_t = tick("dve", DVE.tensor_copy(out=finc[:, 1, 0:N], in_=O[1].ap[:, 0:N]))
                L[0].free = L[1].free = a_t
                O[0].free = O[1].free = d_t
                wait("dve", a2_t)
                DVE.tensor_tensor(out=fin[:, 2, 0:N], in0=finc[:, 0, 0:N], in1=finc[:, 2, 0:N], op=ALU.mult)
                DVE.tensor_tensor(out=fin[:, 3, 0:N], in0=finc[:, 1, 0:N], in1=finc[:, 3, 0:N], op=ALU.mult)
                DVE.scalar_tensor_tensor(out=fin[:, 4, 0:N], in0=fin[:, 3, 0:N], scalar=lamt[:, l:l + 1], in1=fin[:, 2, 0:N], op0=ALU.mult, op1=ALU.add)
                sq_t = tick("dve", DVE.tensor_tensor(out=sqb[:, 0:N], in0=fin[:, 4, 0:N], in1=fin[:, 4, 0:N], op=ALU.mult))

                def fin2(h=h, q0=q0, N=N, sq_t=sq_t):
                    pr = Sp[sn[0] % 2]
                    sn[0] += 1
                    bk = pr[0]
                    wait("pe", pr[0].free); wait("pe", pr[1].free); wait("pe", sq_t)
                    ins = PE.matmul(bk.ap[:, 0:N], lhsT=onesB[:], rhs=sqb[:, 0:N], start=True, stop=True)
                    pr[0].ready = pr[1].ready = tick("pe", ins)
                    t = rstd_act(bk, rsa[:, 0:N], bk.ap[:, 0:N], 128.0 * EPS)
                    pr[0].free = pr[1].free = t
                    wait("dve", t)
                    return tick("dve", DVE.scalar_tensor_tensor(out=o_att[:, h, q0:q0 + N], in0=fin[:, 4, 0:N], scalar=gatt[:, l:l + 1], in1=rsa[:, 0:N], op0=ALU.mult, op1=ALU.mult))
                pending.append(fin2)
            oa_tok = None
            while pending:
                oa_tok = pending.pop(0)()
            if cfg.DEBUG and isp and T.t == 0 and l == 0:
                wait("pool", oa_tok)
                last_store.append(dtick("dbg", POOL.dma_start(out=dbg_att, in_=o_att[:])))
            k.barrier(pool_tokens=last_store)
        k.es_cur = es

        with contextlib.ExitStack() as pes:
            k.es_cur = pes
            if isp:
                Lc = 64; h2 = 32
                nch = NT // 64
                M0 = M0p
            else:
                Lc = TS; h2 = TS
                nch = NS
                M0 = M0s
            two = h2 < Lc
            hq = sb("hq", [128, NH, NT], F32)
            sigf = sb("sigf", [128, NH, NT], F32)
            hgt = sb("hgt", [128, NH, NT], BF16)
            hi_tm = sb("hi_tm", [64, NH, nch, 128], BF16)
            fw = sb("fw", [128, 10, NT], F32)
            qk4 = sb("qk4", [128, 2, 5, NT], BF16)
            kd_tm = sb("kd_tm", [64, 2, nch, 128], BF16)
            Am = sb("Am", [64, 2, nch, 64], BF16)
            sm = sb("sm", [128, 4, TT // 32], F32)
            osq = sb("osq", [128, NT], BF16)
            ot = sb("ot", [128, NT], F32)
            rsh = sb("rsh", [128, NT], F32)
            if not isp:
                Ssm = sb("Ssm", [128, NS, NH, 128], F32)
                Sbs = sb("Sbs", [128, NS, NH, 128], BF16)
            if two:
                DVE.memset(Am[h2:Lc, :, :, 0:h2], 0.0)
            hq_tok = sig_tok = hg_tok = hi_tok = None
            for i in range(8):
                w = k.wnext("hg")
                if i in (4, 5):
                    for b in range(nch):
                        bk = k.bank_next()
                        wait("pe", bk.free); wait("pe", h_tok)
                        for kc in range(KC):
                            ins = PE.matmul(bk.ap[0:Lc, 0:512], lhsT=h_bf[:, kc, b * Lc:(b + 1) * Lc], rhs=w[:, kc, :], start=(kc == 0), stop=(kc == KC - 1))
                        bk.ready = tick("pe", ins)
                        hi_tok = evac_copy(hi_tm[0:Lc, 4 * (i - 4):4 * (i - 4) + 4, b, :], bk.ap[0:Lc, 0:512].rearrange("p (h e) -> p h e", e=128), bk)
                else:
                    for j in range(4):
                        bk = k.bank_next()
                        wait("pe", bk.free); wait("pe", h_tok)
                        for kc in range(KC):
                            ins = PE.matmul(bk.ap[:, 0:NT], lhsT=w[:, kc, 128 * j:128 * j + 128], rhs=h_bf[:, kc, 0:NT], start=(kc == 0), stop=(kc == KC - 1))
                        bk.ready = tick("pe", ins)
                        wait("act", bk.ready)
                        if i < 2:
                            hq_tok = bk.free = tick("act", ACT.activation(out=hq[:, 4 * i + j, 0:NT], in_=bk.ap[:, 0:NT], func=AF.Silu))
                        elif i < 4:
                            sig_tok = bk.free = tick("act", ACT.activation(out=sigf[:, 4 * (i - 2) + j, 0:NT], in_=bk.ap[:, 0:NT], func=AF.Sigmoid))
                        else:
                            hg_tok = bk.free = tick("act", ACT.activation(out=hgt[:, 4 * (i - 6) + j, 0:NT], in_=bk.ap[:, 0:NT], func=AF.Silu))
                k.wfree_tok[k.w_cur] = bk.ready
            hi_toks = [("act", k.cnt["act"]), ("dve", k.cnt["dve"])]
            if isp:
                sb_tok = tick("dve", DVE.tensor_copy(out=Sb[:], in_=S_all[:, l]))
            else:
                s_ld = dtick("sld", POOL.dma_start(out=Ssm[:].rearrange("p i h v -> p (i h) v"), in_=st[l].rearrange("i h kk v -> kk (i h) v")))
                wait("dve", s_ld)
                sb_tok = tick("dve", DVE.tensor_copy(out=Sbs[:], in_=Ssm[:]))
            wait("dve", sig_tok); wait("dve", hq_tok)
            for t_ in hi_toks:
                wait("pe", t_)
            pe_use = [None, None]
            last_o = None
            act_rd = [None]
            dve_rd = [None]
            dve_use = [None, None]

            def c3(ap):
                return ap.rearrange("p (c t) -> p c t", t=Lc)
            def prep(h):
                    hs = h % 2
                    fA, fK, fB, fC, fD, fE, gA, gB, gC, gD = [fw[:, i, 0:NT] for i in range(10)]
                    qe, qa2, ka0, ka1, kd = [qk4[:, hs, i, 0:NT] for i in range(5)]
                    PL = POOL
                    wait("pool", sig_tok); wait("pool", hq_tok); wait("pool", act_rd[0])
                    t0_ = tick("pool", PL.tensor_scalar(out=fA, in0=sigf[:, h, 0:NT], scalar1=omlt[:, l, h:h + 1], scalar2=lbt[:, l, h:h + 1], op0=ALU.mult, op1=ALU.add))
                    wait("dve", t0_)
                    t1 = tick("dve", DVE.tensor_scalar(out=fA, in0=fA, scalar1=TINY, scalar2=None, op0=ALU.max))
                    wait("act", t1)
                    t2 = tick("act", ACT.activation(out=fB, in_=fA, func=AF.Ln))
                    wait("pool", dve_rd[0])
                    PL.tensor_scalar(out=fK, in0=sigf[:, h, 0:NT], scalar1=nomlt[:, l, h:h + 1], scalar2=omlt[:, l, h:h + 1], op0=ALU.mult, op1=ALU.add)
                    wait("pool", t2)
                    wait("dve", t2)
                    DVE.tensor_tensor_scan(out=fC, data0=M0[:, 0:NT], data1=fB, initial=0.0, op0=ALU.mult, op1=ALU.add)
                    sc_t = tick("dve", k.last_ins["dve"])
                    wait("pool", sc_t)
                    b3 = c3(fC)
                    if two:
                        PL.tensor_tensor(out=c3(fD), in0=b3, in1=b3[:, :, h2 - 1:h2].to_broadcast([128, nch, Lc]), op=ALU.subtract)
                    PL.tensor_tensor(out=c3(fE), in0=b3[:, :, Lc - 1:Lc].to_broadcast([128, nch, Lc]), in1=b3, op=ALU.subtract)
                    t3 = tick("pool", PL.tensor_copy(out=sm[:, 0, 0:nch], in_=b3[:, :, Lc - 1]))
                    wait("act", t3)
                    ACT.activation(out=gC, in_=fC, func=AF.Exp)
                    ACT.activation(out=c3(gA)[:, :, 0:h2], in_=c3(fC)[:, :, 0:h2], func=AF.Exp, scale=-1.0)
                    if two:
                        ACT.activation(out=c3(gA)[:, :, h2:Lc], in_=c3(fD)[:, :, h2:Lc], func=AF.Exp)
                        ACT.activation(out=gB, in_=fD, func=AF.Exp, scale=-1.0)
                    ACT.activation(out=gD, in_=fE, func=AF.Exp)
                    t4 = tick("act", ACT.activation(out=sm[:, 1, 0:nch], in_=sm[:, 0, 0:nch], func=AF.Exp))
                    act_rd[0] = t4
                    wait("pool", t4); wait("pool", pe_use[hs]); wait("pool", dve_use[hs])
                    PL.tensor_tensor(out=qe, in0=hq[:, h, 0:NT], in1=gC, op=ALU.mult)
                    PL.tensor_tensor(out=c3(ka0)[:, :, 0:h2], in0=c3(fK)[:, :, 0:h2], in1=c3(gA)[:, :, 0:h2], op=ALU.mult)
                    if two:
                        PL.tensor_tensor(out=c3(qa2)[:, :, h2:Lc], in0=c3(hq[:, h, 0:NT])[:, :, h2:Lc], in1=c3(gA)[:, :, h2:Lc], op=ALU.mult)
                        PL.tensor_tensor(out=ka1, in0=fK, in1=gB, op=ALU.mult)
                    PL.tensor_copy(out=sm[:, 2 + hs, 0:nch], in_=sm[:, 1, 0:nch])
                    t5 = tick("pool", PL.tensor_tensor(out=kd, in0=fK, in1=gD, op=ALU.mult))
                    wait("dve", t5)
                    return t5

            HD = {}
            osq_free = [None]
            rsh_free = [None]

            def stageA(h, t5):
                hs = h % 2
                qe, qa2, ka0, ka1, kd = [qk4[:, hs, i, 0:NT] for i in range(5)]
                bA = k.banks[0]; bT = k.banks[1]
                wait("pe", t5); wait("pe", bA.free); wait("pe", bT.free)
                for ci in range(nch):
                    cs = ci * Lc
                    ins = PE.matmul(bA.ap[0:h2, cs:cs + h2], lhsT=ka0[:, cs:cs + h2], rhs=qe[:, cs:cs + h2], start=True, stop=True)
                    if two:
                        ins = PE.matmul(bA.ap[0:Lc, cs + h2:cs + Lc], lhsT=ka1[:, cs:cs + Lc], rhs=qa2[:, cs + h2:cs + Lc], start=True, stop=True)
                bA.ready = tick("pe", ins)
                bTv = bT.ap.bitcast(BF16)
                for ci in range(nch):
                    ins = PE.transpose(out=bTv[0:Lc, 128 * ci:128 * ci + 128], in_=kd[:, ci * Lc:(ci + 1) * Lc], identity=identB[:])
                bT.ready = tick("pe", ins)
                wait("dve", bA.ready); wait("dve", pe_use[hs])
                bA3 = bA.ap[:, 0:nch * Lc].rearrange("p (c t) -> p c t", t=Lc)
                am_t = tick("dve", DVE.tensor_tensor(out=Am[0:h2, hs, :, 0:Lc], in0=bA3[0:h2], in1=triP[0:h2, 0:Lc].unsqueeze(1).to_broadcast([h2, nch, Lc]), op=ALU.mult))
                if two:
                    am_t = tick("dve", DVE.tensor_tensor(out=Am[h2:Lc, hs, :, h2:Lc], in0=bA3[h2:Lc, :, h2:Lc],
                                                         in1=triP[h2:Lc, h2:Lc].unsqueeze(1).to_broadcast([Lc - h2, nch, Lc - h2]), op=ALU.mult))
                bA.free = am_t
                wait("act", bT.ready); wait("act", pe_use[hs])
                kt_t = bT.free = tick("act", ACT.copy(out=kd_tm[0:Lc, hs], in_=bTv[0:Lc, 0:nch * 128].rearrange("p (b e) -> p b e", e=128)))
                HD[h] = dict(am_t=am_t, kt_t=kt_t, t5=t5)

            def chunk(h, ci):
                hs = h % 2
                qe = qk4[:, hs, 0, 0:NT]
                ebl = sm[:, 2 + hs, :]
                bO = k.banks[2 + (h % 2)]
                if ci == 0:
                    wait("pe", bO.free); wait("pe", HD[h]["am_t"]); wait("pe", HD[h]["kt_t"]); wait("dve", HD[h]["t5"])
                cs = ci * Lc
                if isp:
                    Sst = S_all[:, l, h, :]; Sbh = Sb[:, h, :]
                else:
                    Sst = Ssm[:, ci, h, :]; Sbh = Sbs[:, ci, h, :]
                wait("pe", nonloc["sb_tok"])
                PE.matmul(bO.ap[:, cs:cs + Lc], lhsT=Sbh, rhs=qe[:, cs:cs + Lc], start=True, stop=False)
                PE.matmul(bO.ap[:, cs:cs + Lc], lhsT=hi_tm[0:Lc, h, ci, :], rhs=Am[0:Lc, hs, ci, 0:Lc], start=False, stop=True)
                bS = k.banks[4 + (ci % 2)]
                wait("pe", bS.free)
                ins = PE.matmul(bS.ap[:, 0:128], lhsT=kd_tm[0:Lc, hs, ci, :], rhs=hi_tm[0:Lc, h, ci, :], start=True, stop=True)
                bS.ready = tick("pe", ins)
                wait("dve", bS.ready)
                bS.free = tick("dve", DVE.scalar_tensor_tensor(out=Sst, in0=Sst, scalar=ebl[:, ci:ci + 1], in1=bS.ap[:, 0:128], op0=ALU.mult, op1=ALU.add))
                nonloc["sb_tok"] = tick("dve", DVE.tensor_copy(out=Sbh, in_=Sst))
                if ci == nch - 1:
                    bO.ready = bS.ready
                    pe_use[hs] = bS.ready
                    dve_use[hs] = nonloc["sb_tok"]

            def fin_a(h):
                bO = k.banks[2 + (h % 2)]
                wait("act", bO.ready); wait("act", osq_free[0])
                HD[h]["q_t"] = tick("act", ACT.activation(out=osq[:, 0:NT], in_=bO.ap[:, 0:NT], func=AF.Square))

            def fin_b(h):
                bO = k.banks[2 + (h % 2)]
                bq = k.banks[6 + (h % 2)]
                wait("pe", bq.free); wait("pe", HD[h]["q_t"])
                bq.ready = tick("pe", PE.matmul(bq.ap[:, 0:NT], lhsT=onesB[:], rhs=osq[:, 0:NT], start=True, stop=True))
                osq_free[0] = bq.ready
                wait("act", rsh_free[0])
                wait("dve", rstd_act(bq, rsh[:, 0:NT], bq.ap[:, 0:NT], 128.0 * EPS)); wait("dve", hg_tok)
                bO.free = tick("dve", DVE.scalar_tensor_tensor(out=ot[:, 0:NT], in0=bO.ap[:, 0:NT], scalar=ghg[:, l:l + 1], in1=rsh[:, 0:NT], op0=ALU.mult, op1=ALU.mult))
                rsh_free[0] = bO.free
                nonloc["last_o"] = tick("dve", DVE.tensor_tensor(out=o_h[:, h, 0:NT], in0=ot[:, 0:NT], in1=hgt[:, h, 0:NT], op=ALU.mult))
                if cfg.DEBUG and isp and T.t == 0 and l == 0:
                    wait("pool", nonloc["last_o"])
                    dd_ = dtick("dbg", POOL.dma_start(out=dbg_h[:, h, :], in_=ot[:, :]))
                    wait("dve", dd_)

            nonloc = {"sb_tok": sb_tok, "last_o": None}
            pend = []
            t5_0 = prep(0)
            stageA(0, t5_0)
            for h in range(NH):
                t5n = prep(h + 1) if h + 1 < NH else None
                for ci in range(nch):
                    chunk(h, ci)
                    if ci == min(2, nch - 1) and pend:
                        pend.pop(0)()
                if h + 1 < NH:
                    stageA(h + 1, t5n)
                fin_a(h)
                pend.append(lambda h=h: fin_b(h))
            while pend:
                pend.pop(0)()
            sb_tok = nonloc["sb_tok"]; last_o = nonloc["last_o"]
            ptoks = []
            if isp and T.t == NPT - 1:
                wait("pool", sb_tok)
                ptoks.append(dtick("out", POOL.dma_start(out=sp_o[l].rearrange("h kk v -> kk h v"), in_=S_all[:, l])))
            if not isp:
                wait("pool", sb_tok)
                ptoks.append(dtick("out", POOL.dma_start(out=ss_o[l].rearrange("i h kk v -> kk (i h) v"), in_=Ssm[:].rearrange("p i h v -> p (i h) v"))))
            k.barrier(pool_tokens=ptoks)
        k.es_cur = es

        with contextlib.ExitStack() as pes:
            k.es_cur = pes
            mt = sb("mt", [128, 2, 4, NT], F32)
            mt_free = [None, None]
            m_tok = None
            for m in range(8):
                w = k.wnext("mrg")
                bks = [k.bank_next() for _ in range(4)]
                srcs = [h_bf, h_bf, o_att, o_h]
                for bi in range(4):
                    wait("pe", bks[bi].free)
                wait("pe", h_tok); wait("pe", oa_tok); wait("pe", last_o)
                for bi in range(4):
                    for kc in range(KC):
                        ins = PE.matmul(bks[bi].ap[:, 0:NT], lhsT=w[:, kc, 128 * bi:128 * bi + 128], rhs=srcs[bi][:, kc, 0:NT], start=(kc == 0), stop=(kc == KC - 1))
                r_t = k.wdone(ins)
                for bi in range(4):
                    bks[bi].ready = r_t
                s = m % 2
                wait("act", r_t); wait("act", mt_free[s])
                ACT.activation(out=mt[:, s, 0, 0:NT], in_=bks[0].ap[:, 0:NT], func=AF.Sigmoid)
                a_t = tick("act", ACT.activation(out=mt[:, s, 1, 0:NT], in_=bks[1].ap[:, 0:NT], func=AF.Sigmoid))
                bks[0].free = bks[1].free = a_t
                wait("dve", a_t); wait("dve", r_t)
                DVE.tensor_tensor(out=mt[:, s, 2, 0:NT], in0=mt[:, s, 0, 0:NT], in1=bks[2].ap[:, 0:NT], op=ALU.mult)
                d_t = tick("dve", DVE.tensor_tensor(out=mt[:, s, 3, 0:NT], in0=mt[:, s, 1, 0:NT], in1=bks[3].ap[:, 0:NT], op=ALU.mult))
                bks[2].free = bks[3].free = d_t
                m_tok = tick("dve", DVE.tensor_tensor(out=merged[:, m, 0:NT], in0=mt[:, s, 2, 0:NT], in1=mt[:, s, 3, 0:NT], op=ALU.add))
                mt_free[s] = m_tok
            for g in range(2):
                w = k.wnext("wo")
                for j in range(4):
                    m = 4 * g + j
                    bk = k.bank_next()
                    wait("pe", bk.free); wait("pe", m_tok)
                    for kc in range(KC):
                        ins = PE.matmul(bk.ap[:, 0:NT], lhsT=w[:, kc, 128 * j:128 * j + 128], rhs=merged[:, kc, 0:NT], start=(kc == 0), stop=(kc == KC - 1))
                    bk.ready = tick("pe", ins)
                    wait("dve", bk.ready)
                    for (c0, c1, sq) in T.segs:
                        ins = DVE.scalar_tensor_tensor(out=x_fm[:, m, c0:c1], in0=bk.ap[:, c0:c1], scalar=MOD[:, l, 5, m, sq:sq + 1],
                                                       in1=x_fm[:, m, c0:c1], op0=ALU.mult, op1=ALU.add)
                    bk.free = tick("dve", ins)
                k.wfree_tok[k.w_cur] = bk.ready
            k.barrier()
        k.es_cur = es

    def final_out(T):
        NT, nblk, bp = T.NT, T.nblk, T.bp
        with contextlib.ExitStack() as pes:
            k.es_cur = pes
            x_tmp = sb("x_tmp", [128, KC, NT], F32)
            ystage = sb("ystage", [128, nblk, D], F32)
            yt = norm_phase(T, 0, None, None, x_tmp)
            wait("pe", yt)
            et = None
            for b in range(nblk):
                for cg in range(2):
                    bk = k.bank_next()
                    wait("pe", bk.free)
                    for j in range(4):
                        ins = PE.transpose(out=bk.ap[0:bp, 128 * j:128 * j + 128], in_=x_tmp[:, 4 * cg + j, b * bp:(b + 1) * bp], identity=identF[:])
                    bk.ready = tick("pe", ins)
                    et = evac_copy(ystage[0:bp, b, 512 * cg:512 * cg + 512], bk.ap[0:bp, 0:512], bk)
                    wait("pool", et)
            wait("pool", ("act", k.cnt["act"])) if k.cnt["act"] else None
            wait("pool", ("dve", k.cnt["dve"])) if k.cnt["dve"] else None
            if T.kind == "p":
                dst = yp[T.t * TT:(T.t + 1) * TT, :].rearrange("(b p) f -> p b f", p=128)
            else:
                dst = ys.rearrange("(b p) f -> p b f", p=bp)
            ot_ = dtick("out", POOL.dma_start(out=dst, in_=ystage[0:bp]))
            k.barrier(pool_tokens=[ot_])
        k.es_cur = es

    DVE.memset(S_all[:], 0.0)
    for T in tiles:
        load_x(T)
        for l in range(DEPTH):
            ht = norm_mod(T, l, 1, 0)
            ffn_phase(T, l, 1, ht)
            ht = norm_mod(T, l, 4, 3)
            mixing(T, l, ht)
            ht = norm_mod(T, l, 7, 6)
            ffn_phase(T, l, 2, ht)
        final_out(T)
    for s in ("out", "st0", "st1", "st2", "kvst") + (("dbg",) if cfg.DEBUG else ()):
        if k.cnt[s]:
            POOL.wait_ge(k.sem[s], k.cnt[s])
    assert k.w_used == len(k.wseq)
    es.close()
    return nc


_WNAMES = ["w_ada", "b_ada", "g_ffn1", "w_ffn1_gu", "w_ffn1_d", "g_mix", "w_in", "att_lambda", "g_att_sub",
           "hg_lb_logits", "g_hg_norm", "w_br_att", "w_br_hg", "w_out", "g_ffn2", "w_ffn2_gu", "w_ffn2_d", "g_final"]


def run(cfg, inputs, n_cores=8):
    nc = build(cfg)
    NS, TS = cfg.NS, cfg.TS
    f = lambda a: np.ascontiguousarray(np.asarray(a, dtype=np.float32))
    in_maps = []
    nb = inputs["x_prompt"].shape[0]
    for c in range(n_cores):
        sq = (c * nb) // n_cores
        s0 = c * NS
        m = {
            "xp": f(inputs["x_prompt"][sq]),
            "xs": f(inputs["x_sample"][s0:s0 + NS]).reshape(NS * TS, D),
            "ck": f(np.asarray(inputs["cache_k"])[:, s0:s0 + NS].reshape(cfg.DEPTH, NS, cfg.PAST, D)),
            "cv": f(np.asarray(inputs["cache_v"])[:, s0:s0 + NS].reshape(cfg.DEPTH, NS, cfg.PAST, D)),
            "st": f(np.asarray(inputs["state_hgrn"])[:, s0:s0 + NS]),
            "cc": f(np.concatenate([np.asarray(inputs["c_prompt"])[sq:sq + 1], np.asarray(inputs["c_sample"])[s0:s0 + NS]], 0)),
        }
        for n in _WNAMES:
            m[n] = f(inputs[n])
        in_maps.append(m)
    res = run_bass_kernel_spmd(nc, in_maps, core_ids=list(range(n_cores)))
    R = res.results
    run.last = R
    per = n_cores // nb
    DEPTH = cfg.DEPTH
    y_prompt = np.stack([R[per * b]["yp"] for b in range(nb)], 0)
    k_prompt = np.stack([R[per * b]["kp"] for b in range(nb)], 1).reshape(DEPTH, nb, cfg.SEQ, NH, 128)
    v_prompt = np.stack([R[per * b]["vp"] for b in range(nb)], 1).reshape(DEPTH, nb, cfg.SEQ, NH, 128)
    s_prompt = np.stack([R[per * b]["sp"] for b in range(nb)], 1)
    y_sample = np.concatenate([R[c]["ys"].reshape(NS, TS, D) for c in range(n_cores)], 0)
    k_sample = np.concatenate([R[c]["ks"].reshape(DEPTH, NS, TS, NH, 128) for c in range(n_cores)], 1)
    v_sample = np.concatenate([R[c]["vs"].reshape(DEPTH, NS, TS, NH, 128) for c in range(n_cores)], 1)
    s_sample = np.concatenate([R[c]["ss"] for c in range(n_cores)], 1)
    return tuple(np.ascontiguousarray(a, dtype=np.float32) for a in
                 (y_prompt, y_sample, k_prompt, v_prompt, s_prompt, k_sample, v_sample, s_sample))


def kernel(**inputs):
    cfg = Cfg()
    return run(cfg, inputs, 8)
```

```python
import contextlib
import math
import numpy as np
import concourse.bass as bass
import concourse.mybir as mybir
from concourse.bass_utils import run_bass_kernel_spmd

F32 = mybir.dt.float32
BF16 = mybir.dt.bfloat16
AF = mybir.ActivationFunctionType
ALU = mybir.AluOpType

D = 1024
DFF = 2816
NH = 8
KC = 8
FC = 22
DIN = 9216
EPS = 1e-6
TINY = 1e-30
NWG = 62
WSLOT = 4096
NBUF = 4


class Cfg:
    def __init__(self, SEQ=8192, DEPTH=4, PAST=2048, TS=32, NS=2, TT=512):
        self.SEQ, self.DEPTH, self.PAST, self.TS, self.NS, self.TT = SEQ, DEPTH, PAST, TS, NS, TT
        self.DEBUG = False


class Tile:
    pass


class B:
    def __init__(self, cfg):
        self.cfg = cfg
        self.nc = bass.Bass("TRN2", target_bir_lowering=False)
        self.es = contextlib.ExitStack()
        self.cnt = {}
        self.sem = {}
        self.waited = {}
        self.eng = {"pe": self.nc.tensor, "act": self.nc.scalar, "dve": self.nc.vector,
                    "pool": self.nc.gpsimd, "sp": self.nc.sync}
        for e in ("pe", "act", "dve", "pool"):
            self.newsem(e)
        self.bar_n = 0
        self.newsem("bar")
        self.last_tok = {}
        self.last_ins = {}
        self.skip_sync = {}

    def newsem(self, name):
        self.sem[name] = self.es.enter_context(self.nc.semaphore(name))
        self.cnt[name] = 0
        return name

    def tick(self, E, ins):
        if self.last_ins.get(E) is ins and self.last_tok.get(E) is not None:
            return self.last_tok[E]
        ins.then_inc(self.sem[E], 1)
        self.cnt[E] += 1
        tok = (E, self.cnt[E])
        if self.last_ins.get(E) is ins:
            self.last_tok[E] = tok
        return tok

    def pre_issue(self, E):
        li = self.last_ins.get(E)
        if li is None:
            return
        tok = self.tick(E, li)
        k = (E, E)
        if self.waited.get(k, 0) < tok[1]:
            self.eng[E].wait_ge(self.sem[E], tok[1])
            self.waited[k] = tok[1]

    def post_issue(self, E, ins):
        self.last_ins[E] = ins
        self.last_tok[E] = None

    def dtick(self, S, ins):
        ins.then_inc(self.sem[S], 16)
        self.cnt[S] += 16
        return (S, self.cnt[S])

    def wait(self, who, tok):
        if tok is None:
            return
        S, v = tok
        if S == who:
            return
        k = (who, S)
        if self.waited.get(k, 0) >= v:
            return
        self.eng[who].wait_ge(self.sem[S], v)
        self.waited[k] = v

    def sb(self, name, shape, dt):
        self.uid = getattr(self, "uid", 0) + 1
        return self.es_cur.enter_context(self.nc.sbuf_tensor("%s_%d" % (name, self.uid), shape, dt))

    def barrier(self, pool_tokens=()):
        nc = self.nc
        for t in pool_tokens:
            self.wait("pool", t)
        self.pre_issue("act")
        nc.scalar.copy(out=self.scrA[0:1, 0:1], in_=self.scrA[0:1, 1:2]).then_inc(self.sem["bar"], 1)
        self.pre_issue("dve")
        nc.vector.memset(self.scrV[0:1, 0:1], 0.0).then_inc(self.sem["bar"], 1)
        nc.gpsimd.memset(self.scrP[0:1, 0:1], 0.0).then_inc(self.sem["bar"], 1)
        self.bar_n += 3
        for e in ("act", "dve", "pool"):
            self.eng[e].wait_ge(self.sem["bar"], self.bar_n)
        self.last_ins["act"] = None
        self.last_ins["dve"] = None

    def bank_next(self):
        b = self.banks[self.bank_rr % 8]
        self.bank_rr += 1
        return b

    def wnext(self, kind):
        nc = self.nc
        i = self.w_used
        assert self.wseq[i][0] == kind, (self.wseq[i], kind)
        upto = min(i + NBUF - 1, len(self.wseq) - 1)
        while self.w_issued <= upto:
            j = self.w_issued
            slot = j % NBUF
            if j >= NBUF:
                self.wait("sp", self.wfree_tok[j - NBUF])
            _, l, gi, nk, ncol, first = self.wseq[j]
            if first:
                self.wait("sp", self.conv_tok[l])
            src = self.wscr[l * NWG + gi, :, 0:nk * ncol]
            ins = nc.sync.dma_start(out=self.wbuf[:, slot, 0:nk * ncol], in_=src)
            self.wld_tok[j] = self.dtick("wld%d" % slot, ins)
            self.w_issued += 1
        self.wait("pe", self.wld_tok[i])
        _, l, gi, nk, ncol, _ = self.wseq[i]
        self.w_used += 1
        self.w_cur = i
        return self.wbuf[:, i % NBUF, 0:nk * ncol].rearrange("p (k c) -> p k c", k=nk)

    def wdone(self, ins):
        self.wfree_tok[self.w_cur] = self.tick("pe", ins)
        return self.wfree_tok[self.w_cur]


class EngProxy:
    def __init__(self, k, name, eng):
        self._k, self._name, self._eng = k, name, eng

    def __getattr__(self, attr):
        f = getattr(self._eng, attr)
        if attr in ("wait_ge",):
            return f
        k, name = self._k, self._name

        def wrapped(*a, **kw):
            if k.skip_sync.get(name):
                k.skip_sync[name] = False
            else:
                k.pre_issue(name)
            ins = f(*a, **kw)
            k.post_issue(name, ins)
            return ins
        return wrapped


def _wgroups():
    g = []
    for i in range(11):
        g.append(("gu1", i, 8, 512 if i < 10 else 512))
    for i in range(8):
        g.append(("d1", i, 22, 128))
    for i in range(4):
        g.append(("qk", i, 8, 512))
    for i in range(2):
        g.append(("v", i, 8, 512))
    for i in range(8):
        g.append(("hg", i, 8, 512))
    for i in range(8):
        g.append(("mrg", i, 8, 512))
    for i in range(2):
        g.append(("wo", i, 8, 512))
    for i in range(11):
        g.append(("gu2", i, 8, 512))
    for i in range(8):
        g.append(("d2", i, 22, 128))
    assert len(g) == NWG
    return g


def build(cfg):
    k = B(cfg)
    nc = k.nc
    es = k.es
    SEQ, DEPTH, PAST, TS, NS, TT = cfg.SEQ, cfg.DEPTH, cfg.PAST, cfg.TS, cfg.NS, cfg.TT
    NTS = NS * TS
    NPT = SEQ // TT
    NSQ = 1 + NS

    def din(name, shape):
        return nc.dram_tensor(name, list(shape), F32, kind="ExternalInput").ap()

    def dout(name, shape):
        return nc.dram_tensor(name, list(shape), F32, kind="ExternalOutput").ap()

    xp = din("xp", [SEQ, D]); xs = din("xs", [NTS, D])
    ck = din("ck", [DEPTH, NS, PAST, D]); cv = din("cv", [DEPTH, NS, PAST, D])
    st = din("st", [DEPTH, NS, NH, 128, 128]); cc = din("cc", [NSQ, D])
    w_ada = din("w_ada", [DEPTH, D, 9 * D]); b_ada = din("b_ada", [DEPTH, 9 * D])
    g_ffn1 = din("g_ffn1", [DEPTH, D]); w_gu1 = din("w_ffn1_gu", [DEPTH, D, 2 * DFF]); w_d1 = din("w_ffn1_d", [DEPTH, DFF, D])
    g_mix = din("g_mix", [DEPTH, D]); w_in = din("w_in", [DEPTH, D, DIN])
    att_lambda = din("att_lambda", [DEPTH, 4, 64]); g_att_sub = din("g_att_sub", [DEPTH, 128])
    hg_lb = din("hg_lb_logits", [DEPTH, D]); g_hg_norm = din("g_hg_norm", [DEPTH, 128])
    w_ba = din("w_br_att", [DEPTH, D, D]); w_bh = din("w_br_hg", [DEPTH, D, D]); w_o = din("w_out", [DEPTH, D, D])
    g_ffn2 = din("g_ffn2", [DEPTH, D]); w_gu2 = din("w_ffn2_gu", [DEPTH, D, 2 * DFF]); w_d2 = din("w_ffn2_d", [DEPTH, DFF, D])
    g_final = din("g_final", [D])

    yp = dout("yp", [SEQ, D]); ys = dout("ys", [NTS, D])
    kp = dout("kp", [DEPTH, SEQ, D]); vp = dout("vp", [DEPTH, SEQ, D]); sp_o = dout("sp", [DEPTH, NH, 128, 128])
    ks = dout("ks", [DEPTH, NTS, D]); vs = dout("vs", [DEPTH, NTS, D]); ss_o = dout("ss", [DEPTH, NS, NH, 128, 128])

    if cfg.DEBUG:
        dbg_att = dout("dbg_att", [128, NH, TT]); dbg_h = dout("dbg_h", [128, NH, TT])
        dbg_mod = dout("dbg_mod", [128, DEPTH * 9 * KC * NSQ]); dbg_x = dout("dbg_x", [128, KC * TT]); dbg_h1 = dout("dbg_h1", [128, KC * TT])
        dbg_hid = dout("dbg_hid", [128, FC * TT])
        k.newsem("dbg")
    k.wscr = nc.dram_tensor("wscr", [DEPTH * NWG, 128, WSLOT], BF16, kind="Internal").ap()
    kscr = nc.dram_tensor("kscr", [DEPTH, NPT, 128, NH * TT], BF16, kind="Internal").ap()
    vscr = nc.dram_tensor("vscr", [DEPTH, NPT, 128, NH * TT], BF16, kind="Internal").ap()

    k.es_cur = es
    sb = k.sb
    x_fm = sb("x_fm", [128, KC, TT], F32)
    h_bf = sb("h_bf", [128, KC, TT], BF16)
    o_att = sb("o_att", [128, NH, TT], BF16)
    o_h = sb("o_h", [128, NH, TT], BF16)
    merged = sb("merged", [128, KC, TT], BF16)
    rs = sb("rs", [128, TT], F32)
    S_all = sb("S_all", [128, DEPTH, NH, 128], F32)
    Sb = sb("Sb", [128, NH, 128], BF16)
    k.wbuf = sb("wbuf", [128, NBUF, WSLOT], BF16)
    identF = sb("identF", [128, 128], F32)
    identB = sb("identB", [128, 128], BF16)
    onesB = sb("onesB", [128, 128], BF16)
    onesF = sb("onesF", [128, 128], F32)
    triP = sb("triP", [128, 128], F32)
    triS = sb("triS", [128, 128], F32)
    M0p = sb("M0p", [128, TT], F32)
    M0s = sb("M0s", [128, NTS], F32)
    MOD = sb("MOD", [128, DEPTH, 9, KC, NSQ], F32)
    gfin = sb("gfin", [128, KC], F32)
    gatt = sb("gatt", [128, DEPTH], F32)
    ghg = sb("ghg", [128, DEPTH], F32)
    lamt = sb("lamt", [128, DEPTH], F32)
    lbt = sb("lbt", [128, DEPTH, NH], F32)
    omlt = sb("omlt", [128, DEPTH, NH], F32)
    nomlt = sb("nomlt", [128, DEPTH, NH], F32)
    k.scrA = sb("scrA", [128, 4], F32); k.scrV = sb("scrV", [128, 4], F32); k.scrP = sb("scrP", [128, 4], F32)

    pbs = [es.enter_context(nc.psum_tensor("pb%d" % i, [128, 2, 512], F32)) for i in range(4)]

    class Bank:
        pass
    k.banks = []
    for i in range(8):
        b = Bank()
        b.ap = pbs[i // 2][:, i % 2, :]
        b.pair = pbs[i // 2]
        b.ready = None
        b.free = None
        k.banks.append(b)
    k.bank_rr = 0

    for i in range(NBUF):
        k.newsem("wld%d" % i)
    for s in ("wad0", "wad1", "cv0", "ld", "sld", "st0", "st1", "st2", "xld", "kvst", "kv0", "kv1", "kv2", "kv3", "out"):
        k.newsem(s)

    tick, dtick, wait = k.tick, k.dtick, k.wait
    PE, POOL = nc.tensor, nc.gpsimd
    ACT = EngProxy(k, "act", nc.scalar)
    DVE = EngProxy(k, "dve", nc.vector)

    groups = _wgroups()
    k.conv_tok = {}
    for l in range(DEPTH):
        k.newsem("cvl%d" % l)
        gu = {1: w_gu1[l].rearrange("(kc p) c -> p kc c", p=128), 2: w_gu2[l].rearrange("(kc p) c -> p kc c", p=128)}
        dd = {1: w_d1[l].rearrange("(kc p) c -> p kc c", p=128), 2: w_d2[l].rearrange("(kc p) c -> p kc c", p=128)}
        wi = w_in[l].rearrange("(kc p) c -> p kc c", p=128)
        wba = w_ba[l].rearrange("(kc p) c -> p kc c", p=128)
        wbh = w_bh[l].rearrange("(kc p) c -> p kc c", p=128)
        wo = w_o[l].rearrange("(kc p) c -> p kc c", p=128)
        for gi, (kind, i, nk, ncol) in enumerate(groups):
            dst = k.wscr[l * NWG + gi, :, 0:nk * ncol].rearrange("p (k c) -> p k c", k=nk)
            parts = []
            if kind in ("gu1", "gu2"):
                W = gu[1 if kind == "gu1" else 2]
                parts = [(0, 256, W[:, :, 256 * i:256 * i + 256]), (256, 512, W[:, :, DFF + 256 * i:DFF + 256 * i + 256])]
            elif kind in ("d1", "d2"):
                W = dd[1 if kind == "d1" else 2]
                parts = [(0, 128, W[:, :, 128 * i:128 * i + 128])]
            elif kind == "qk":
                parts = [(0, 512, wi[:, :, 512 * i:512 * i + 512])]
            elif kind == "v":
                parts = [(0, 512, wi[:, :, 2048 + 512 * i:2048 + 512 * i + 512])]
            elif kind == "hg":
                parts = [(0, 512, wi[:, :, 3072 + 512 * i:3072 + 512 * i + 512])]
            elif kind == "mrg":
                parts = [(0, 128, wi[:, :, 7168 + 128 * i:7168 + 128 * i + 128]),
                         (128, 256, wi[:, :, 8192 + 128 * i:8192 + 128 * i + 128]),
                         (256, 384, wba[:, :, 128 * i:128 * i + 128]),
                         (384, 512, wbh[:, :, 128 * i:128 * i + 128])]
            elif kind == "wo":
                parts = [(0, 512, wo[:, :, 512 * i:512 * i + 512])]
            for (c0, c1, src) in parts:
                ins = POOL.dma_start(out=dst[:, :, c0:c1], in_=src)
                k.conv_tok[l] = dtick("cvl%d" % l, ins)

    tiles = []
    for t in range(NPT):
        T = Tile(); T.kind = "p"; T.t = t; T.NT = TT; T.nblk = TT // 128; T.bp = 128
        T.segs = [(0, TT, 0)]
        tiles.append(T)
    T = Tile(); T.kind = "s"; T.t = 0; T.NT = NTS; T.nblk = NS; T.bp = TS
    T.segs = [(i * TS, (i + 1) * TS, 1 + i) for i in range(NS)]
    tiles.append(T)
    k.wseq = []
    for ti, T in enumerate(tiles):
        for l in range(DEPTH):
            for gi, (kind, i, nk, ncol) in enumerate(groups):
                k.wseq.append((kind, l, gi, nk, ncol, ti == 0 and gi == 0))
    k.w_used = 0; k.w_issued = 0; k.wld_tok = {}; k.wfree_tok = {}

    POOL.memset(identF[:], 1.0)
    POOL.affine_select(out=identF[:], in_=identF[:], pattern=[[-1, 128]], compare_op=ALU.is_equal, fill=0.0, base=0, channel_multiplier=1)
    POOL.tensor_copy(out=identB[:], in_=identF[:])
    POOL.memset(onesB[:], 1.0)
    POOL.memset(onesF[:], 1.0)
    POOL.memset(triP[:], 1.0)
    POOL.affine_select(out=triP[:], in_=triP[:], pattern=[[1, 128]], compare_op=ALU.is_ge, fill=0.0, base=0, channel_multiplier=-1)
    POOL.tensor_copy(out=triS[:], in_=triP[:])
    POOL.memset(triP[0:64, 64:128], 0.0)
    POOL.memset(triS[0:32, 32:64], 0.0)
    POOL.memset(M0p[:], 1.0)
    for c in range(TT // 64):
        POOL.memset(M0p[:, 64 * c:64 * c + 1], 0.0)
    POOL.memset(M0s[:], 1.0)
    for c in range(NS):
        POOL.memset(M0s[:, TS * c:TS * c + 1], 0.0)
    POOL.memset(k.scrP[:], 0.0)
    c_tok = tick("pool", POOL.memset(k.scrP[:, 0:1], 0.0))
    DVE.memset(k.scrV[:], 0.0)
    wait("act", c_tok)
    ACT.copy(out=k.scrA[:], in_=identF[:, 0:4])

    with contextlib.ExitStack() as pes:
        k.es_cur = pes
        cT = sb("cT", [128, KC, NSQ], F32)
        cact = sb("cact", [128, KC, NSQ], F32)
        badaT = sb("badaT", [128, DEPTH, 72], F32)
        gT = sb("gT", [128, 3, DEPTH, KC], F32)
        lbl = sb("lbl", [128, DEPTH, NH], F32)
        lam_in = sb("lam_in", [128, DEPTH, 4, 64], F32)
        lam_w = sb("lam_w", [128, DEPTH, 2, 64], F32)
        lam_s = sb("lam_s", [128, DEPTH, 2], F32)
        wad = sb("wad", [128, 2, KC, 1024], F32)
        ld = []
        for s_ in range(NSQ):
            ld.append(dtick("ld", nc.sync.dma_start(out=cT[:, :, s_], in_=cc[s_].rearrange("(c p) -> p c", p=128), allow_slow_non_contiguous=True)))
        for l in range(DEPTH):
            ld.append(dtick("ld", nc.sync.dma_start(out=badaT[:, l, :], in_=b_ada[l].rearrange("(j p) -> p j", p=128), allow_slow_non_contiguous=True)))
            for gi, gsrc in enumerate((g_ffn1, g_mix, g_ffn2)):
                ld.append(dtick("ld", nc.sync.dma_start(out=gT[:, gi, l, :], in_=gsrc[l].rearrange("(c p) -> p c", p=128), allow_slow_non_contiguous=True)))
            ld.append(dtick("ld", nc.sync.dma_start(out=lbl[:, l, :], in_=hg_lb[l].rearrange("(h p) -> p h", p=128), allow_slow_non_contiguous=True)))
        ld.append(dtick("ld", nc.sync.dma_start(out=gfin[:], in_=g_final.rearrange("(c p) -> p c", p=128), allow_slow_non_contiguous=True)))
        ld.append(dtick("ld", nc.sync.dma_start(out=gatt[:], in_=g_att_sub.rearrange("l p -> p l"), allow_slow_non_contiguous=True)))
        ld.append(dtick("ld", nc.sync.dma_start(out=ghg[:], in_=g_hg_norm.rearrange("l p -> p l"), allow_slow_non_contiguous=True)))
        ld.append(dtick("ld", nc.sync.dma_start(out=lam_in[:].rearrange("p l a b -> p (l a b)"),
                                                 in_=att_lambda.rearrange("l a b -> (l a b)").partition_broadcast(128))))
        ldall = ld[-1]
        wait("dve", ldall); wait("act", ldall)
        ct = tick("act", ACT.activation(out=cact[:], in_=cT[:], func=AF.Silu))
        for l in range(DEPTH):
            DVE.tensor_tensor(out=lam_w[:, l, 0, :], in0=lam_in[:, l, 0, :], in1=lam_in[:, l, 1, :], op=ALU.mult)
            DVE.tensor_tensor(out=lam_w[:, l, 1, :], in0=lam_in[:, l, 2, :], in1=lam_in[:, l, 3, :], op=ALU.mult)
        dt_ = tick("dve", DVE.reduce_sum(out=lam_s[:].rearrange("p l a -> p (l a)"), in_=lam_w[:].rearrange("p l a b -> p (l a) b"), axis=mybir.AxisListType.X))
        wait("act", dt_)
        at_ = tick("act", ACT.activation(out=lam_s[:], in_=lam_s[:], func=AF.Exp))
        wait("dve", at_)
        for l in range(DEPTH):
            lam_init = 0.8 - 0.6 * math.exp(-0.3 * l)
            DVE.tensor_tensor(out=lamt[:, l:l + 1], in0=lam_s[:, l, 1:2], in1=lam_s[:, l, 0:1], op=ALU.subtract)
            DVE.tensor_scalar(out=lamt[:, l:l + 1], in0=lamt[:, l:l + 1], scalar1=-lam_init, scalar2=None, op0=ALU.add)
            DVE.tensor_scalar(out=gatt[:, l:l + 1], in0=gatt[:, l:l + 1], scalar1=(1.0 - lam_init) * math.sqrt(128.0), scalar2=None, op0=ALU.mult)
        DVE.tensor_scalar(out=ghg[:], in0=ghg[:], scalar1=math.sqrt(128.0), scalar2=None, op0=ALU.mult)
        DVE.tensor_scalar(out=gfin[:], in0=gfin[:], scalar1=32.0, scalar2=None, op0=ALU.mult)
        at2 = tick("act", ACT.activation(out=lbl[:], in_=lbl[:], func=AF.Exp))
        wait("dve", at2)
        lsum = lam_w[:, 0, 0, 0:NH]
        DVE.tensor_copy(out=lsum, in_=lbl[:, 0, :])
        for l in range(1, DEPTH):
            DVE.tensor_tensor(out=lsum, in0=lsum, in1=lbl[:, l, :], op=ALU.add)
        DVE.reciprocal(out=lsum, in_=lsum)
        DVE.memset(lbt[:, 0, :], 0.0)
        for l in range(1, DEPTH):
            DVE.tensor_tensor(out=lbl[:, l, :], in0=lbl[:, l, :], in1=lsum, op=ALU.mult)
            DVE.tensor_tensor(out=lbt[:, l, :], in0=lbt[:, l - 1, :], in1=lbl[:, l, :], op=ALU.add)
        DVE.tensor_scalar(out=omlt[:], in0=lbt[:], scalar1=-1.0, scalar2=1.0, op0=ALU.mult, op1=ALU.add)
        DVE.tensor_scalar(out=nomlt[:], in0=omlt[:], scalar1=-1.0, scalar2=None, op0=ALU.mult)
        wait("pe", ct)
        wtok = [None, None]
        wfree = [None, None]
        gidx = 0
        for l in range(DEPTH):
            wv = w_ada[l].rearrange("(kc p) c -> p kc c", p=128)
            bk = k.bank_next()
            wait("pe", bk.free)
            outv = bk.ap[:, 0:72 * NSQ].rearrange("p (j s) -> p j s", s=NSQ)
            for g9 in range(9):
                slot = gidx % 2
                wait("sp", wfree[slot])
                wtok[slot] = dtick("wad%d" % slot, nc.sync.dma_start(out=wad[:, slot], in_=wv[:, :, 1024 * g9:1024 * g9 + 1024]))
                wait("pe", wtok[slot])
                for f in range(8):
                    for kc in range(KC):
                        ins = PE.matmul(outv[:, g9 * 8 + f, :], lhsT=wad[:, slot, kc, 128 * f:128 * f + 128], rhs=cact[:, kc, :],
                                        start=(kc == 0), stop=(kc == KC - 1))
                wfree[slot] = tick("pe", ins)
                gidx += 1
            bk.ready = wfree[(gidx - 1) % 2]
            wait("dve", bk.ready)
            ins = DVE.tensor_tensor(out=MOD[:, l].rearrange("p j c s -> p (j c) s"), in0=outv,
                                    in1=badaT[:, l, :].unsqueeze(2).to_broadcast([128, 72, NSQ]), op=ALU.add)
            bk.free = tick("dve", ins)
            for (j_sc, gi_, j_g, half) in ((1, 0, 2, 0.5), (4, 1, 5, 1.0), (7, 2, 8, 0.5)):
                DVE.tensor_scalar(out=MOD[:, l, j_sc], in0=MOD[:, l, j_sc], scalar1=1.0, scalar2=32.0, op0=ALU.add, op1=ALU.mult)
                DVE.tensor_tensor(out=MOD[:, l, j_sc], in0=MOD[:, l, j_sc],
                                  in1=gT[:, gi_, l, :].unsqueeze(2).to_broadcast([128, KC, NSQ]), op=ALU.mult)
                if half != 1.0:
                    DVE.tensor_scalar(out=MOD[:, l, j_g], in0=MOD[:, l, j_g], scalar1=half, scalar2=None, op0=ALU.mult)
        if cfg.DEBUG:
            wait("pool", ("dve", k.cnt["dve"]))
            dmt = dtick("dbg", POOL.dma_start(out=dbg_mod, in_=MOD[:].rearrange("p l j c s -> p (l j c s)")))
            wait("pool", dmt)
        k.barrier()
    k.es_cur = es


    evac_rr = [0]

    def evac_copy(out, in_, bank, scale=None):
        evac_rr[0] += 1
        if scale is not None or evac_rr[0] % 2 == 0:
            wait("act", bank.ready)
            if scale is not None:
                t = tick("act", ACT.mul(out=out, in_=in_, mul=scale))
            else:
                t = tick("act", ACT.copy(out=out, in_=in_))
        else:
            wait("dve", bank.ready)
            t = tick("dve", DVE.tensor_copy(out=out, in_=in_))
        bank.free = t
        return t

    def rstd_act(bank, out_ap, in_ap, eps_total):
        wait("act", bank.ready)
        ACT.activation(out=out_ap, in_=in_ap, func=AF.Ln, bias=eps_total, scale=1.0)
        t = tick("act", ACT.activation(out=out_ap, in_=out_ap, func=AF.Exp, scale=-0.5))
        bank.free = t
        return t

    def norm_phase(T, l, ja, jb, x_tmp):
        NT = T.NT
        sqt = None
        for c in range(KC):
            sqt = tick("act", ACT.activation(out=h_bf[:, c, 0:NT], in_=x_fm[:, c, 0:NT], func=AF.Square))
        bk = k.bank_next()
        wait("pe", bk.free); wait("pe", sqt)
        for c in range(KC):
            ins = PE.matmul(bk.ap[:, 0:NT], lhsT=onesB[:], rhs=h_bf[:, c, 0:NT], start=(c == 0), stop=(c == KC - 1))
        bk.ready = tick("pe", ins)
        wait("dve", rstd_act(bk, rs[:, 0:NT], bk.ap[:, 0:NT], 1024.0 * EPS))
        ht = None
        for c in range(KC):
            for (c0, c1, sq) in T.segs:
                a_ap = gfin[:, c:c + 1] if ja is None else MOD[:, l, ja, c, sq:sq + 1]
                ht = tick("dve", DVE.scalar_tensor_tensor(out=x_tmp[:, c, c0:c1], in0=x_fm[:, c, c0:c1], scalar=a_ap, in1=rs[:, c0:c1], op0=ALU.mult, op1=ALU.mult))
                if jb is not None:
                    ht = tick("dve", DVE.tensor_scalar(out=h_bf[:, c, c0:c1], in0=x_tmp[:, c, c0:c1], scalar1=MOD[:, l, jb, c, sq:sq + 1], scalar2=None, op0=ALU.add))
        return ht

    def norm_mod(T, l, ja, jb):
        with contextlib.ExitStack() as pes:
            k.es_cur = pes
            NT = T.NT
            x_tmp = sb("x_tmp", [128, KC, NT], F32)
            ht = norm_phase(T, l, ja, jb, x_tmp)
            if cfg.DEBUG and T.kind == "p" and T.t == 0 and l == 0 and ja == 1:
                wait("pool", ht)
                wait("pool", dtick("dbg", POOL.dma_start(out=dbg_h1, in_=h_bf[:].rearrange("p c t -> p (c t)"))))
            k.barrier()
        k.es_cur = es
        return ht

    def ffn_phase(T, l, which, h_tok):
        NT = T.NT
        jg = 2 if which == 1 else 8
        with contextlib.ExitStack() as pes:
            k.es_cur = pes
            hid = sb("hid", [128, FC, NT], BF16)
            stmp = sb("stmp", [128, 2, NT], F32)
            st_free = [None, None]
            hid_tok = None
            u = 0
            for g in range(11):
                w = k.wnext("gu%d" % which)
                for j in range(2):
                    jj = 2 * g + j
                    bA = k.bank_next(); bB = k.bank_next()
                    wait("pe", bA.free); wait("pe", bB.free); wait("pe", h_tok)
                    for kc in range(KC):
                        PE.matmul(bA.ap[:, 0:NT], lhsT=w[:, kc, 128 * j:128 * j + 128], rhs=h_bf[:, kc, 0:NT], start=(kc == 0), stop=(kc == KC - 1))
                    for kc in range(KC):
                        ins = PE.matmul(bB.ap[:, 0:NT], lhsT=w[:, kc, 256 + 128 * j:256 + 128 * j + 128], rhs=h_bf[:, kc, 0:NT], start=(kc == 0), stop=(kc == KC - 1))
                    if j == 1:
                        bA.ready = bB.ready = k.wdone(ins)
                    else:
                        bA.ready = bB.ready = tick("pe", ins)
                    s = u % 2
                    wait("act", bA.ready); wait("act", st_free[s])
                    k.skip_sync["act"] = True
                    a_t = tick("act", ACT.activation(out=stmp[:, s, 0:NT], in_=bA.ap[:, 0:NT], func=AF.Silu))
                    bA.free = a_t
                    wait("dve", a_t); wait("dve", bB.ready)
                    k.skip_sync["dve"] = True
                    d_t = tick("dve", DVE.tensor_tensor(out=hid[:, jj, 0:NT], in0=stmp[:, s, 0:NT], in1=bB.ap[:, 0:NT], op=ALU.mult))
                    bB.free = d_t; st_free[s] = d_t; hid_tok = d_t
                    u += 1
            if cfg.DEBUG and T.kind == "p" and T.t == 0 and l == 0 and which == 1:
                wait("pool", hid_tok)
                wait("pool", dtick("dbg", POOL.dma_start(out=dbg_hid, in_=hid[:].rearrange("p c t -> p (c t)"))))
            for m in range(8):
                w = k.wnext("d%d" % which)
                bk = k.bank_next()
                wait("pe", bk.free); wait("pe", hid_tok)
                for kc in range(FC):
                    ins = PE.matmul(bk.ap[:, 0:NT], lhsT=w[:, kc, :], rhs=hid[:, kc, 0:NT], start=(kc == 0), stop=(kc == FC - 1))
                bk.ready = k.wdone(ins)
                wait("dve", bk.ready)
                for (c0, c1, sq) in T.segs:
                    ins = DVE.scalar_tensor_tensor(out=x_fm[:, m, c0:c1], in0=bk.ap[:, c0:c1], scalar=MOD[:, l, jg, m, sq:sq + 1],
                                                   in1=x_fm[:, m, c0:c1], op0=ALU.mult, op1=ALU.add)
                bk.free = tick("dve", ins)
            k.barrier()
        k.es_cur = es

    def load_x(T):
        NT, nblk, bp = T.NT, T.nblk, T.bp
        with contextlib.ExitStack() as pes:
            k.es_cur = pes
            xin = sb("xin", [128, nblk, D], F32)
            if T.kind == "p":
                src = xp[T.t * TT:(T.t + 1) * TT, :].rearrange("(b p) f -> p b f", p=128)
            else:
                src = xs.rearrange("(b p) f -> p b f", p=bp)
            tok = dtick("xld", POOL.dma_start(out=xin[0:bp], in_=src))
            wait("pe", tok)
            for c in range(KC):
                bk = k.bank_next()
                wait("pe", bk.free)
                for b in range(nblk):
                    ins = PE.transpose(out=bk.ap[:, b * bp:(b + 1) * bp], in_=xin[0:bp, b, 128 * c:128 * c + 128], identity=identF[0:bp, 0:bp])
                bk.ready = tick("pe", ins)
                evac_copy(x_fm[:, c, 0:NT], bk.ap[:, 0:NT], bk)
            if cfg.DEBUG and T.kind == "p" and T.t == 0:
                wait("pool", ("dve", k.cnt["dve"])); wait("pool", ("act", k.cnt["act"]))
                wait("pool", dtick("dbg", POOL.dma_start(out=dbg_x, in_=x_fm[:].rearrange("p c t -> p (c t)"))))
            k.barrier()
        k.es_cur = es

    kvst_tok = {}

    def mixing(T, l, h_tok):
        NT, nblk, bp = T.NT, T.nblk, T.bp
        isp = T.kind == "p"
        kout = kp if isp else ks
        vout = vp if isp else vs
        r0 = T.t * TT if isp else 0
        with contextlib.ExitStack() as pes:
            k.es_cur = pes
            qT = sb("qT", [128, NH, NT], BF16)
            kT = sb("kT", [128, NH, NT], BF16)
            v_tm = sb("v_tm", [128, NH, nblk, 128], BF16)
            stage = sb("stage", [128, 3, 512], F32)
            Pb = sb("Pb", [128, 3, 2, NT], BF16)
            fin = sb("fin", [128, 5, NT], F32)
            finc = sb("finc", [128, 4, NT], F32)
            lacc = sb("lacc", [128, NT], F32)
            lacc_free = [None]
            p_free2 = [None, None, None]
            sqb = sb("sqb", [128, NT], BF16)
            rsa = sb("rsa", [128, NT], F32)
            st_tok = [None, None, None]
            st_n = [0]
            last_store = []

            def store_rows(bank, dst_ap, also=None):
                s = st_n[0] % 3
                st_n[0] += 1
                wait("act", bank.ready); wait("act", st_tok[s])
                t1 = tick("act", ACT.copy(out=stage[0:bp, s, :], in_=bank.ap[0:bp, 0:512]))
                if also is not None:
                    wait("dve", bank.ready)
                    wait("dve", t1)
                    bank.free = tick("dve", DVE.tensor_copy(out=also, in_=bank.ap[0:bp, 0:512].rearrange("p (h e) -> p h e", e=128)))
                    ret = bank.free
                else:
                    bank.free = t1
                    ret = None
                wait("pool", t1)
                st_tok[s] = dtick("st%d" % s, POOL.dma_start(out=dst_ap, in_=stage[0:bp, s, :]))
                last_store.append(st_tok[s])
                return ret

            kt_tok = None
            vt_tok = None
            q_tok = None
            for g in range(4):
                w = k.wnext("qk")
                ins = None
                for j in range(4):
                    bk = k.bank_next()
                    wait("pe", bk.free); wait("pe", h_tok)
                    for kc in range(KC):
                        ins = PE.matmul(bk.ap[:, 0:NT], lhsT=w[:, kc, 128 * j:128 * j + 128], rhs=h_bf[:, kc, 0:NT], start=(kc == 0), stop=(kc == KC - 1))
                    bk.ready = tick("pe", ins)
                    if g < 2:
                        q_tok = evac_copy(qT[:, 4 * g + j, 0:NT], bk.ap[:, 0:NT], bk, scale=0.125)
                    else:
                        wait("dve", bk.ready)
                        kt_tok = bk.free = tick("dve", DVE.tensor_copy(out=kT[:, 4 * (g - 2) + j, 0:NT], in_=bk.ap[:, 0:NT]))
                if g >= 2:
                    for b in range(nblk):
                        bk = k.bank_next()
                        wait("pe", bk.free)
                        for kc in range(KC):
                            ins = PE.matmul(bk.ap[0:bp, 0:512], lhsT=h_bf[:, kc, b * bp:(b + 1) * bp], rhs=w[:, kc, :], start=(kc == 0), stop=(kc == KC - 1))
                        bk.ready = tick("pe", ins)
                        store_rows(bk, kout[l, r0 + b * bp:r0 + (b + 1) * bp, 512 * (g - 2):512 * (g - 2) + 512])
                k.wfree_tok[k.w_cur] = bk.ready
            for g in range(2):
                w = k.wnext("v")
                for b in range(nblk):
                    bk = k.bank_next()
                    wait("pe", bk.free); wait("pe", h_tok)
                    for kc in range(KC):
                        ins = PE.matmul(bk.ap[0:bp, 0:512], lhsT=h_bf[:, kc, b * bp:(b + 1) * bp], rhs=w[:, kc, :], start=(kc == 0), stop=(kc == KC - 1))
                    bk.ready = tick("pe", ins)
                    vt_tok = store_rows(bk, vout[l, r0 + b * bp:r0 + (b + 1) * bp, 512 * g:512 * g + 512], also=v_tm[0:bp, 4 * g:4 * g + 4, b, :])
                k.wfree_tok[k.w_cur] = bk.ready
            if isp and T.t < NPT - 1:
                wait("pool", kt_tok); wait("pool", vt_tok)
                dtick("kvst", POOL.dma_start(out=kscr[l, T.t].rearrange("p (h t) -> p h t", h=NH), in_=kT[:, :, :]))
                kvst_tok[(l, T.t)] = dtick("kvst", POOL.dma_start(out=vscr[l, T.t], in_=v_tm[:].rearrange("p h b e -> p (h b e)")))
                last_store.append(kvst_tok[(l, T.t)])

            O = [k.banks[0], k.banks[1]]
            L = [k.banks[2], k.banks[3]]
            Sp = [(k.banks[4], k.banks[5]), (k.banks[6], k.banks[7])]
            if isp:
                kring = sb("kring", [128, 4, TT], BF16)
                vring = sb("vring", [128, 4, TT], BF16)
            ring_free = [None] * 4
            if not isp:
                NBK = PAST // 128
                kc_in = sb("kc_in", [128, 2, NBK, 128], F32)
                vc_in = sb("vc_in", [128, 2, NBK, 128], F32)
                kcT = sb("kcT", [128, 2, PAST], BF16)
                vcb = sb("vcb", [128, 2, NBK, 128], BF16)
            units = []
            if isp:
                for h in range(NH):
                    units.append(dict(h=h, q0=0, N=NT, seq=None))
            else:
                for i in range(NS):
                    for h in range(NH):
                        units.append(dict(h=h, q0=i * TS, N=TS, seq=i))
            loads = []
            if isp:
                for ui, u in enumerate(units):
                    for s in range(T.t):
                        loads.append((ui, s))
            load_tok = {}
            issued = [0]

            def issue_upto(n):
                while issued[0] < min(n, len(loads)):
                    i = issued[0]
                    ui, s = loads[i]
                    slot = i % 4
                    wait("pool", ring_free[slot]); wait("pool", kvst_tok[(l, s)])
                    h = units[ui]["h"]
                    dtick("kv%d" % slot, POOL.dma_start(out=kring[:, slot, :], in_=kscr[l, s][:, h * TT:(h + 1) * TT]))
                    load_tok[i] = dtick("kv%d" % slot, POOL.dma_start(out=vring[:, slot, :], in_=vscr[l, s][:, h * TT:(h + 1) * TT]))
                    issued[0] += 1

            cin_tok = {}
            cin_free = [None, None]
            cprep_tok = {}

            def cache_load(ui):
                u = units[ui]
                s = ui % 2
                wait("pool", cin_free[s])
                h = u["h"]; i = u["seq"]
                dtick("kv%d" % s, POOL.dma_start(out=kc_in[:, s], in_=ck[l, i, :, 128 * h:128 * h + 128].rearrange("(b p) d -> p b d", p=128)))
                cin_tok[ui] = dtick("kv%d" % s, POOL.dma_start(out=vc_in[:, s], in_=cv[l, i, :, 128 * h:128 * h + 128].rearrange("(b p) d -> p b d", p=128)))

            cprep_free = [None, None]

            def cache_prep(ui):
                s = ui % 2
                wait("pe", cin_tok[ui]); wait("dve", cin_tok[ui]); wait("act", cin_tok[ui])
                wait("dve", cprep_free[s]); wait("act", cprep_free[s])
                t = None
                for b4 in range(NBK // 4):
                    bk = k.bank_next()
                    wait("pe", bk.free)
                    for j in range(4):
                        ins = PE.transpose(out=bk.ap[:, 128 * j:128 * j + 128], in_=kc_in[:, s, 4 * b4 + j, :], identity=identF[:])
                    bk.ready = tick("pe", ins)
                    t = evac_copy(kcT[:, s, 512 * b4:512 * b4 + 512], bk.ap[:, :], bk)
                wait("dve", t)
                t2 = tick("dve", DVE.tensor_copy(out=vcb[:, s], in_=vc_in[:, s]))
                wait("act", t2)
                t3 = tick("act", ACT.copy(out=k.scrA[0:1, 2:3], in_=k.scrA[0:1, 1:2]))
                cin_free[s] = t3
                cprep_tok[ui] = t3

            pending = []
            pn = [0]
            sn = [0]
            p_free = [None, None, None]
            wait("pe", q_tok); wait("pe", kt_tok); wait("pe", vt_tok)
            if not isp:
                cache_load(0)
            for ui, u in enumerate(units):
                h, q0, N = u["h"], u["q0"], u["N"]
                blocks = []
                if isp:
                    base = sum(1 for (a, _) in loads if a < ui)
                    for s in range(T.t):
                        li = base + s
                        for b in range(4):
                            blocks.append(dict(kt=kring[:, li % 4, 128 * b:128 * b + 128], v=vring[:, li % 4, 128 * b:128 * b + 128], nk=128, qs=0, zero=False,
                                               li=li, last=(b == 3)))
                    for b in range(nblk):
                        blocks.append(dict(kt=kT[:, h, 128 * b:128 * b + 128], v=v_tm[:, h, b, :], nk=128, qs=128 * b, zero=True, li=None, last=False))
                else:
                    s2 = ui % 2
                    if ui + 1 < len(units):
                        cache_load(ui + 1)
                    cache_prep(ui)
                    wait("pe", cprep_tok[ui])
                    for b in range(NBK):
                        blocks.append(dict(kt=kcT[:, s2, 128 * b:128 * b + 128], v=vcb[:, s2, b, :], nk=128, qs=0, zero=False, li=None, last=False))
                    i = u["seq"]
                    blocks.append(dict(kt=kT[:, h, i * TS:(i + 1) * TS], v=v_tm[0:TS, h, i, :], nk=TS, qs=0, zero=False, li=None, last=False))
                nb = len(blocks)

                def emit_S(j):
                    bl = blocks[j]
                    pr = Sp[sn[0] % 2]
                    bl["pr"] = pr
                    sn[0] += 1
                    if bl["li"] is not None:
                        issue_upto(bl["li"] + 3)
                        wait("pe", load_tok[bl["li"]])
                    wait("pe", pr[0].free); wait("pe", pr[1].free)
                    nk, qs = bl["nk"], bl["qs"]
                    PE.matmul(pr[0].ap[0:nk, qs:N], lhsT=bl["kt"][0:64, :], rhs=qT[0:64, h, q0 + qs:q0 + N], start=True, stop=True)
                    ins = PE.matmul(pr[1].ap[0:nk, qs:N], lhsT=bl["kt"][64:128, :], rhs=qT[64:128, h, q0 + qs:q0 + N], start=True, stop=True)
                    pr[0].ready = pr[1].ready = tick("pe", ins)

                wait("pe", O[0].free); wait("pe", O[1].free); wait("pe", L[0].free); wait("pe", L[1].free)
                emit_S(0)
                for j in range(nb):
                    bl = blocks[j]
                    nk, qs = bl["nk"], bl["qs"]
                    pr = bl["pr"]
                    ps = pn[0] % 3
                    pn[0] += 1
                    wait("act", pr[0].ready); wait("act", p_free[ps]); wait("act", p_free2[ps])
                    if isp:
                        k.skip_sync["act"] = True
                    e_t = tick("act", ACT.activation(out=Pb[0:nk, ps, :, qs:N], in_=pr[0].pair[0:nk, :, qs:N], func=AF.Exp))
                    if bl["zero"]:
                        e_t = tick("act", ACT.mul(out=Pb[64:128, ps, :, qs:qs + 64], in_=Pb[64:128, ps, :, qs:qs + 64], mul=0.0))
                    pr[0].free = pr[1].free = e_t
                    if j + 1 < nb:
                        emit_S(j + 1)
                    wait("pe", e_t)
                    for m in range(2):
                        PE.matmul(O[m].ap[:, qs:N], lhsT=bl["v"], rhs=Pb[0:nk, ps, m, qs:N], start=(j == 0), stop=(j == nb - 1))
                    for m in ((1,) if isp else (0, 1)):
                        ins = PE.matmul(L[m].ap[:, qs:N], lhsT=onesB[0:nk, :], rhs=Pb[0:nk, ps, m, qs:N], start=(j == 0), stop=(j == nb - 1))
                    pv_t = tick("pe", ins)
                    p_free[ps] = pv_t
                    if isp:
                        wait("dve", e_t)
                        if j == 0:
                            wait("dve", lacc_free[0])
                            dacc_t = tick("dve", DVE.tensor_copy(out=lacc[:, 0:N], in_=Pb[:, ps, 0, 0:N]))
                        else:
                            dacc_t = tick("dve", DVE.tensor_tensor(out=lacc[:, qs:N], in0=lacc[:, qs:N], in1=Pb[:, ps, 0, qs:N], op=ALU.add))
                        p_free2[ps] = dacc_t
                    if bl["last"]:
                        ring_free[bl["li"] % 4] = pv_t
                    if j == 1 and pending:
                        pending.pop(0)()
                while pending:
                    pending.pop(0)()
                if isp:
                    wait("pe", dacc_t)
                    pv_t = tick("pe", PE.matmul(L[0].ap[:, 0:N], lhsT=onesF[:], rhs=lacc[:, 0:N], start=True, stop=True))
                    lacc_free[0] = pv_t
                wait("act", pv_t); wait("dve", pv_t)
                ACT.activation(out=finc[:, 2, 0:N], in_=L[0].ap[:, 0:N], func=AF.Ln)
                a_t = tick("act", ACT.activation(out=finc[:, 3, 0:N], in_=L[1].ap[:, 0:N], func=AF.Ln))
                a2_t = tick("act", ACT.activation(out=finc[:, 2:4, 0:N], in_=finc[:, 2:4, 0:N], func=AF.Exp, scale=-1.0))
                DVE.tensor_copy(out=finc[:, 0, 0:N], in_=O[0].ap[:, 0:N])
                d_t = tick("dve", DVE.tensor_copy(out=finc[:, 1, 0:N], in_=O[1].ap[:, 0:N]))
                L[0].free = L[1].free = a_t
                O[0].free = O[1].free = d_t
                wait("dve", a2_t)
                DVE.tensor_tensor(out=fin[:, 2, 0:N], in0=finc[:, 0, 0:N], in1=finc[:, 2, 0:N], op=ALU.mult)
                DVE.tensor_tensor(out=fin[:, 3, 0:N], in0=finc[:, 1, 0:N], in1=finc[:, 3, 0:N], op=ALU.mult)
                DVE.scalar_tensor_tensor(out=fin[:, 4, 0:N], in0=fin[:, 3, 0:N], scalar=lamt[:, l:l + 1], in1=fin[:, 2, 0:N], op0=ALU.mult, op1=ALU.add)
                sq_t = tick("dve", DVE.tensor_tensor(out=sqb[:, 0:N], in0=fin[:, 4, 0:N], in1=fin[:, 4, 0:N], op=ALU.mult))

                def fin2(h=h, q0=q0, N=N, sq_t=sq_t):
                    pr = Sp[sn[0] % 2]
                    sn[0] += 1
                    bk = pr[0]
                    wait("pe", pr[0].free); wait("pe", pr[1].free); wait("pe", sq_t)
                    ins = PE.matmul(bk.ap[:, 0:N], lhsT=onesB[:], rhs=sqb[:, 0:N], start=True, stop=True)
                    pr[0].ready = pr[1].ready = tick("pe", ins)
                    t = rstd_act(bk, rsa[:, 0:N], bk.ap[:, 0:N], 128.0 * EPS)
                    pr[0].free = pr[1].free = t
                    wait("dve", t)
                    return tick("dve", DVE.scalar_tensor_tensor(out=o_att[:, h, q0:q0 + N], in0=fin[:, 4, 0:N], scalar=gatt[:, l:l + 1], in1=rsa[:, 0:N], op0=ALU.mult, op1=ALU.mult))
                pending.append(fin2)
            oa_tok = None
            while pending:
                oa_tok = pending.pop(0)()
            if cfg.DEBUG and isp and T.t == 0 and l == 0:
                wait("pool", oa_tok)
                last_store.append(dtick("dbg", POOL.dma_start(out=dbg_att, in_=o_att[:])))
            k.barrier(pool_tokens=last_store)
        k.es_cur = es

        with contextlib.ExitStack() as pes:
            k.es_cur = pes
            if isp:
                Lc = 64; h2 = 32
                nch = NT // 64
                M0 = M0p
            else:
                Lc = TS; h2 = TS
                nch = NS
                M0 = M0s
            two = h2 < Lc
            hq = sb("hq", [128, NH, NT], F32)
            sigf = sb("sigf", [128, NH, NT], F32)
            hgt = sb("hgt", [128, NH, NT], BF16)
            hi_tm = sb("hi_tm", [64, NH, nch, 128], BF16)
            fw = sb("fw", [128, 10, NT], F32)
            qk4 = sb("qk4", [128, 2, 5, NT], BF16)
            kd_tm = sb("kd_tm", [64, 2, nch, 128], BF16)
            Am = sb("Am", [64, 2, nch, 64], BF16)
            sm = sb("sm", [128, 4, TT // 32], F32)
            osq = sb("osq", [128, NT], BF16)
            ot = sb("ot", [128, NT], F32)
            rsh = sb("rsh", [128, NT], F32)
            if not isp:
                Ssm = sb("Ssm", [128, NS, NH, 128], F32)
                Sbs = sb("Sbs", [128, NS, NH, 128], BF16)
            if two:
                DVE.memset(Am[h2:Lc, :, :, 0:h2], 0.0)
            hq_tok = sig_tok = hg_tok = hi_tok = None
            for i in range(8):
                w = k.wnext("hg")
                if i in (4, 5):
                    for b in range(nch):
                        bk = k.bank_next()
                        wait("pe", bk.free); wait("pe", h_tok)
                        for kc in range(KC):
                            ins = PE.matmul(bk.ap[0:Lc, 0:512], lhsT=h_bf[:, kc, b * Lc:(b + 1) * Lc], rhs=w[:, kc, :], start=(kc == 0), stop=(kc == KC - 1))
                        bk.ready = tick("pe", ins)
                        hi_tok = evac_copy(hi_tm[0:Lc, 4 * (i - 4):4 * (i - 4) + 4, b, :], bk.ap[0:Lc, 0:512].rearrange("p (h e) -> p h e", e=128), bk)
                else:
                    for j in range(4):
                        bk = k.bank_next()
                        wait("pe", bk.free); wait("pe", h_tok)
                        for kc in range(KC):
                            ins = PE.matmul(bk.ap[:, 0:NT], lhsT=w[:, kc, 128 * j:128 * j + 128], rhs=h_bf[:, kc, 0:NT], start=(kc == 0), stop=(kc == KC - 1))
                        bk.ready = tick("pe", ins)
                        wait("act", bk.ready)
                        if i < 2:
                            hq_tok = bk.free = tick("act", ACT.activation(out=hq[:, 4 * i + j, 0:NT], in_=bk.ap[:, 0:NT], func=AF.Silu))
                        elif i < 4:
                            sig_tok = bk.free = tick("act", ACT.activation(out=sigf[:, 4 * (i - 2) + j, 0:NT], in_=bk.ap[:, 0:NT], func=AF.Sigmoid))
                        else:
                            hg_tok = bk.free = tick("act", ACT.activation(out=hgt[:, 4 * (i - 6) + j, 0:NT], in_=bk.ap[:, 0:NT], func=AF.Silu))
                k.wfree_tok[k.w_cur] = bk.ready
            hi_toks = [("act", k.cnt["act"]), ("dve", k.cnt["dve"])]
            if isp:
                sb_tok = tick("dve", DVE.tensor_copy(out=Sb[:], in_=S_all[:, l]))
            else:
                s_ld = dtick("sld", POOL.dma_start(out=Ssm[:].rearrange("p i h v -> p (i h) v"), in_=st[l].rearrange("i h kk v -> kk (i h) v")))
                wait("dve", s_ld)
                sb_tok = tick("dve", DVE.tensor_copy(out=Sbs[:], in_=Ssm[:]))
            wait("dve", sig_tok); wait("dve", hq_tok)
            for t_ in hi_toks:
                wait("pe", t_)
            pe_use = [None, None]
            last_o = None
            act_rd = [None]
            dve_rd = [None]
            dve_use = [None, None]

            def c3(ap):
                return ap.rearrange("p (c t) -> p c t", t=Lc)
            def prep(h):
                    hs = h % 2
                    fA, fK, fB, fC, fD, fE, gA, gB, gC, gD = [fw[:, i, 0:NT] for i in range(10)]
                    qe, qa2, ka0, ka1, kd = [qk4[:, hs, i, 0:NT] for i in range(5)]
                    PL = POOL
                    wait("pool", sig_tok); wait("pool", hq_tok); wait("pool", act_rd[0])
                    t0_ = tick("pool", PL.tensor_scalar(out=fA, in0=sigf[:, h, 0:NT], scalar1=omlt[:, l, h:h + 1], scalar2=lbt[:, l, h:h + 1], op0=ALU.mult, op1=ALU.add))
                    wait("dve", t0_)
                    t1 = tick("dve", DVE.tensor_scalar(out=fA, in0=fA, scalar1=TINY, scalar2=None, op0=ALU.max))
                    wait("act", t1)
                    t2 = tick("act", ACT.activation(out=fB, in_=fA, func=AF.Ln))
                    wait("pool", dve_rd[0])
                    PL.tensor_scalar(out=fK, in0=sigf[:, h, 0:NT], scalar1=nomlt[:, l, h:h + 1], scalar2=omlt[:, l, h:h + 1], op0=ALU.mult, op1=ALU.add)
                    wait("pool", t2)
                    wait("dve", t2)
                    DVE.tensor_tensor_scan(out=fC, data0=M0[:, 0:NT], data1=fB, initial=0.0, op0=ALU.mult, op1=ALU.add)
                    sc_t = tick("dve", k.last_ins["dve"])
                    wait("pool", sc_t)
                    b3 = c3(fC)
                    if two:
                        PL.tensor_tensor(out=c3(fD), in0=b3, in1=b3[:, :, h2 - 1:h2].to_broadcast([128, nch, Lc]), op=ALU.subtract)
                    PL.tensor_tensor(out=c3(fE), in0=b3[:, :, Lc - 1:Lc].to_broadcast([128, nch, Lc]), in1=b3, op=ALU.subtract)
                    t3 = tick("pool", PL.tensor_copy(out=sm[:, 0, 0:nch], in_=b3[:, :, Lc - 1]))
                    wait("act", t3)
                    ACT.activation(out=gC, in_=fC, func=AF.Exp)
                    ACT.activation(out=c3(gA)[:, :, 0:h2], in_=c3(fC)[:, :, 0:h2], func=AF.Exp, scale=-1.0)
                    if two:
                        ACT.activation(out=c3(gA)[:, :, h2:Lc], in_=c3(fD)[:, :, h2:Lc], func=AF.Exp)
                        ACT.activation(out=gB, in_=fD, func=AF.Exp, scale=-1.0)
                    ACT.activation(out=gD, in_=fE, func=AF.Exp)
                    t4 = tick("act", ACT.activation(out=sm[:, 1, 0:nch], in_=sm[:, 0, 0:nch], func=AF.Exp))
                    act_rd[0] = t4
                    wait("pool", t4); wait("pool", pe_use[hs]); wait("pool", dve_use[hs])
                    PL.tensor_tensor(out=qe, in0=hq[:, h, 0:NT], in1=gC, op=ALU.mult)
                    PL.tensor_tensor(out=c3(ka0)[:, :, 0:h2], in0=c3(fK)[:, :, 0:h2], in1=c3(gA)[:, :, 0:h2], op=ALU.mult)
                    if two:
                        PL.tensor_tensor(out=c3(qa2)[:, :, h2:Lc], in0=c3(hq[:, h, 0:NT])[:, :, h2:Lc], in1=c3(gA)[:, :, h2:Lc], op=ALU.mult)
                        PL.tensor_tensor(out=ka1, in0=fK, in1=gB, op=ALU.mult)
                    PL.tensor_copy(out=sm[:, 2 + hs, 0:nch], in_=sm[:, 1, 0:nch])
                    t5 = tick("pool", PL.tensor_tensor(out=kd, in0=fK, in1=gD, op=ALU.mult))
                    wait("dve", t5)
                    return t5

            HD = {}
            osq_free = [None]
            rsh_free = [None]

            def stageA(h, t5):
                hs = h % 2
                qe, qa2, ka0, ka1, kd = [qk4[:, hs, i, 0:NT] for i in range(5)]
                bA = k.banks[0]; bT = k.banks[1]
                wait("pe", t5); wait("pe", bA.free); wait("pe", bT.free)
                for ci in range(nch):
                    cs = ci * Lc
                    ins = PE.matmul(bA.ap[0:h2, cs:cs + h2], lhsT=ka0[:, cs:cs + h2], rhs=qe[:, cs:cs + h2], start=True, stop=True)
                    if two:
                        ins = PE.matmul(bA.ap[0:Lc, cs + h2:cs + Lc], lhsT=ka1[:, cs:cs + Lc], rhs=qa2[:, cs + h2:cs + Lc], start=True, stop=True)
                bA.ready = tick("pe", ins)
                bTv = bT.ap.bitcast(BF16)
                for ci in range(nch):
                    ins = PE.transpose(out=bTv[0:Lc, 128 * ci:128 * ci + 128], in_=kd[:, ci * Lc:(ci + 1) * Lc], identity=identB[:])
                bT.ready = tick("pe", ins)
                wait("dve", bA.ready); wait("dve", pe_use[hs])
                bA3 = bA.ap[:, 0:nch * Lc].rearrange("p (c t) -> p c t", t=Lc)
                am_t = tick("dve", DVE.tensor_tensor(out=Am[0:h2, hs, :, 0:Lc], in0=bA3[0:h2], in1=triP[0:h2, 0:Lc].unsqueeze(1).to_broadcast([h2, nch, Lc]), op=ALU.mult))
                if two:
                    am_t = tick("dve", DVE.tensor_tensor(out=Am[h2:Lc, hs, :, h2:Lc], in0=bA3[h2:Lc, :, h2:Lc],
                                                         in1=triP[h2:Lc, h2:Lc].unsqueeze(1).to_broadcast([Lc - h2, nch, Lc - h2]), op=ALU.mult))
                bA.free = am_t
                wait("act", bT.ready); wait("act", pe_use[hs])
                kt_t = bT.free = tick("act", ACT.copy(out=kd_tm[0:Lc, hs], in_=bTv[0:Lc, 0:nch * 128].rearrange("p (b e) -> p b e", e=128)))
                HD[h] = dict(am_t=am_t, kt_t=kt_t, t5=t5)

            def chunk(h, ci):
                hs = h % 2
                qe = qk4[:, hs, 0, 0:NT]
                ebl = sm[:, 2 + hs, :]
                bO = k.banks[2 + (h % 2)]
                if ci == 0:
                    wait("pe", bO.free); wait("pe", HD[h]["am_t"]); wait("pe", HD[h]["kt_t"]); wait("dve", HD[h]["t5"])
                cs = ci * Lc
                if isp:
                    Sst = S_all[:, l, h, :]; Sbh = Sb[:, h, :]
                else:
                    Sst = Ssm[:, ci, h, :]; Sbh = Sbs[:, ci, h, :]
                wait("pe", nonloc["sb_tok"])
                PE.matmul(bO.ap[:, cs:cs + Lc], lhsT=Sbh, rhs=qe[:, cs:cs + Lc], start=True, stop=False)
                PE.matmul(bO.ap[:, cs:cs + Lc], lhsT=hi_tm[0:Lc, h, ci, :], rhs=Am[0:Lc, hs, ci, 0:Lc], start=False, stop=True)
                bS = k.banks[4 + (ci % 2)]
                wait("pe", bS.free)
                ins = PE.matmul(bS.ap[:, 0:128], lhsT=kd_tm[0:Lc, hs, ci, :], rhs=hi_tm[0:Lc, h, ci, :], start=True, stop=True)
                bS.ready = tick("pe", ins)
                wait("dve", bS.ready)
                bS.free = tick("dve", DVE.scalar_tensor_tensor(out=Sst, in0=Sst, scalar=ebl[:, ci:ci + 1], in1=bS.ap[:, 0:128], op0=ALU.mult, op1=ALU.add))
                nonloc["sb_tok"] = tick("dve", DVE.tensor_copy(out=Sbh, in_=Sst))
                if ci == nch - 1:
                    bO.ready = bS.ready
                    pe_use[hs] = bS.ready
                    dve_use[hs] = nonloc["sb_tok"]

            def fin_a(h):
                bO = k.banks[2 + (h % 2)]
                wait("act", bO.ready); wait("act", osq_free[0])
                HD[h]["q_t"] = tick("act", ACT.activation(out=osq[:, 0:NT], in_=bO.ap[:, 0:NT], func=AF.Square))

            def fin_b(h):
                bO = k.banks[2 + (h % 2)]
                bq = k.banks[6 + (h % 2)]
                wait("pe", bq.free); wait("pe", HD[h]["q_t"])
                bq.ready = tick("pe", PE.matmul(bq.ap[:, 0:NT], lhsT=onesB[:], rhs=osq[:, 0:NT], start=True, stop=True))
                osq_free[0] = bq.ready
                wait("act", rsh_free[0])
                wait("dve", rstd_act(bq, rsh[:, 0:NT], bq.ap[:, 0:NT], 128.0 * EPS)); wait("dve", hg_tok)
                bO.free = tick("dve", DVE.scalar_tensor_tensor(out=ot[:, 0:NT], in0=bO.ap[:, 0:NT], scalar=ghg[:, l:l + 1], in1=rsh[:, 0:NT], op0=ALU.mult, op1=ALU.mult))
                rsh_free[0] = bO.free
                nonloc["last_o"] = tick("dve", DVE.tensor_tensor(out=o_h[:, h, 0:NT], in0=ot[:, 0:NT], in1=hgt[:, h, 0:NT], op=ALU.mult))
                if cfg.DEBUG and isp and T.t == 0 and l == 0:
                    wait("pool", nonloc["last_o"])
                    dd_ = dtick("dbg", POOL.dma_start(out=dbg_h[:, h, :], in_=ot[:, :]))
                    wait("dve", dd_)

            nonloc = {"sb_tok": sb_tok, "last_o": None}
            pend = []
            t5_0 = prep(0)
            stageA(0, t5_0)
            for h in range(NH):
                t5n = prep(h + 1) if h + 1 < NH else None
                for ci in range(nch):
                    chunk(h, ci)
                    if ci == min(2, nch - 1) and pend:
                        pend.pop(0)()
                if h + 1 < NH:
                    stageA(h + 1, t5n)
                fin_a(h)
                pend.append(lambda h=h: fin_b(h))
            while pend:
                pend.pop(0)()
            sb_tok = nonloc["sb_tok"]; last_o = nonloc["last_o"]
            ptoks = []
            if isp and T.t == NPT - 1:
                wait("pool", sb_tok)
                ptoks.append(dtick("out", POOL.dma_start(out=sp_o[l].rearrange("h kk v -> kk h v"), in_=S_all[:, l])))
            if not isp:
                wait("pool", sb_tok)
                ptoks.append(dtick("out", POOL.dma_start(out=ss_o[l].rearrange("i h kk v -> kk (i h) v"), in_=Ssm[:].rearrange("p i h v -> p (i h) v"))))
            k.barrier(pool_tokens=ptoks)
        k.es_cur = es

        with contextlib.ExitStack() as pes:
            k.es_cur = pes
            mt = sb("mt", [128, 2, 4, NT], F32)
            mt_free = [None, None]
            m_tok = None
            for m in range(8):
                w = k.wnext("mrg")
                bks = [k.bank_next() for _ in range(4)]
                srcs = [h_bf, h_bf, o_att, o_h]
                for bi in range(4):
                    wait("pe", bks[bi].free)
                wait("pe", h_tok); wait("pe", oa_tok); wait("pe", last_o)
                for bi in range(4):
                    for kc in range(KC):
                        ins = PE.matmul(bks[bi].ap[:, 0:NT], lhsT=w[:, kc, 128 * bi:128 * bi + 128], rhs=srcs[bi][:, kc, 0:NT], start=(kc == 0), stop=(kc == KC - 1))
                r_t = k.wdone(ins)
                for bi in range(4):
                    bks[bi].ready = r_t
                s = m % 2
                wait("act", r_t); wait("act", mt_free[s])
                ACT.activation(out=mt[:, s, 0, 0:NT], in_=bks[0].ap[:, 0:NT], func=AF.Sigmoid)
                a_t = tick("act", ACT.activation(out=mt[:, s, 1, 0:NT], in_=bks[1].ap[:, 0:NT], func=AF.Sigmoid))
                bks[0].free = bks[1].free = a_t
                wait("dve", a_t); wait("dve", r_t)
                DVE.tensor_tensor(out=mt[:, s, 2, 0:NT], in0=mt[:, s, 0, 0:NT], in1=bks[2].ap[:, 0:NT], op=ALU.mult)
                d_t = tick("dve", DVE.tensor_tensor(out=mt[:, s, 3, 0:NT], in0=mt[:, s, 1, 0:NT], in1=bks[3].ap[:, 0:NT], op=ALU.mult))
                bks[2].free = bks[3].free = d_t
                m_tok = tick("dve", DVE.tensor_tensor(out=merged[:, m, 0:NT], in0=mt[:, s, 2, 0:NT], in1=mt[:, s, 3, 0:NT], op=ALU.add))
                mt_free[s] = m_tok
            for g in range(2):
                w = k.wnext("wo")
                for j in range(4):
                    m = 4 * g + j
                    bk = k.bank_next()
                    wait("pe", bk.free); wait("pe", m_tok)
                    for kc in range(KC):
                        ins = PE.matmul(bk.ap[:, 0:NT], lhsT=w[:, kc, 128 * j:128 * j + 128], rhs=merged[:, kc, 0:NT], start=(kc == 0), stop=(kc == KC - 1))
                    bk.ready = tick("pe", ins)
                    wait("dve", bk.ready)
                    for (c0, c1, sq) in T.segs:
                        ins = DVE.scalar_tensor_tensor(out=x_fm[:, m, c0:c1], in0=bk.ap[:, c0:c1], scalar=MOD[:, l, 5, m, sq:sq + 1],
                                                       in1=x_fm[:, m, c0:c1], op0=ALU.mult, op1=ALU.add)
                    bk.free = tick("dve", ins)
                k.wfree_tok[k.w_cur] = bk.ready
            k.barrier()
        k.es_cur = es

    def final_out(T):
        NT, nblk, bp = T.NT, T.nblk, T.bp
        with contextlib.ExitStack() as pes:
            k.es_cur = pes
            x_tmp = sb("x_tmp", [128, KC, NT], F32)
            ystage = sb("ystage", [128, nblk, D], F32)
            yt = norm_phase(T, 0, None, None, x_tmp)
            wait("pe", yt)
            et = None
            for b in range(nblk):
                for cg in range(2):
                    bk = k.bank_next()
                    wait("pe", bk.free)
                    for j in range(4):
                        ins = PE.transpose(out=bk.ap[0:bp, 128 * j:128 * j + 128], in_=x_tmp[:, 4 * cg + j, b * bp:(b + 1) * bp], identity=identF[:])
                    bk.ready = tick("pe", ins)
                    et = evac_copy(ystage[0:bp, b, 512 * cg:512 * cg + 512], bk.ap[0:bp, 0:512], bk)
                    wait("pool", et)
            wait("pool", ("act", k.cnt["act"])) if k.cnt["act"] else None
            wait("pool", ("dve", k.cnt["dve"])) if k.cnt["dve"] else None
            if T.kind == "p":
                dst = yp[T.t * TT:(T.t + 1) * TT, :].rearrange("(b p) f -> p b f", p=128)
            else:
                dst = ys.rearrange("(b p) f -> p b f", p=bp)
            ot_ = dtick("out", POOL.dma_start(out=dst, in_=ystage[0:bp]))
            k.barrier(pool_tokens=[ot_])
        k.es_cur = es

    DVE.memset(S_all[:], 0.0)
    for T in tiles:
        load_x(T)
        for l in range(DEPTH):
            ht = norm_mod(T, l, 1, 0)
            ffn_phase(T, l, 1, ht)
            ht = norm_mod(T, l, 4, 3)
            mixing(T, l, ht)
            ht = norm_mod(T, l, 7, 6)
            ffn_phase(T, l, 2, ht)
        final_out(T)
    for s in ("out", "st0", "st1", "st2", "kvst") + (("dbg",) if cfg.DEBUG else ()):
        if k.cnt[s]:
            POOL.wait_ge(k.sem[s], k.cnt[s])
    assert k.w_used == len(k.wseq)
    es.close()
    return nc


_WNAMES = ["w_ada", "b_ada", "g_ffn1", "w_ffn1_gu", "w_ffn1_d", "g_mix", "w_in", "att_lambda", "g_att_sub",
           "hg_lb_logits", "g_hg_norm", "w_br_att", "w_br_hg", "w_out", "g_ffn2", "w_ffn2_gu", "w_ffn2_d", "g_final"]


def run(cfg, inputs, n_cores=8):
    nc = build(cfg)
    NS, TS = cfg.NS, cfg.TS
    f = lambda a: np.ascontiguousarray(np.asarray(a, dtype=np.float32))
    in_maps = []
    nb = inputs["x_prompt"].shape[0]
    for c in range(n_cores):
        sq = (c * nb) // n_cores
        s0 = c * NS
        m = {
            "xp": f(inputs["x_prompt"][sq]),
            "xs": f(inputs["x_sample"][s0:s0 + NS]).reshape(NS * TS, D),
            "ck": f(np.asarray(inputs["cache_k"])[:, s0:s0 + NS].reshape(cfg.DEPTH, NS, cfg.PAST, D)),
            "cv": f(np.asarray(inputs["cache_v"])[:, s0:s0 + NS].reshape(cfg.DEPTH, NS, cfg.PAST, D)),
            "st": f(np.asarray(inputs["state_hgrn"])[:, s0:s0 + NS]),
            "cc": f(np.concatenate([np.asarray(inputs["c_prompt"])[sq:sq + 1], np.asarray(inputs["c_sample"])[s0:s0 + NS]], 0)),
        }
        for n in _WNAMES:
            m[n] = f(inputs[n])
        in_maps.append(m)
    res = run_bass_kernel_spmd(nc, in_maps, core_ids=list(range(n_cores)))
    R = res.results
    run.last = R
    per = n_cores // nb
    DEPTH = cfg.DEPTH
    y_prompt = np.stack([R[per * b]["yp"] for b in range(nb)], 0)
    k_prompt = np.stack([R[per * b]["kp"] for b in range(nb)], 1).reshape(DEPTH, nb, cfg.SEQ, NH, 128)
    v_prompt = np.stack([R[per * b]["vp"] for b in range(nb)], 1).reshape(DEPTH, nb, cfg.SEQ, NH, 128)
    s_prompt = np.stack([R[per * b]["sp"] for b in range(nb)], 1)
    y_sample = np.concatenate([R[c]["ys"].reshape(NS, TS, D) for c in range(n_cores)], 0)
    k_sample = np.concatenate([R[c]["ks"].reshape(DEPTH, NS, TS, NH, 128) for c in range(n_cores)], 1)
    v_sample = np.concatenate([R[c]["vs"].reshape(DEPTH, NS, TS, NH, 128) for c in range(n_cores)], 1)
    s_sample = np.concatenate([R[c]["ss"] for c in range(n_cores)], 1)
    return tuple(np.ascontiguousarray(a, dtype=np.float32) for a in
                 (y_prompt, y_sample, k_prompt, v_prompt, s_prompt, k_sample, v_sample, s_sample))


def kernel(**inputs):
    cfg = Cfg()
    return run(cfg, inputs, 8)
```
